# Optimizing a Trainium2 kernel written in Bass

```python
import functools
import jax, jax.numpy as jnp
from jax import lax
import numpy as np

D_MODEL = 2048
BATCH = 32
SEQ = 256
DEPTH = 4
DEC_BATCH = 2
DEC_SEQ = 1024
PAST_LEN = 512

GRID_W = 64
HEAD_DIM = 128
ATT_HEADS = 8
KV_HEADS = 2
Q_PER_KV = ATT_HEADS // KV_HEADS
ATT_DIM = ATT_HEADS * HEAD_DIM
KV_DIM = KV_HEADS * HEAD_DIM
WINDOW = 128
BLOCK = 128
ATT_SCALE = HEAD_DIM ** -0.5
ROPE_THETA = 10000.0
ROPE_FREQS = HEAD_DIM // 4
HG_HEADS = 8
HG_DK = 128
HG_DV = 128
HG_DIM = HG_HEADS * HG_DK
HG_VDIM = HG_HEADS * HG_DV
HG_CHUNK = 32
MIX_DIM = ATT_DIM + HG_VDIM
IN_DIM = ATT_DIM + 2 * KV_DIM + 3 * HG_DIM + 2 * HG_VDIM
D_FF = 4 * D_MODEL
LN_EPS = 1e-5
RMS_EPS = 1e-6
DEEPNORM_ALPHA = (2 * DEPTH) ** 0.25
DEEPNORM_BETA = (8 * DEPTH) ** -0.25
NEG_INF = -1e30
F32 = jnp.float32

kernel_name = 'hybrid_swa_hgrn2_diffusion_step'


def layer_norm(x, g, b):
    xf = x.astype(F32)
    xc = xf - jnp.mean(xf, -1, keepdims=True)
    var = jnp.mean(xc * xc, -1, keepdims=True)
    return (xc * lax.rsqrt(var + LN_EPS) * g.astype(F32) + b.astype(F32)).astype(x.dtype)


def rms_norm(x, g):
    xf = x.astype(F32)
    return (xf * lax.rsqrt(jnp.mean(xf * xf, -1, keepdims=True) + RMS_EPS) * g.astype(F32)).astype(x.dtype)


def modulation(c, w_mod_l, b_mod_l):
    return jnp.split(jax.nn.silu(c) @ w_mod_l + b_mod_l, 6, axis=-1)


def modulate(x, shift, scale):
    return x * (1 + scale) + shift


def split_in(z):
    cuts = np.cumsum([ATT_DIM, KV_DIM, KV_DIM, HG_DIM, HG_DIM, HG_DIM, HG_VDIM]).tolist()
    return jnp.split(z, cuts, axis=-1)


def axial_angles(n_tokens):
    rows = n_tokens // GRID_W
    row = jnp.repeat(jnp.arange(rows, dtype=F32), GRID_W)
    col = jnp.tile(jnp.arange(GRID_W, dtype=F32), rows)
    inv = ROPE_THETA ** (-jnp.arange(ROPE_FREQS, dtype=F32) / ROPE_FREQS)
    return row[:, None] * inv, col[:, None] * inv


def rope_axis(x, ang):
    cos = jnp.cos(ang).astype(x.dtype)[None, :, None, :]
    sin = jnp.sin(ang).astype(x.dtype)[None, :, None, :]
    x1, x2 = x[..., :ROPE_FREQS], x[..., ROPE_FREQS:]
    return jnp.concatenate([x1 * cos - x2 * sin, x1 * sin + x2 * cos], -1)


def axial_rope(x, ang_r, ang_c):
    half = HEAD_DIM // 2
    return jnp.concatenate([rope_axis(x[..., :half], ang_r), rope_axis(x[..., half:], ang_c)], -1)


def softmax_with_sink(s, sink):
    sk = sink.astype(F32).reshape(KV_HEADS, Q_PER_KV, 1, 1)
    m = jnp.maximum(jnp.max(s, -1, keepdims=True), sk)
    e = jnp.exp(s - m)
    return e / (jnp.sum(e, -1, keepdims=True) + jnp.exp(sk - m))


def context_attention(q, k, v, sink):
    B, T = q.shape[0], q.shape[1]
    nb = T // BLOCK
    qb = jnp.swapaxes(q.reshape(B, nb, BLOCK, KV_HEADS, Q_PER_KV, HEAD_DIM), 0, 1)

    def one_block(qblk):
        s = jnp.einsum('bqgrd,bkgd->bgrqk', qblk, k, preferred_element_type=F32) * ATT_SCALE
        p = softmax_with_sink(s, sink).astype(v.dtype)
        return jnp.einsum('bgrqk,bkgd->bqgrd', p, v)

    o = lax.map(one_block, qb)
    return jnp.swapaxes(o, 0, 1).reshape(B, T, ATT_DIM)


def latent_attention(q, k, v, sink, ang_r, ang_c, k_ctx, v_ctx):
    B, T = q.shape[0], q.shape[1]
    nb = T // BLOCK
    q = axial_rope(q, ang_r, ang_c)
    k = axial_rope(k, ang_r, ang_c)
    qb = q.reshape(B, nb, BLOCK, KV_HEADS, Q_PER_KV, HEAD_DIM)

    def neighbours(a):
        ap = jnp.pad(a, ((0, 0), (BLOCK, BLOCK), (0, 0), (0, 0))).reshape(B, nb + 2, BLOCK, KV_HEADS, HEAD_DIM)
        return jnp.concatenate([ap[:, :-2], ap[:, 1:-1], ap[:, 2:]], axis=2)

    kb, vb = neighbours(k), neighbours(v)
    qpos = jnp.arange(nb)[:, None] * BLOCK + jnp.arange(BLOCK)[None, :]
    kpos = jnp.arange(nb)[:, None] * BLOCK - BLOCK + jnp.arange(3 * BLOCK)[None, :]
    kp = kpos[:, None, :]
    valid = (kp >= 0) & (kp < T) & (jnp.abs(kp - qpos[:, :, None]) <= WINDOW)
    s_loc = jnp.einsum('bnqgrd,bnkgd->bngrqk', qb, kb, preferred_element_type=F32) * ATT_SCALE
    s_loc = jnp.where(valid[None, :, None, None], s_loc, NEG_INF)
    s_ctx = jnp.einsum('bnqgrd,bpgd->bngrqp', qb, k_ctx, preferred_element_type=F32) * ATT_SCALE
    p = softmax_with_sink(jnp.concatenate([s_loc, s_ctx], -1), sink).astype(v.dtype)
    o = (jnp.einsum('bngrqk,bnkgd->bnqgrd', p[..., :3 * BLOCK], vb)
         + jnp.einsum('bngrqp,bpgd->bnqgrd', p[..., 3 * BLOCK:], v_ctx))
    return o.reshape(B, T, ATT_DIM)


def hgrn_chunk_scan(q, k, v, log_f, s0):
    B, T, H, _ = q.shape
    n, C = T // HG_CHUNK, HG_CHUNK

    def chunks(a):
        return a.reshape(B, n, C, H, a.shape[-1]).astype(F32)

    qc, kc, vc = chunks(q), chunks(k), chunks(v)
    b = jnp.cumsum(chunks(log_f), axis=2)
    b_mid = b[:, :, C // 2 - 1:C // 2]
    b_last = b[:, :, C - 1:C]
    a = jnp.einsum('bnthk,bnshk->bnhts', qc * jnp.exp(b - b_mid), kc * jnp.exp(b_mid - b))
    a = jnp.where(jnp.tril(jnp.ones((C, C), bool)), a, 0.0)
    o_intra = jnp.einsum('bnhts,bnshv->bnthv', a, vc)
    decay = jnp.exp(b_last[:, :, 0])
    u = jnp.einsum('bnshk,bnshv->bnhkv', kc * jnp.exp(b_last - b), vc)

    def step(s, inp):
        d, uu = inp
        return d[..., None] * s + uu, s

    s_final, s_starts = lax.scan(step, s0.astype(F32), (jnp.moveaxis(decay, 1, 0), jnp.moveaxis(u, 1, 0)))
    o_inter = jnp.einsum('bnthk,bnhkv->bnthv', qc * jnp.exp(b), jnp.moveaxis(s_starts, 0, 1))
    o = (o_intra + o_inter).reshape(B, T, H, v.shape[-1])
    return o.astype(v.dtype), s_final.astype(v.dtype)


def hgrn_mixer(hq, hff, hfb, hi, hgt, lb_f, lb_b, norm_g, s_f0, s_b0):
    B, T, _ = hq.shape
    q = jax.nn.silu(hq).reshape(B, T, HG_HEADS, HG_DK)
    v = hi.reshape(B, T, HG_HEADS, HG_DV)

    def gates(z, lb):
        zf = z.reshape(B, T, HG_HEADS, HG_DK).astype(F32)
        lbh = lb.reshape(HG_HEADS, HG_DK)
        log_f = jnp.log(lbh + (1 - lbh) * jax.nn.sigmoid(zf))
        k = ((1 - lbh) * jax.nn.sigmoid(-zf)).astype(v.dtype)
        return log_f, k

    logf_f, k_f = gates(hff, lb_f)
    logf_b, k_b = gates(hfb, lb_b)
    o_f, s_f = hgrn_chunk_scan(q, k_f, v, logf_f, s_f0)
    flip = lambda a: jnp.flip(a, axis=1)
    o_b, s_b = hgrn_chunk_scan(flip(q), flip(k_b), flip(v), flip(logf_b), s_b0)
    o = rms_norm(o_f + flip(o_b), norm_g) * jax.nn.silu(hgt.reshape(B, T, HG_HEADS, HG_DV))
    return o.reshape(B, T, HG_VDIM), s_f, s_b


def lower_bounds(lb_logits):
    p = jax.nn.softmax(lb_logits.astype(F32), axis=0)
    cs = jnp.cumsum(p, axis=0)
    return cs - cs[0:1]


def sq_relu_mlp(h, w_up, w_down):
    return jnp.square(jax.nn.relu(h @ w_up)) @ w_down


def hybrid_layer(x, mods, w_in_l, sink_l, attn_g_l, lb_f_l, lb_b_l, hg_g_l, w_o_l, ln_g_l, ln_b_l,
                 w_up_l, w_down_l, s_f0, s_b0, attend):
    shift1, scale1, gate1, shift2, scale2, gate2 = mods
    B, T, _ = x.shape
    h = modulate(x, shift1, scale1)
    q, k, v, hq, hff, hfb, hi, hgt = split_in(h @ w_in_l)
    q = q.reshape(B, T, ATT_HEADS, HEAD_DIM)
    k = k.reshape(B, T, KV_HEADS, HEAD_DIM)
    v = v.reshape(B, T, KV_HEADS, HEAD_DIM)
    att = rms_norm(attend(q, k, v, sink_l), attn_g_l)
    hg, s_f, s_b = hgrn_mixer(hq, hff, hfb, hi, hgt, lb_f_l, lb_b_l, hg_g_l, s_f0, s_b0)
    mix = jnp.concatenate([att, hg], -1) @ w_o_l
    x = layer_norm(DEEPNORM_ALPHA * x + gate1 * mix, ln_g_l[0], ln_b_l[0])
    ffn = sq_relu_mlp(modulate(x, shift2, scale2), w_up_l, w_down_l)
    x = layer_norm(DEEPNORM_ALPHA * x + gate2 * ffn, ln_g_l[1], ln_b_l[1])
    return x, k, v, s_f, s_b


def setup_inputs(seed: int = 0) -> dict:
    key = jax.random.key(seed)
    ks = jax.random.split(key, 20)

    def nrm(k, shape, s):
        return jax.random.normal(k, shape, F32) * s

    return {
        'x_prompt': nrm(ks[0], (BATCH, SEQ, D_MODEL), 1.0),
        'x_sample': nrm(ks[1], (DEC_BATCH, DEC_SEQ, D_MODEL), 1.0),
        'cache_k': nrm(ks[2], (DEC_BATCH, DEPTH, PAST_LEN, KV_HEADS, HEAD_DIM), 1.0),
        'cache_v': nrm(ks[3], (DEC_BATCH, DEPTH, PAST_LEN, KV_HEADS, HEAD_DIM), 1.0),
        'state_hgrn_fwd': nrm(ks[4], (DEC_BATCH, DEPTH, HG_HEADS, HG_DK, HG_DV), 0.5),
        'state_hgrn_bwd': nrm(ks[5], (DEC_BATCH, DEPTH, HG_HEADS, HG_DK, HG_DV), 0.5),
        'c': nrm(ks[6], (DEC_BATCH, D_MODEL), 1.0),
        'c_ctx': nrm(ks[7], (D_MODEL,), 1.0),
        'w_mod': nrm(ks[8], (DEPTH, D_MODEL, 6 * D_MODEL), 0.5 * D_MODEL ** -0.5),
        'b_mod': nrm(ks[9], (DEPTH, 6 * D_MODEL), 0.02),
        'w_in': nrm(ks[10], (DEPTH, D_MODEL, IN_DIM), D_MODEL ** -0.5),
        'attn_sink': nrm(ks[11], (DEPTH, ATT_HEADS), 0.5),
        'attn_norm_g': 1.0 + nrm(ks[12], (DEPTH, ATT_DIM), 0.02),
        'hg_lb_logits': nrm(ks[13], (2, DEPTH, HG_DIM), 0.5),
        'hg_norm_g': 1.0 + nrm(ks[14], (DEPTH, HG_DV), 0.02),
        'w_o': nrm(ks[15], (DEPTH, MIX_DIM, D_MODEL), MIX_DIM ** -0.5 * DEEPNORM_BETA),
        'ln_g': 1.0 + nrm(ks[16], (DEPTH, 2, D_MODEL), 0.02),
        'ln_b': nrm(ks[17], (DEPTH, 2, D_MODEL), 0.02),
        'w_up': nrm(ks[18], (DEPTH, D_MODEL, D_FF), D_MODEL ** -0.5),
        'w_down': nrm(ks[19], (DEPTH, D_FF, D_MODEL), D_FF ** -0.5 * DEEPNORM_BETA),
    }


def reference(x_prompt, x_sample, cache_k, cache_v, state_hgrn_fwd, state_hgrn_bwd, c, c_ctx,
              w_mod, b_mod, w_in, attn_sink, attn_norm_g, hg_lb_logits, hg_norm_g, w_o, ln_g, ln_b,
              w_up, w_down):
    lb_fwd = lower_bounds(hg_lb_logits[0]).astype(x_prompt.dtype)
    lb_bwd = lower_bounds(hg_lb_logits[1]).astype(x_prompt.dtype)
    Bp = x_prompt.shape[0]
    Ts = x_sample.shape[1]
    ang_r, ang_c = axial_angles(Ts)
    zero_state = jnp.zeros((Bp, HG_HEADS, HG_DK, HG_DV), x_prompt.dtype)
    xp, xs = x_prompt, x_sample
    new_k, new_v, new_sf, new_sb = [], [], [], []
    for l in range(DEPTH):
        mods_ctx = modulation(c_ctx, w_mod[l], b_mod[l])
        xp, k_c, v_c, s_f, s_b = hybrid_layer(
            xp, mods_ctx, w_in[l], attn_sink[l], attn_norm_g[l], lb_fwd[l], lb_bwd[l], hg_norm_g[l],
            w_o[l], ln_g[l], ln_b[l], w_up[l], w_down[l], zero_state, zero_state, context_attention)
        new_k.append(k_c)
        new_v.append(v_c)
        new_sf.append(s_f)
        new_sb.append(s_b)
        mods_lat = [m[:, None, :] for m in modulation(c, w_mod[l], b_mod[l])]
        attend_lat = functools.partial(latent_attention, ang_r=ang_r, ang_c=ang_c,
                                       k_ctx=cache_k[:, l], v_ctx=cache_v[:, l])
        xs, _, _, _, _ = hybrid_layer(
            xs, mods_lat, w_in[l], attn_sink[l], attn_norm_g[l], lb_fwd[l], lb_bwd[l], hg_norm_g[l],
            w_o[l], ln_g[l], ln_b[l], w_up[l], w_down[l], state_hgrn_fwd[:, l], state_hgrn_bwd[:, l], attend_lat)
    new_cache_k = jnp.stack(new_k, axis=1)
    new_cache_v = jnp.stack(new_v, axis=1)
    new_state_hgrn_fwd = jnp.stack(new_sf, axis=1)
    new_state_hgrn_bwd = jnp.stack(new_sb, axis=1)
    return (xp, xs, new_cache_k, new_cache_v, new_state_hgrn_fwd, new_state_hgrn_bwd)
```

```python
from contextlib import ExitStack
import numpy as np
import concourse.bass as bass
import concourse.mybir as mybir
from concourse.bass_utils import run_bass_kernel_spmd

F32 = mybir.dt.float32
BF16 = mybir.dt.bfloat16
AF = mybir.ActivationFunctionType
ALU = mybir.AluOpType

D = 2048
KC = 16
NTOK = 1280
NBLK = 10
NCH = 40
NSLOT = 5
DEPTH = 4
ALPHA = (2 * DEPTH) ** 0.25
ATT_SCALE = 128 ** -0.5
LN_EPS = 1e-5
RMS_EPS = 1e-6
TT = [(0, 512), (512, 512), (1024, 256)]
NRING = 6
N_IN_TILES = 62
_STOP = [None]


class _StopBuild(Exception):
    pass


class _Op:
    __slots__ = ("eng", "fn", "deps", "sig", "sem", "tick", "dma", "ndma", "epoch")


class Sched:
    def __init__(self):
        self.ops = []
        self.lastw = {}
        self.readers = {}
        self.epoch = 0

    def add(self, eng, fn, reads=(), writes=(), dma=None, ndma=1):
        op = _Op()
        op.eng, op.fn, op.dma, op.ndma, op.sig = eng, fn, dma, ndma, False
        op.epoch = self.epoch
        deps = {}
        psr = [r for r in reads if isinstance(r, tuple) and r and r[0] == "ps"]
        if psr:
            reads = [r for r in reads if r not in psr]
            writes = list(writes) + [r for r in psr if r not in writes]

        def consider(d, kind):
            if d is None or d is op:
                return
            if d.dma is None and op.dma is None and d.eng == eng:
                if eng == "pe":
                    return
            deps[id(d)] = d

        for r in reads:
            consider(self.lastw.get(r), "raw")
        for w in writes:
            consider(self.lastw.get(w), "waw")
            for rd in self.readers.get(w, ()):
                consider(rd, "war")
        op.deps = list(deps.values())
        for d in op.deps:
            d.sig = True
        for r in reads:
            self.readers.setdefault(r, []).append(op)
        for w in writes:
            self.lastw[w] = op
            self.readers[w] = []
        self.ops.append(op)
        return op

    def emit(self, nc, stack, block):
        sems = {}

        def get_sem(key):
            if key not in sems:
                sems[key] = stack.enter_context(nc.semaphore("s_" + "_".join(str(k) for k in key)))
            return sems[key]

        counts = {}
        for op in self.ops:
            if op.dma is not None:
                key = ("d", op.dma)
                counts[key] = counts.get(key, 0) + 16 * op.ndma
            else:
                key = ("e", op.eng, op.epoch)
                if op.sig:
                    counts[key] = counts.get(key, 0) + 1
            op.sem = key
            op.tick = counts.get(key, 0)
        by_eng = {}
        for op in self.ops:
            by_eng.setdefault(op.eng, []).append(op)
        for op in self.ops:
            if op.sig or op.dma is not None:
                get_sem(op.sem)

        def run(engname, e):
            waited = {}
            for op in by_eng.get(engname, []):
                for d in op.deps:
                    if waited.get(d.sem, 0) < d.tick:
                        e.wait_ge(get_sem(d.sem), d.tick)
                        waited[d.sem] = d.tick
                ins = op.fn(e)
                if op.dma is not None:
                    if not isinstance(ins, (list, tuple)):
                        ins = [ins]
                    assert len(ins) == op.ndma, (len(ins), op.ndma)
                    for i_ in ins:
                        i_.then_inc(get_sem(op.sem), 16)
                elif op.sig:
                    ins.then_inc(get_sem(op.sem), 1)
            for op in by_eng.get(engname, []):
                if op.dma is not None and waited.get(op.sem, 0) < op.tick:
                    e.wait_ge(get_sem(op.sem), op.tick)
                    waited[op.sem] = op.tick

        @block.tensor
        def _(e):
            run("pe", e)

        @block.scalar
        def _(e):
            run("act", e)

        @block.vector
        def _(e):
            run("dve", e)

        @block.gpsimd
        def _(e):
            run("pool", e)

        @block.sync
        def _(e):
            run("sp", e)


def build_program(n_layers=DEPTH):
    L = n_layers
    nc = bass.Bass("TRN2", target_bir_lowering=False)

    def din(name, shape):
        return nc.dram_tensor(name, list(shape), F32, kind="ExternalInput").ap()

    def dout(name, shape):
        return nc.dram_tensor(name, list(shape), F32, kind="ExternalOutput").ap()

    xT_d = din("xT", [128, KC, NTOK])
    cond_d = din("cond", [128, KC, 2])
    bmod_d = din("bmod", [128, L, 96])
    lng_d = din("lng", [128, L, 2, KC])
    lnb_d = din("lnb", [128, L, 2, KC])
    attg_d = din("attg", [128, L, 8])
    hgg_d = din("hgg", [128, L])
    lbl_d = din("lbl", [128, 2, DEPTH, 8])
    sink_d = din("sink", [128, L, 8])
    cos_d = din("cosT", [128, NTOK])
    sin_d = din("sinT", [128, NTOK])
    msk_d = din("masks", [128, NBLK, 3, 128])
    ctxf_d = din("ctxflag", [128, NBLK])
    cfl_d = din("cflag", [128, 1])
    ckT_d = din("ckT", [L, 128, 2, 512])
    cv_d = din("cv", [L, 128, 4, 256])
    s0_d = din("s0", [L, 2, 8, 128, 128])
    cm_d = din("cmats", [4, 128, 128])
    rmask_d = din("rowmask", [128, 4])
    wmod_d = [din(f"wmod{l}", [96, 128, 2048]) for l in range(L)]
    win_d = [din(f"win{l}", [N_IN_TILES, 128, 2048]) for l in range(L)]
    wo_d = [din(f"wo{l}", [16, 128, 2048]) for l in range(L)]
    wup_d = [din(f"wup{l}", [64, 128, 2048]) for l in range(L)]
    wdn_d = [din(f"wdn{l}", [64, 128, 2048]) for l in range(L)]

    yT_d = dout("yT", [128, KC, NTOK])
    okT_d = dout("okT", [L, 2, 128, NTOK])
    ov_d = dout("ov", [L, NBLK, 128, 256])
    ost_d = dout("ost", [L, 2, 8, 128, NSLOT, 128])
    xsp_d = nc.dram_tensor("xspill", [128, KC, NTOK], F32, kind="Internal").ap()

    S = Sched()
    stack = ExitStack()

    def sb(name, shape, dt):
        return stack.enter_context(nc.sbuf_tensor(name, list(shape), dt))

    RA = sb("RA", [128, KC * NTOK], F32)
    XM = sb("XM", [128, KC, NTOK], BF16)
    RC = sb("RC", [128, KC * NTOK], BF16)
    RING = [sb(f"ring{i}", [128, KC, 128], BF16) for i in range(NRING)]
    X = RA[:].rearrange("p (k t) -> p k t", k=KC)
    MIX = RC[:].rearrange("p (k t) -> p k t", k=KC)
    HID = MIX

    cond_f = sb("cond_f", [128, KC, 2], F32)
    condS = sb("condS", [128, KC, 2], BF16)
    bmod = sb("bmod_s", [128, L, 96], F32)
    lng = sb("lng_s", [128, L, 2, KC], F32)
    lnb = sb("lnb_s", [128, L, 2, KC], F32)
    attg = sb("attg_s", [128, L, 8], F32)
    hgg = sb("hgg_s", [128, L], F32)
    lbl = sb("lbl_s", [128, 2, DEPTH, 8], F32)
    lbv = sb("lbv", [128, 2, DEPTH, 8], F32)
    oml = sb("oml", [128, 2, DEPTH, 8], F32)
    lbsum = sb("lbsum", [128, 2, 8], F32)
    esink = sb("esink", [128, L, 8], F32)
    ctxf = sb("ctxf", [128, NBLK], F32)
    cfl = sb("cfl", [128, 1], F32)
    rmask = sb("rmask", [128, 4], F32)
    cmats = sb("cmats_s", [128, 3, 128], BF16)
    ones = sb("ones", [128, 128], BF16)
    scanm = sb("scanm", [128, NCH, 32], BF16)
    mods = sb("mods", [128, 96, 2], F32)
    sc1p = sb("sc1p", [128, KC, 2], F32)
    sc2p = sb("sc2p", [128, KC, 2], F32)
    dummy = sb("dummy_t", [128, 8], F32)

    PS = [stack.enter_context(nc.psum_tensor(f"ps{i}", [128, 512], F32)) for i in range(8)]

    ra_off = [0]

    def ra_f32(nwords):
        o = ra_off[0]
        ra_off[0] += nwords
        assert ra_off[0] <= KC * NTOK, ra_off[0]
        return RA[:, o:o + nwords]

    def ra_bf16(nel):
        assert nel % 2 == 0
        return ra_f32(nel // 2).bitcast(BF16)

    cosT = ra_f32(NTOK)
    sinT = ra_f32(NTOK)
    masks = ra_bf16(NBLK * 3 * 128).rearrange("p (b n q) -> p b n q", b=NBLK, n=3)
    ckT = ra_bf16(2 * 512).rearrange("p (g t) -> p g t", g=2)
    cvt = ra_bf16(4 * 256).rearrange("p (b c) -> p b c", b=4)
    qT = ra_bf16(4 * NTOK).rearrange("p (r t) -> p r t", r=4)
    kT = ra_bf16(NTOK)
    vtok = ra_bf16(NBLK * 256).rearrange("p (b c) -> p b c", b=NBLK)
    kraw = ra_f32(NTOK)
    vraw = [ra_f32(256), ra_f32(256)]
    tmpA = [ra_f32(512), ra_f32(512)]
    tmpB = [ra_f32(512), ra_f32(512)]
    eT = [ra_bf16(512) for _ in range(4)]
    attraw1 = ra_f32(4 * NTOK).rearrange("p (r t) -> p r t", r=4)
    attraw0 = RC[:, 8 * NTOK:16 * NTOK].bitcast(F32).rearrange("p (r t) -> p r t", r=4)
    attraw = [attraw0, attraw1]
    ATTK = [[("MIX", k) for k in range(8, 16)], ["attraw1"]]
    ra_off[0] = 0
    qf = ra_f32(NTOK)
    sigf = [ra_f32(NTOK), ra_f32(NTOK)]
    gsil = ra_f32(NTOK)
    vtokH = ra_bf16(NBLK * 128).rearrange("p (b c) -> p b c", b=NBLK)
    kk = ra_f32(NTOK)
    E1 = ra_f32(NTOK)
    logf = E1
    bc = ra_f32(NTOK)
    d1 = ra_f32(NTOK)
    qmid = ra_bf16(NTOK)
    kmid = ra_bf16(NTOK)
    vTh = kmid
    klast = ra_bf16(NTOK)
    qb = ra_bf16(NTOK)
    decay = ra_f32(NCH)
    klastM = ra_bf16(NCH * 128).rearrange("p (c k) -> p c k", c=NCH)
    aT = [ra_bf16(128), ra_bf16(128)]
    Sall = ra_bf16(NCH * 128).rearrange("p (c v) -> p c v", c=NCH)
    Sf = [ra_f32(128), ra_f32(128)]
    Sfin = ra_f32(NSLOT * 128).rearrange("p (s v) -> p s v", s=NSLOT)
    S0t = ra_f32(128)
    osq = d1[:, 0:256].bitcast(BF16)
    rstdh = d1[:, 256:768]
    th = d1[:, 768:1280]
    xb = RC[:, 0:KC * 512].rearrange("p (k t) -> p k t", k=KC)
    xsq = RC[:, 7 * NTOK:7 * NTOK + KC * 512].rearrange("p (k t) -> p k t", k=KC)
    XBK = [("MIX", k) for k in range(0, 7)]
    XSQK = [("MIX", k) for k in range(7, 14)]
    mean = sb("ln_mean", [128, 512], F32)
    rstd = sb("ln_rstd", [128, 512], F32)
    rtmp = sb("ln_tmp", [128, 512], F32)
    tmp_relu = [sb("relu0", [128, 512], F32), sb("relu1", [128, 512], F32)]

    ps_rr = [0]

    def next_ps(exclude=()):
        while True:
            i = ps_rr[0] % 8
            ps_rr[0] += 1
            if i not in exclude:
                return i

    ring_rr = [0]

    def load_w(src_ap):
        i = ring_rr[0] % NRING
        ring_rr[0] += 1
        dst = RING[i]
        S.add("pool", lambda e, dst=dst, src=src_ap: e.dma_start(
            out=dst[:].rearrange("p k c -> p (k c)"), in_=src),
            writes=[("ring", i)], dma=("ring", i))
        return i

    def mm_group(out_ap, pairs, reads, writes):
        def fn(e, out_ap=out_ap, pairs=pairs):
            ins = None
            n = len(pairs)
            for j, (l_, r_) in enumerate(pairs):
                ins = e.matmul(out_ap, l_, r_, start=(j == 0), stop=(j == n - 1))
            return ins
        return S.add("pe", fn, reads=reads, writes=writes)

    def mm1(out_ap, l_, r_, start, stop, reads, writes):
        return S.add("pe", lambda e: e.matmul(out_ap, l_, r_, start=start, stop=stop, skip_group_check=True),
                     reads=reads, writes=writes)

    def act(out, in_, func, reads, writes, scale=None, bias=None):
        kw = {}
        if scale is not None:
            kw["scale"] = scale
        if bias is not None:
            kw["bias"] = bias
        return S.add("act", lambda e: e.activation(out=out, in_=in_, func=func, **kw), reads=reads, writes=writes)

    def tt(eng, out, in0, in1, op, reads, writes):
        return S.add(eng, lambda e: e.tensor_tensor(out=out, in0=in0, in1=in1, op=op), reads=reads, writes=writes)

    def ts(eng, out, in0, s1, s2, op0, op1, reads, writes):
        if op1 is None:
            return S.add(eng, lambda e: e.tensor_scalar(out=out, in0=in0, scalar1=s1, scalar2=None, op0=op0),
                         reads=reads, writes=writes)
        return S.add(eng, lambda e: e.tensor_scalar(out=out, in0=in0, scalar1=s1, scalar2=s2, op0=op0, op1=op1),
                     reads=reads, writes=writes)

    def stt(out, in0, scalar, in1, op0, op1, reads, writes):
        return S.add("dve", lambda e: e.scalar_tensor_tensor(out=out, in0=in0, scalar=scalar, in1=in1,
                                                             op0=op0, op1=op1), reads=reads, writes=writes)

    def vcopy(out, in_, reads, writes):
        return S.add("dve", lambda e: e.tensor_copy(out=out, in_=in_), reads=reads, writes=writes)

    def vmemset(ap, val, writes):
        return S.add("dve", lambda e: e.memset(ap, val), writes=writes)

    def dma(q, out, in_, reads, writes, key):
        return S.add(q, lambda e: e.dma_start(out=out, in_=in_), reads=reads, writes=writes, dma=key)

    def barrier(keys):
        return S.add("dve", lambda e: e.memset(dummy[:, 0:1], 0.0), writes=list(keys))

    XK = [("X", k) for k in range(KC)]
    XMK = [("XM", k) for k in range(KC)]
    MIXK = [("MIX", k) for k in range(KC)]
    ATT_TEMPK = (["cosT", "sinT", "masks", "ckT", "cvt", "kT", "kraw", "vraw0", "vraw1", "attraw1"] +
                 [("qT", r) for r in range(4)] + [("vtok", B) for B in range(NBLK)] +
                 [("tmpA", 0), ("tmpA", 1), ("tmpB", 0), ("tmpB", 1)] + [("eT", i) for i in range(4)])
    HG_TEMPK = ["qf", "sig0", "sig1", "gsil", "vtokH", "kk", "E1", "bc", "d1", "qmid", "kmid", "klast", "qb",
                "decay", "klastM", "aT0", "aT1", "Sall", "Sf0", "Sf1", "Sfin", "S0t"]

    for k4 in range(4):
        dma("sp", X[:, 4 * k4:4 * k4 + 4, :], xT_d[:, 4 * k4:4 * k4 + 4, :], [], XK[4 * k4:4 * k4 + 4], ("xin", k4))
    small = [(cond_f, cond_d, "cond"), (bmod, bmod_d, "bmod"), (lng, lng_d, "lng"), (lnb, lnb_d, "lnb"),
             (attg, attg_d, "attg"), (hgg, hgg_d, "hgg"), (lbl, lbl_d, "lbl"), (esink, sink_d, "esink"),
             (ctxf, ctxf_d, "ctxf"), (cfl, cfl_d, "cfl"), (rmask, rmask_d, "rmask")]
    for t_, d_, nm in small:
        dma("sp", t_[:], d_, [], [nm], ("c", nm))
    dma("pool", cmats[:], cm_d[0:3].rearrange("m p c -> p m c"), [], ["cmats"], ("c", "cmats"))
    vmemset(ones[:], 1.0, ["ones"])
    vmemset(scanm[:], 1.0, ["scanm"])
    vmemset(scanm[:, :, 0:1], 0.0, ["scanm"])
    act(condS[:], cond_f[:], AF.Silu, ["cond"], ["condS"])
    act(esink[:], esink[:], AF.Exp, ["esink"], ["esink"])
    act(lbl[:], lbl[:], AF.Exp, ["lbl"], ["lbl"])
    tt("dve", lbsum[:], lbl[:, :, 0, :], lbl[:, :, 1, :], ALU.add, ["lbl"], ["lbsum"])
    for l_ in range(2, DEPTH):
        tt("dve", lbsum[:], lbsum[:], lbl[:, :, l_, :], ALU.add, ["lbl", "lbsum"], ["lbsum"])
    S.add("dve", lambda e: e.reciprocal(out=lbsum[:], in_=lbsum[:]), reads=["lbsum"], writes=["lbsum"])
    vmemset(lbv[:, :, 0, :], 0.0, ["lbv"])
    for l_ in range(1, DEPTH):
        tt("dve", lbv[:, :, l_, :], lbl[:, :, l_, :], lbsum[:], ALU.mult, ["lbl", "lbsum", "lbv"], ["lbv"])
        if l_ > 1:
            tt("dve", lbv[:, :, l_, :], lbv[:, :, l_, :], lbv[:, :, l_ - 1, :], ALU.add, ["lbv"], ["lbv"])
    ts("dve", oml[:], lbv[:], -1.0, 1.0, ALU.mult, ALU.add, ["lbv"], ["oml"])

    def r4(ap):
        return ap.rearrange("p (r q) -> p r q", r=4)

    def c32(ap):
        return ap.rearrange("p (c s) -> p c s", s=32)

    def layer(l):
        S.epoch = l
        pm = next_ps()
        for ct in range(96):
            ri = load_w(wmod_d[l][ct])
            mm_group(PS[pm][:, 2 * ct:2 * ct + 2],
                     [(RING[ri][:, k, :], condS[:, k, :]) for k in range(KC)],
                     reads=[("ring", ri), "condS"], writes=[("ps", pm)])
        tt("dve", mods[:], PS[pm][:, 0:192].rearrange("p (c j) -> p c j", j=2),
           bmod[:, l, :].unsqueeze(2).to_broadcast([128, 96, 2]), ALU.add,
           [("ps", pm), "bmod"], ["mods"])
        ts("dve", sc1p[:], mods[:, 16:32, :], 1.0, None, ALU.add, None, ["mods"], ["sc1p"])
        ts("dve", sc2p[:], mods[:, 64:80, :], 1.0, None, ALU.add, None, ["mods"], ["sc2p"])
        sh1 = mods[:, 0:16, :]
        g1 = mods[:, 32:48, :]
        sh2 = mods[:, 48:64, :]
        g2 = mods[:, 80:96, :]

        if _STOP[0] == 1:
            raise _StopBuild()
        for k in range(KC):
            for j, (a, b) in enumerate([(0, 1024), (1024, 1280)]):
                if (k + j) % 2 == 0:
                    ts("dve", XM[:, k, a:b], X[:, k, a:b], sc1p[:, k, j:j + 1], sh1[:, k, j:j + 1], ALU.mult, ALU.add,
                       [("X", k), "sc1p", "mods"], [("XM", k)])
                else:
                    act(XM[:, k, a:b], X[:, k, a:b], AF.Identity, [("X", k), "sc1p", "mods"], [("XM", k)],
                        scale=sc1p[:, k, j:j + 1], bias=sh1[:, k, j:j + 1])
        if _STOP[0] == 2:
            raise _StopBuild()
        for k4 in range(4):
            dma("sp", xsp_d[:, 4 * k4:4 * k4 + 4, :], X[:, 4 * k4:4 * k4 + 4, :], XK[4 * k4:4 * k4 + 4], ["xspill"],
                ("xsp", k4))
        barrier(XK + ATT_TEMPK + HG_TEMPK)
        dma("sp", cosT, cos_d, [], ["cosT"], ("c", "cosT"))
        dma("sp", sinT, sin_d, [], ["sinT"], ("c", "sinT"))
        dma("pool", masks, msk_d, [], ["masks"], ("c", "masks"))
        dma("pool", ckT, ckT_d[l], [], ["ckT"], ("c", "ckT"))
        dma("pool", cvt, cv_d[l], [], ["cvt"], ("c", "cvt"))

        if _STOP[0] == 3:
            raise _StopBuild()
        wt = [0]

        def project(ri, a, n, pbank):
            mm_group(PS[pbank][:, 0:n], [(RING[ri][:, k, :], XM[:, k, a:a + n]) for k in range(KC)],
                     reads=[("ring", ri)] + XMK, writes=[("ps", pbank)])

        def rope_pair(dst_fn, keyname, raw_out=None):
            r0 = load_w(win_d[l][wt[0]])
            r1 = load_w(win_d[l][wt[0] + 1])
            wt[0] += 2
            for ti, (a, n) in enumerate(TT):
                p0 = next_ps()
                p1 = next_ps()
                project(r0, a, n, p0)
                project(r1, a, n, p1)
                tA = tmpA[ti % 2]
                tB = tmpB[ti % 2]
                if raw_out is not None:
                    vcopy(raw_out[:, a:a + n], PS[p0][:, 0:n], [("ps", p0)], ["kraw"])
                tt("dve", tA[:, 0:n], PS[p0][:, 0:n], cosT[:, a:a + n], ALU.mult, [("ps", p0), "cosT"], [("tmpA", ti % 2)])
                tt("dve", tB[:, 0:n], PS[p1][:, 0:n], sinT[:, a:a + n], ALU.mult, [("ps", p1), "sinT"], [("tmpB", ti % 2)])
                tt("dve", dst_fn(a, n), tA[:, 0:n], tB[:, 0:n], ALU.add, [("tmpA", ti % 2), ("tmpB", ti % 2)], [keyname])

        for g in range(2):
            if _STOP[0] == 34 and g == 1:
                raise _StopBuild()
            for r in range(4):
                rope_pair(lambda a, n, r=r: qT[:, r, a:a + n], ("qT", r))
            if _STOP[0] == 30:
                raise _StopBuild()
            rope_pair(lambda a, n: kT[:, a:a + n], "kT", raw_out=kraw)
            if _STOP[0] == 31:
                raise _StopBuild()
            dma("sp", okT_d[l, g], kraw, ["kraw"], [("okT", l, g)], ("okT",))
            if _STOP[0] == 32:
                raise _StopBuild()
            if g == 0:
                rv = [load_w(win_d[l][wt[0]]), load_w(win_d[l][wt[0] + 1])]
                wt[0] += 2
                for B in range(NBLK):
                    pv = next_ps()
                    for vg in range(2):
                        mm_group(PS[pv][:, vg * 128:(vg + 1) * 128],
                                 [(XM[:, k, B * 128:(B + 1) * 128], RING[rv[vg]][:, k, :]) for k in range(KC)],
                                 reads=[("ring", rv[vg])] + XMK, writes=[("ps", pv)])
                    act(vtok[:, B, :], PS[pv][:, 0:256], AF.Identity, [("ps", pv)], [("vtok", B)])
                    vcopy(vraw[B % 2], PS[pv][:, 0:256], [("ps", pv)], ["vraw%d" % (B % 2)])
                    dma("sp", ov_d[l, B], vraw[B % 2], ["vraw%d" % (B % 2)], [("ov", l, B)], ("ov", B % 2))
            if _STOP[0] == 33:
                raise _StopBuild()
            for B in range(NBLK):
                kbs = []
                if B - 1 >= 0:
                    kbs.append(("loc", B - 1, 0))
                kbs.append(("loc", B, None))
                if B + 1 < NBLK:
                    kbs.append(("loc", B + 1, 2))
                for j in range(4):
                    kbs.append(("ctx", j, None))
                po = next_ps()
                pd = next_ps(exclude=(po,))
                qrhs = qT[:, :, B * 128:(B + 1) * 128]
                qkeys = [("qT", r) for r in range(4)]
                nkb = len(kbs)
                sbank = {}

                def issue_s(i):
                    kind, j, mi = kbs[i]
                    pb = next_ps(exclude=(po, pd))
                    sbank[i] = pb
                    if kind == "loc":
                        lhs = kT[:, j * 128:(j + 1) * 128]
                        rd = ["kT"]
                    else:
                        lhs = ckT[:, g, j * 128:(j + 1) * 128]
                        rd = ["ckT"]
                    mm_group(r4(PS[pb][:]), [(lhs, qrhs)], reads=rd + qkeys, writes=[("ps", pb)])

                issue_s(0)
                for i in range(nkb):
                    if i + 1 < nkb:
                        issue_s(i + 1)
                    kind, j, mi = kbs[i]
                    pb = sbank[i]
                    et = eT[i % 4]
                    ek = ("eT", i % 4)
                    act(et, PS[pb][:], AF.Exp, [("ps", pb)], [ek], scale=ATT_SCALE)
                    if kind == "loc" and mi is not None:
                        tt("dve", r4(et), r4(et), masks[:, B, mi, :].unsqueeze(1).to_broadcast([128, 4, 128]),
                           ALU.mult, [ek, "masks"], [ek])
                    if kind == "ctx":
                        ts("dve", et, et, ctxf[:, B:B + 1], None, ALU.mult, None, [ek, "ctxf"], [ek])
                    if kind == "loc":
                        vl = vtok[:, j, g * 128:(g + 1) * 128]
                        vrd = [("vtok", j)]
                    else:
                        vl = cvt[:, j, g * 128:(g + 1) * 128]
                        vrd = ["cvt"]
                    mm1(PS[po][:], vl, et, (i == 0), (i == nkb - 1), vrd + [ek], [("ps", po)])
                    mm1(PS[pd][:], ones[:], et, (i == 0), (i == nkb - 1), ["ones", ek], [("ps", pd)])
                tA = tmpA[B % 2]
                tk = ("tmpA", B % 2)
                tt("dve", r4(tA), r4(PS[pd][:]),
                   esink[:, l, 4 * g:4 * g + 4].unsqueeze(2).to_broadcast([128, 4, 128]), ALU.add,
                   [("ps", pd), "esink"], [tk])
                S.add("dve", lambda e, tA=tA: e.reciprocal(out=tA, in_=tA), reads=[tk], writes=[tk])
                tt("dve", attraw[g][:, :, B * 128:(B + 1) * 128], r4(PS[po][:]), r4(tA), ALU.mult,
                   [("ps", po), tk], ATTK[g])
        if _STOP[0] == 4:
            raise _StopBuild()
        for ti, (a, n) in enumerate(TT):
            pss = next_ps()
            for h in range(8):
                o_ = eT[h % 4]
                act(o_[:, 0:n], attraw[h // 4][:, h % 4, a:a + n], AF.Square, ATTK[h // 4], [("eT", h % 4)])
                mm1(PS[pss][:, 0:n], ones[:], o_[:, 0:n], (h == 0), (h == 7), ["ones", ("eT", h % 4)], [("ps", pss)])
            tA = tmpA[ti % 2]
            tk = ("tmpA", ti % 2)
            act(tA[:, 0:n], PS[pss][:, 0:n], AF.Ln, [("ps", pss)], [tk], scale=1.0 / 1024.0, bias=RMS_EPS)
            act(tA[:, 0:n], tA[:, 0:n], AF.Exp, [tk], [tk], scale=-0.5)
            for h in range(8):
                stt(MIX[:, h, a:a + n], attraw[h // 4][:, h % 4, a:a + n], attg[:, l, h:h + 1], tA[:, 0:n],
                    ALU.mult, ALU.mult, ATTK[h // 4] + ["attg", tk], [("MIX", h)])

        if _STOP[0] == 5:
            raise _StopBuild()
        barrier(ATT_TEMPK + HG_TEMPK)
        ident = cmats[:, 0, :]
        for h in range(8):
            rts = [load_w(win_d[l][wt[0] + i]) for i in range(5)]
            wt[0] += 5
            for ti, (a, n) in enumerate(TT):
                pb = [next_ps() for _ in range(5)]
                for i in range(5):
                    project(rts[i], a, n, pb[i])
                act(qf[:, a:a + n], PS[pb[0]][:, 0:n], AF.Silu, [("ps", pb[0])], ["qf"])
                act(sigf[0][:, a:a + n], PS[pb[1]][:, 0:n], AF.Sigmoid, [("ps", pb[1])], ["sig0"])
                act(sigf[1][:, a:a + n], PS[pb[2]][:, 0:n], AF.Sigmoid, [("ps", pb[2])], ["sig1"])
                act(vTh[:, a:a + n], PS[pb[3]][:, 0:n], AF.Identity, [("ps", pb[3])], ["kmid"])
                act(gsil[:, a:a + n], PS[pb[4]][:, 0:n], AF.Silu, [("ps", pb[4])], ["gsil"])
            for B in range(NBLK):
                pt = next_ps()
                ptv = PS[pt][:].bitcast(BF16)[:, 0:128]
                S.add("pe", lambda e, ptv=ptv, B=B: e.transpose(ptv, vTh[:, B * 128:(B + 1) * 128], ident),
                      reads=["kmid", "cmats"], writes=[("ps", pt)])
                act(vtokH[:, B, :], ptv, AF.Identity, [("ps", pt)], ["vtokH"])
            pos = [next_ps() for _ in range(3)]
            for i in range(3):
                vmemset(PS[pos[i]][:], 0.0, [("ps", pos[i])])

            def o_ap(t0, n):
                return PS[pos[t0 // 512]][:, t0 % 512:t0 % 512 + n]

            for dr in range(2):
                fwd = (dr == 0)
                ts("dve", kk, sigf[dr], oml[:, dr, l, h:h + 1], lbv[:, dr, l, h:h + 1], ALU.mult, ALU.add,
                   ["sig%d" % dr, "oml", "lbv"], ["kk"])
                act(logf, kk, AF.Ln, ["kk"], ["E1"])
                ts("dve", kk, kk, -1.0, 1.0, ALU.mult, ALU.add, ["kk", "E1"], ["kk"])
                S.add("dve", lambda e: e.tensor_tensor_scan(out=bc, data0=scanm[:].rearrange("p c s -> p (c s)"),
                                                            data1=logf, initial=0.0, op0=ALU.mult, op1=ALU.add),
                      reads=["scanm", "E1"], writes=["bc"])
                if not fwd:
                    tt("dve", d1, logf, bc, ALU.subtract, ["E1", "bc"], ["d1"])
                    tt("dve", c32(bc), c32(d1), c32(bc)[:, :, 31:32].to_broadcast([128, NCH, 32]), ALU.add,
                       ["d1", "bc"], ["bc"])
                mid = 15 if fwd else 16
                lastp = 31 if fwd else 0
                bmid = c32(bc)[:, :, mid:mid + 1].to_broadcast([128, NCH, 32])
                blast = c32(bc)[:, :, lastp:lastp + 1].to_broadcast([128, NCH, 32])
                tt("dve", c32(d1), c32(bc), bmid, ALU.subtract, ["bc"], ["d1"])
                act(E1, d1, AF.Exp, ["d1"], ["E1"])
                tt("dve", qmid, qf, E1, ALU.mult, ["qf", "E1"], ["qmid"])
                act(E1, d1, AF.Exp, ["d1", "qmid"], ["E1"], scale=-1.0)
                tt("dve", kmid, kk, E1, ALU.mult, ["kk", "E1"], ["kmid"])
                tt("dve", c32(d1), blast, c32(bc), ALU.subtract, ["bc", "kmid"], ["d1"])
                act(E1, d1, AF.Exp, ["d1"], ["E1"])
                tt("dve", klast, kk, E1, ALU.mult, ["kk", "E1"], ["klast"])
                act(E1, bc, AF.Exp, ["bc", "klast"], ["E1"])
                tt("dve", qb, qf, E1, ALU.mult, ["qf", "E1"], ["qb"])
                act(decay.unsqueeze(2), c32(bc)[:, :, lastp:lastp + 1], AF.Exp, ["bc"], ["decay"])
                for B in range(NBLK):
                    pt = next_ps(exclude=pos)
                    ptv = PS[pt][:].bitcast(BF16)[:, 0:128]
                    S.add("pe", lambda e, ptv=ptv, B=B: e.transpose(ptv, klast[:, B * 128:(B + 1) * 128], ident),
                          reads=["klast", "cmats"], writes=[("ps", pt)])
                    for c4 in range(4):
                        act(klastM[:, 4 * B + c4, :], ptv, AF.Identity, [("ps", pt), "rmask"], ["klastM"],
                            scale=rmask[:, c4:c4 + 1])
                mk = cmats[:, 1 if fwd else 2, :]
                for B in range(NBLK):
                    pa = next_ps(exclude=pos)
                    mm_group(PS[pa][:, 0:128], [(kmid[:, B * 128:(B + 1) * 128], qmid[:, B * 128:(B + 1) * 128])],
                             reads=["kmid", "qmid"], writes=[("ps", pa)])
                    tt("dve", aT[B % 2], PS[pa][:, 0:128], mk, ALU.mult, [("ps", pa), "cmats"], ["aT%d" % (B % 2)])
                    mm1(o_ap(B * 128, 128), vtokH[:, B, :], aT[B % 2], False, False,
                        ["vtokH", "aT%d" % (B % 2)], [("ps", pos[(B * 128) // 512])])
                order = list(range(NCH)) if fwd else list(range(NCH - 1, -1, -1))
                cur = 0
                for c in order:
                    slot = c // 8
                    first_in_slot = (c % 8 == 0) if fwd else (c % 8 == 7)
                    if first_in_slot:
                        nxt = Sf[(cur + 1) % 2]
                        nk = "Sf%d" % ((cur + 1) % 2)
                        if slot == 4:
                            vmemset(nxt, 0.0, [nk])
                        elif (fwd and slot == 0) or ((not fwd) and slot == 3):
                            dma("sp", S0t, s0_d[l, dr, h], [], ["S0t"], ("c", "S0t"))
                            vcopy(nxt, S0t, ["S0t"], [nk])
                        else:
                            ts("dve", nxt, Sf[cur], cfl[:, 0:1], None, ALU.mult, None, ["Sf%d" % cur, "cfl"], [nk])
                        cur = (cur + 1) % 2
                    act(Sall[:, c, :], Sf[cur], AF.Identity, ["Sf%d" % cur], ["Sall"])
                    pu = next_ps(exclude=pos)
                    mm_group(PS[pu][:, 0:128], [(klastM[:, c, :], vtokH[:, c // 4, :])], reads=["klastM", "vtokH"],
                             writes=[("ps", pu)])
                    nxt = Sf[(cur + 1) % 2]
                    nk = "Sf%d" % ((cur + 1) % 2)
                    stt(nxt, Sf[cur], decay[:, c:c + 1], PS[pu][:, 0:128], ALU.mult, ALU.add,
                        ["Sf%d" % cur, "decay", ("ps", pu)], [nk])
                    cur = (cur + 1) % 2
                    last_in_slot = (c % 8 == 7) if fwd else (c % 8 == 0)
                    if last_in_slot:
                        vcopy(Sfin[:, slot, :], Sf[cur], ["Sf%d" % cur], ["Sfin"])
                dma("sp", ost_d[l, dr, h], Sfin, ["Sfin"], [("ost", l, dr, h)], ("ost",))
                for c in range(NCH):
                    mm1(o_ap(c * 32, 32), Sall[:, c, :], qb[:, c * 32:(c + 1) * 32], False, False,
                        ["Sall", "qb"], [("ps", pos[(c * 32) // 512])])
            for ti, (a, n) in enumerate(TT):
                act(osq[:, 0:n], PS[pos[ti]][:, 0:n], AF.Square, [("ps", pos[ti]), "d1"], ["d1"])
                pss = next_ps(exclude=pos)
                mm_group(PS[pss][:, 0:n], [(ones[:], osq[:, 0:n])], reads=["ones", "d1"], writes=[("ps", pss)])
                act(rstdh[:, 0:n], PS[pss][:, 0:n], AF.Ln, [("ps", pss), "d1"], ["d1"], scale=1.0 / 128.0, bias=RMS_EPS)
                act(rstdh[:, 0:n], rstdh[:, 0:n], AF.Exp, ["d1"], ["d1"], scale=-0.5)
                tt("dve", th[:, 0:n], PS[pos[ti]][:, 0:n], rstdh[:, 0:n], ALU.mult, [("ps", pos[ti]), "d1"], ["d1"])
                stt(MIX[:, 8 + h, a:a + n], th[:, 0:n], hgg[:, l:l + 1], gsil[:, a:a + n], ALU.mult, ALU.mult,
                    ["d1", "hgg", "gsil"], [("MIX", 8 + h)])
        assert wt[0] == N_IN_TILES

        if _STOP[0] == 6:
            raise _StopBuild()
        barrier(ATT_TEMPK + HG_TEMPK + XK)
        for k4 in range(4):
            dma("sp", X[:, 4 * k4:4 * k4 + 4, :], xsp_d[:, 4 * k4:4 * k4 + 4, :], ["xspill"], XK[4 * k4:4 * k4 + 4],
                ("xin", k4))
        for k in range(KC):
            act(X[:, k, :], X[:, k, :], AF.Identity, [("X", k)], [("X", k)], scale=ALPHA)

        if _STOP[0] == 7:
            raise _StopBuild()
        def resid_accum(oc, ti, a, n, pbank, gate):
            j = 0 if ti < 2 else 1
            stt(X[:, oc, a:a + n], PS[pbank][:, 0:n], gate[:, oc, j:j + 1], X[:, oc, a:a + n], ALU.mult, ALU.add,
                [("ps", pbank), "mods", ("X", oc)], [("X", oc)])

        for oc in range(KC):
            ri = load_w(wo_d[l][oc])
            for ti, (a, n) in enumerate(TT):
                pb = next_ps()
                mm_group(PS[pb][:, 0:n], [(RING[ri][:, k, :], MIX[:, k, a:a + n]) for k in range(KC)],
                         reads=[("ring", ri)] + MIXK, writes=[("ps", pb)])
                resid_accum(oc, ti, a, n, pb, g1)

        if _STOP[0] == 8:
            raise _StopBuild()
        def layer_norm(which, write_xm):
            for ti, (a, n) in enumerate(TT):
                j = 0 if ti < 2 else 1
                for k in range(KC):
                    act(xb[:, k, 0:n], X[:, k, a:a + n], AF.Identity, [("X", k)], XBK)
                    act(xsq[:, k, 0:n], X[:, k, a:a + n], AF.Square, [("X", k)], XSQK)
                p1 = next_ps()
                p2 = next_ps()
                mm_group(PS[p1][:, 0:n], [(ones[:], xb[:, k, 0:n]) for k in range(KC)], reads=["ones"] + XBK,
                         writes=[("ps", p1)])
                mm_group(PS[p2][:, 0:n], [(ones[:], xsq[:, k, 0:n]) for k in range(KC)], reads=["ones"] + XSQK,
                         writes=[("ps", p2)])
                ts("dve", mean[:, 0:n], PS[p1][:, 0:n], 1.0 / D, None, ALU.mult, None, [("ps", p1)], ["mean"])
                tt("dve", rtmp[:, 0:n], mean[:, 0:n], mean[:, 0:n], ALU.mult, ["mean"], ["rtmp"])
                stt(rstd[:, 0:n], PS[p2][:, 0:n], 1.0 / D, rtmp[:, 0:n], ALU.mult, ALU.subtract,
                    [("ps", p2), "rtmp"], ["rstd"])
                act(rstd[:, 0:n], rstd[:, 0:n], AF.Ln, ["rstd"], ["rstd"], bias=LN_EPS)
                act(rstd[:, 0:n], rstd[:, 0:n], AF.Exp, ["rstd"], ["rstd"], scale=-0.5)
                for k in range(KC):
                    tt("dve", X[:, k, a:a + n], X[:, k, a:a + n], mean[:, 0:n], ALU.subtract, [("X", k), "mean"],
                       [("X", k)])
                    tt("dve", X[:, k, a:a + n], X[:, k, a:a + n], rstd[:, 0:n], ALU.mult, [("X", k), "rstd"],
                       [("X", k)])
                    act(X[:, k, a:a + n], X[:, k, a:a + n], AF.Identity, [("X", k), "lng", "lnb"], [("X", k)],
                        scale=lng[:, l, which, k:k + 1], bias=lnb[:, l, which, k:k + 1])
                    if write_xm:
                        ts("dve", XM[:, k, a:a + n], X[:, k, a:a + n], sc2p[:, k, j:j + 1], sh2[:, k, j:j + 1],
                           ALU.mult, ALU.add, [("X", k), "sc2p", "mods"], [("XM", k)])

        layer_norm(0, True)
        for k in range(KC):
            act(X[:, k, :], X[:, k, :], AF.Identity, [("X", k)], [("X", k)], scale=ALPHA)

        if _STOP[0] == 9:
            raise _StopBuild()
        for grp in range(4):
            for jj in range(16):
                ri = load_w(wup_d[l][grp * 16 + jj])
                for ti, (a, n) in enumerate(TT):
                    pb = next_ps()
                    project(ri, a, n, pb)
                    rl = tmp_relu[ti % 2]
                    act(rl[:, 0:n], PS[pb][:, 0:n], AF.Relu, [("ps", pb)], [("relu", ti % 2)])
                    tt("dve", HID[:, jj, a:a + n], rl[:, 0:n], rl[:, 0:n], ALU.mult, [("relu", ti % 2)], [("MIX", jj)])
            for oc in range(KC):
                ri = load_w(wdn_d[l][grp * 16 + oc])
                for ti, (a, n) in enumerate(TT):
                    pb = next_ps()
                    mm_group(PS[pb][:, 0:n], [(RING[ri][:, k, :], HID[:, k, a:a + n]) for k in range(KC)],
                             reads=[("ring", ri)] + MIXK, writes=[("ps", pb)])
                    resid_accum(oc, ti, a, n, pb, g2)
        layer_norm(1, False)

    try:
        for l in range(L):
            layer(l)
    except _StopBuild:
        pass
    for k4 in range(4):
        dma("sp", yT_d[:, 4 * k4:4 * k4 + 4, :], X[:, 4 * k4:4 * k4 + 4, :], XK[4 * k4:4 * k4 + 4], [("yT", k4)],
            ("yout", k4))

    with nc.Block() as block:
        S.emit(nc, stack, block)
    stack.close()
    return nc


def _core_slots(core):
    if core < 2:
        return [("s", core, 256 * s) for s in range(4)] + [("p", 30 + core, 0)]
    return [("p", (core - 2) * 5 + s, 0) for s in range(5)]


def _fm(vec2d):
    a = np.asarray(vec2d, np.float32)
    n = a.shape[-1] // 128
    a = a.reshape(a.shape[:-1] + (n, 128))
    return np.ascontiguousarray(np.moveaxis(a, -1, 0))


def _retile(w):
    K, N = w.shape
    n = N // 128
    return np.ascontiguousarray(w.reshape(K // 128, 128, n, 128).transpose(2, 1, 0, 3)).reshape(n, 128, (K // 128) * 128)


def _win_cols():
    perm = np.arange(128) ^ 32
    ar = np.arange(128)
    cols = []
    for g in range(2):
        for r in range(4):
            h = 4 * g + r
            cols.append(h * 128 + ar)
            cols.append(h * 128 + perm)
        cols.append(1024 + g * 128 + ar)
        cols.append(1024 + g * 128 + perm)
        if g == 0:
            cols.append(1280 + ar)
            cols.append(1280 + 128 + ar)
    for h in range(8):
        for base in (1536, 2560, 3584, 4608, 5632):
            cols.append(base + h * 128 + ar)
    return np.concatenate(cols)


def _rope_tables(slots):
    cosT = np.ones((128, NTOK), np.float32)
    sinT = np.zeros((128, NTOK), np.float32)
    inv = (np.float32(10000.0) ** (-np.arange(32, dtype=np.float32) / np.float32(32))).astype(np.float32)
    for s, (kind, seq, start) in enumerate(slots):
        if kind != "s":
            continue
        t = start + np.arange(256)
        row = (t // 64).astype(np.float32)
        col = (t % 64).astype(np.float32)
        ang_r = (row[None, :] * inv[:, None]).astype(np.float32)
        ang_c = (col[None, :] * inv[:, None]).astype(np.float32)
        sl = slice(256 * s, 256 * (s + 1))
        cr, sr, cc, sc_ = np.cos(ang_r), np.sin(ang_r), np.cos(ang_c), np.sin(ang_c)
        cosT[0:32, sl] = cr
        cosT[32:64, sl] = cr
        cosT[64:96, sl] = cc
        cosT[96:128, sl] = cc
        sinT[0:32, sl] = -sr
        sinT[32:64, sl] = sr
        sinT[64:96, sl] = -sc_
        sinT[96:128, sl] = sc_
    return cosT, sinT


def _masks(slots):
    m = np.zeros((128, NBLK, 3, 128), np.float32)
    ctxflag = np.zeros((128, NBLK), np.float32)
    ar = np.arange(128)
    for B in range(NBLK):
        kind, seq, start = slots[B // 2]
        qpos = start + (B % 2) * 128 + ar
        if kind == "s":
            ctxflag[:, B] = 1.0
        for nb, KB in ((0, B - 1), (1, B), (2, B + 1)):
            if KB < 0 or KB >= NBLK:
                continue
            kkind, kseq, kstart = slots[KB // 2]
            if (kkind, kseq) != (kind, seq):
                continue
            kpos = kstart + (KB % 2) * 128 + ar
            if kind == "s":
                ok = np.abs(kpos[:, None] - qpos[None, :]) <= 128
            else:
                ok = np.ones((128, 128), bool)
            m[:, B, nb, :] = ok.astype(np.float32)
    cfl = np.full((128, 1), 1.0 if slots[0][0] == "s" else 0.0, np.float32)
    return m, ctxflag, cfl


def _const_mats():
    cm = np.zeros((4, 128, 128), np.float32)
    ar = np.arange(128)
    cm[0] = np.eye(128, dtype=np.float32)
    same = (ar[:, None] // 32) == (ar[None, :] // 32)
    cm[1] = (same & (ar[:, None] <= ar[None, :])).astype(np.float32)
    cm[2] = (same & (ar[:, None] >= ar[None, :])).astype(np.float32)
    rowmask = np.zeros((128, 4), np.float32)
    for c in range(4):
        rowmask[32 * c:32 * (c + 1), c] = 1.0
    return cm, rowmask


def _prepare(inp, L, cores):
    f = lambda k: np.asarray(inp[k], np.float32)
    w_in = f("w_in")
    cols = _win_cols()
    shared = {}
    for l in range(L):
        shared[f"wmod{l}"] = _retile(f("w_mod")[l])
        shared[f"win{l}"] = _retile(w_in[l][:, cols])
        shared[f"wo{l}"] = _retile(f("w_o")[l])
        shared[f"wup{l}"] = _retile(f("w_up")[l])
        shared[f"wdn{l}"] = np.ascontiguousarray(
            f("w_down")[l].reshape(4, 16, 128, 16, 128).transpose(0, 3, 2, 1, 4)).reshape(64, 128, 2048)
    shared.update({
        "bmod": _fm(f("b_mod")[:L]),
        "lng": _fm(f("ln_g")[:L]),
        "lnb": _fm(f("ln_b")[:L]),
        "attg": _fm(f("attn_norm_g")[:L]),
        "hgg": np.ascontiguousarray(f("hg_norm_g")[:L].T),
        "lbl": _fm(f("hg_lb_logits")),
        "sink": np.ascontiguousarray(np.broadcast_to(f("attn_sink")[:L][None], (128, L, 8))),
    })
    cm, rowmask = _const_mats()
    shared["cmats"] = cm
    shared["rowmask"] = rowmask
    xp, xs = f("x_prompt"), f("x_sample")
    maps = []
    for core in cores:
        slots = _core_slots(core)
        xs_ = []
        for kind, seq, start in slots:
            xs_.append(xs[seq, start:start + 256] if kind == "s" else xp[seq])
        xc = np.concatenate(xs_, 0)
        d = dict(shared)
        d["xT"] = np.ascontiguousarray(xc.reshape(NTOK, KC, 128).transpose(2, 1, 0))
        condA = f("c")[core] if core < 2 else f("c_ctx")
        condB = f("c_ctx")
        d["cond"] = _fm(np.stack([condA, condB]))
        d["cond"] = np.ascontiguousarray(d["cond"].transpose(0, 2, 1))
        cosT, sinT = _rope_tables(slots)
        d["cosT"], d["sinT"] = cosT, sinT
        m, ctxflag, cfl = _masks(slots)
        d["masks"], d["ctxflag"], d["cflag"] = m, ctxflag, cfl
        if core < 2:
            ck = f("cache_k")[core][:L]
            cv = f("cache_v")[core][:L]
            d["ckT"] = np.ascontiguousarray(ck.transpose(0, 3, 2, 1))
            d["cv"] = np.ascontiguousarray(cv.reshape(L, 4, 128, 256).transpose(0, 2, 1, 3))
            d["s0"] = np.ascontiguousarray(np.stack([f("state_hgrn_fwd")[core][:L], f("state_hgrn_bwd")[core][:L]], 1))
        else:
            d["ckT"] = np.zeros((L, 128, 2, 512), np.float32)
            d["cv"] = np.zeros((L, 128, 4, 256), np.float32)
            d["s0"] = np.zeros((L, 2, 8, 128, 128), np.float32)
        maps.append(d)
    return maps


_PROG = {}


def _run(inp, L=DEPTH, cores=tuple(range(8)), trace=False):
    if L not in _PROG:
        _PROG[L] = build_program(L)
    nc = _PROG[L]
    maps = _prepare(inp, L, cores)
    res = run_bass_kernel_spmd(nc, maps, core_ids=list(range(len(cores))), **({"trace": True} if trace else {}))
    return res


def _gather(results, L, cores):
    BATCH, SEQ, DEC_BATCH, DEC_SEQ = 32, 256, 2, 1024
    y_p = np.zeros((BATCH, SEQ, D), np.float32)
    y_s = np.zeros((DEC_BATCH, DEC_SEQ, D), np.float32)
    nk = np.zeros((BATCH, L, SEQ, 2, 128), np.float32)
    nv = np.zeros((BATCH, L, SEQ, 2, 128), np.float32)
    sf = np.zeros((BATCH, L, 8, 128, 128), np.float32)
    sbw = np.zeros((BATCH, L, 8, 128, 128), np.float32)
    for ci, core in enumerate(cores):
        r = results[ci]
        y = np.asarray(r["yT"]).transpose(2, 1, 0).reshape(NTOK, D)
        okT = np.asarray(r["okT"])
        ov = np.asarray(r["ov"])
        ost = np.asarray(r["ost"])
        for s, (kind, seq, start) in enumerate(_core_slots(core)):
            ys = y[256 * s:256 * (s + 1)]
            if kind == "s":
                y_s[seq, start:start + 256] = ys
            else:
                y_p[seq] = ys
                nk[seq] = okT[:, :, :, 256 * s:256 * (s + 1)].transpose(0, 3, 1, 2)
                nv[seq] = ov[:, 2 * s:2 * s + 2].reshape(L, 256, 2, 128)
                sf[seq] = ost[:, 0, :, :, s, :]
                sbw[seq] = ost[:, 1, :, :, s, :]
    return y_p, y_s, nk, nv, sf, sbw


def kernel(**inputs):
    res = _run(inputs, DEPTH, tuple(range(8)))
    return _gather(res.results, DEPTH, tuple(range(8)))
```

```python
from contextlib import ExitStack
import numpy as np
import concourse.bass as bass
import concourse.mybir as mybir
from concourse.bass_utils import run_bass_kernel_spmd

F32 = mybir.dt.float32
BF16 = mybir.dt.bfloat16
AF = mybir.ActivationFunctionType
ALU = mybir.AluOpType

D = 2048
KC = 16
NTOK = 1280
NBLK = 10
NCH = 40
NSLOT = 5
DEPTH = 4
ALPHA = (2 * DEPTH) ** 0.25
ATT_SCALE = 128 ** -0.5
LN_EPS = 1e-5
RMS_EPS = 1e-6
TT = [(0, 512), (512, 512), (1024, 256)]
NRING = 6
N_IN_TILES = 62
_STOP = [None]


class _StopBuild(Exception):
    pass


class _Op:
    __slots__ = ("eng", "fn", "deps", "sig", "sem", "tick", "dma", "ndma", "epoch", "waits", "vc")


class Sched:
    def __init__(self):
        self.ops = []
        self.lastw = {}
        self.readers = {}
        self.epoch = 0

    def add(self, eng, fn, reads=(), writes=(), dma=None, ndma=1):
        op = _Op()
        op.eng, op.fn, op.dma, op.ndma, op.sig = eng, fn, dma, ndma, False
        op.epoch = self.epoch
        deps = {}
        psr = [r for r in reads if isinstance(r, tuple) and r and r[0] == "ps"]
        if psr:
            reads = [r for r in reads if r not in psr]
            writes = list(writes) + [r for r in psr if r not in writes]

        def consider(d, kind):
            if d is None or d is op:
                return
            if d.dma is None and op.dma is None and d.eng == eng:
                if eng == "pe":
                    return
            deps[id(d)] = d

        for r in reads:
            consider(self.lastw.get(r), "raw")
        for w in writes:
            consider(self.lastw.get(w), "waw")
            for rd in self.readers.get(w, ()):
                consider(rd, "war")
        op.deps = list(deps.values())
        for d in op.deps:
            d.sig = True
        for r in reads:
            self.readers.setdefault(r, []).append(op)
        for w in writes:
            self.lastw[w] = op
            self.readers[w] = []
        self.ops.append(op)
        return op

    def emit(self, nc, stack, block):
        sems = {}

        def get_sem(key):
            if key not in sems:
                sems[key] = stack.enter_context(nc.semaphore("s_" + "_".join(str(k) for k in key)))
            return sems[key]

        counts = {}
        for op in self.ops:
            if op.dma is not None:
                key = ("d", op.dma)
                counts[key] = counts.get(key, 0) + 16 * op.ndma
            else:
                key = ("e", op.eng, op.epoch)
                if op.sig:
                    counts[key] = counts.get(key, 0) + 1
            op.sem = key
            op.tick = counts.get(key, 0)
        known = {}
        for op in self.ops:
            k = known.setdefault(op.eng, {})
            need = []
            for d in sorted(op.deps, key=lambda d_: -d_.tick):
                if k.get(d.sem, 0) < d.tick:
                    need.append((d.sem, d.tick))
                    k[d.sem] = d.tick
                    for s_, t_ in d.vc.items():
                        if k.get(s_, 0) < t_:
                            k[s_] = t_
            op.waits = need
            op.vc = dict(k) if (op.sig or op.dma is not None) else None
        by_eng = {}
        for op in self.ops:
            by_eng.setdefault(op.eng, []).append(op)
        for op in self.ops:
            if op.sig or op.dma is not None:
                get_sem(op.sem)

        def run(engname, e):
            for op in by_eng.get(engname, []):
                waits = op.waits
                for sem_, tick_ in waits[1:]:
                    e.wait_ge(get_sem(sem_), tick_)
                ins = op.fn(e)
                if isinstance(ins, tuple):
                    first, last = ins
                elif isinstance(ins, list):
                    first, last = ins[0], ins
                else:
                    first = last = ins
                if waits:
                    first.wait_op(get_sem(waits[0][0]), waits[0][1], "sem-ge")
                if op.dma is not None:
                    lst = last if isinstance(last, list) else [last]
                    assert len(lst) == op.ndma, (len(lst), op.ndma)
                    for i_ in lst:
                        i_.then_inc(get_sem(op.sem), 16)
                elif op.sig:
                    last.then_inc(get_sem(op.sem), 1)
            done = {}
            for op in by_eng.get(engname, []):
                if op.dma is not None:
                    done[op.sem] = max(done.get(op.sem, 0), op.tick)
            for sem_, tick_ in done.items():
                e.wait_ge(get_sem(sem_), tick_)

        @block.tensor
        def _(e):
            run("pe", e)

        @block.scalar
        def _(e):
            run("act", e)

        @block.vector
        def _(e):
            run("dve", e)

        @block.gpsimd
        def _(e):
            run("pool", e)

        @block.sync
        def _(e):
            run("sp", e)


def build_program(n_layers=DEPTH):
    L = n_layers
    nc = bass.Bass("TRN2", target_bir_lowering=False)

    def din(name, shape):
        return nc.dram_tensor(name, list(shape), F32, kind="ExternalInput").ap()

    def dout(name, shape):
        return nc.dram_tensor(name, list(shape), F32, kind="ExternalOutput").ap()

    xT_d = din("xT", [128, KC, NTOK])
    cond_d = din("cond", [128, KC, 2])
    bmod_d = din("bmod", [128, L, 96])
    lng_d = din("lng", [128, L, 2, KC])
    lnb_d = din("lnb", [128, L, 2, KC])
    attg_d = din("attg", [128, L, 8])
    hgg_d = din("hgg", [128, L])
    lbl_d = din("lbl", [128, 2, DEPTH, 8])
    sink_d = din("sink", [128, L, 8])
    cos_d = din("cosT", [128, NTOK])
    sin_d = din("sinT", [128, NTOK])
    msk_d = din("masks", [128, NBLK, 3, 128])
    ctxf_d = din("ctxflag", [128, NBLK])
    cfl_d = din("cflag", [128, 1])
    ckT_d = din("ckT", [L, 128, 2, 512])
    cv_d = din("cv", [L, 128, 4, 256])
    s0_d = din("s0", [L, 2, 8, 128, 128])
    cm_d = din("cmats", [4, 128, 128])
    rmask_d = din("rowmask", [128, 4])
    wmod_d = [din(f"wmod{l}", [96, 128, 2048]) for l in range(L)]
    win_d = [din(f"win{l}", [N_IN_TILES, 128, 2048]) for l in range(L)]
    wo_d = [din(f"wo{l}", [16, 128, 2048]) for l in range(L)]
    wup_d = [din(f"wup{l}", [64, 128, 2048]) for l in range(L)]
    wdn_d = [din(f"wdn{l}", [64, 128, 2048]) for l in range(L)]

    yT_d = dout("yT", [128, KC, NTOK])
    okT_d = dout("okT", [L, 2, 128, NTOK])
    ov_d = dout("ov", [L, NBLK, 128, 256])
    ost_d = dout("ost", [L, 2, 8, 128, NSLOT, 128])
    xsp_d = nc.dram_tensor("xspill", [128, KC, NTOK], F32, kind="Internal").ap()

    S = Sched()
    stack = ExitStack()

    def sb(name, shape, dt):
        return stack.enter_context(nc.sbuf_tensor(name, list(shape), dt))

    RA = sb("RA", [128, KC * NTOK], F32)
    XM = sb("XM", [128, KC, NTOK], BF16)
    RC = sb("RC", [128, KC * NTOK], BF16)
    RING = [sb(f"ring{i}", [128, KC, 128], BF16) for i in range(NRING)]
    X = RA[:].rearrange("p (k t) -> p k t", k=KC)
    MIX = RC[:].rearrange("p (k t) -> p k t", k=KC)
    HID = MIX

    cond_f = sb("cond_f", [128, KC, 2], F32)
    condS = sb("condS", [128, KC, 2], BF16)
    bmod = sb("bmod_s", [128, L, 96], F32)
    lng = sb("lng_s", [128, L, 2, KC], F32)
    lnb = sb("lnb_s", [128, L, 2, KC], F32)
    attg = sb("attg_s", [128, L, 8], F32)
    hgg = sb("hgg_s", [128, L], F32)
    lbl = sb("lbl_s", [128, 2, DEPTH, 8], F32)
    lbv = sb("lbv", [128, 2, DEPTH, 8], F32)
    oml = sb("oml", [128, 2, DEPTH, 8], F32)
    lbsum = sb("lbsum", [128, 2, 8], F32)
    esink = sb("esink", [128, L, 8], F32)
    ctxf = sb("ctxf", [128, NBLK], F32)
    cfl = sb("cfl", [128, 1], F32)
    rmask = sb("rmask", [128, 4], F32)
    cmats = sb("cmats_s", [128, 3, 128], BF16)
    ones = sb("ones", [128, 128], BF16)
    scanm = sb("scanm", [128, NCH, 32], BF16)
    mods = sb("mods", [128, 96, 2], F32)
    sc1p = sb("sc1p", [128, KC, 2], F32)
    sc2p = sb("sc2p", [128, KC, 2], F32)
    dummy = sb("dummy_t", [128, 8], F32)

    PS = [stack.enter_context(nc.psum_tensor(f"ps{i}", [128, 512], F32)) for i in range(8)]

    ra_off = [0]

    def ra_f32(nwords):
        o = ra_off[0]
        ra_off[0] += nwords
        assert ra_off[0] <= KC * NTOK, ra_off[0]
        return RA[:, o:o + nwords]

    def ra_bf16(nel):
        assert nel % 2 == 0
        return ra_f32(nel // 2).bitcast(BF16)

    cosT = ra_f32(NTOK)
    sinT = ra_f32(NTOK)
    masks = ra_bf16(NBLK * 3 * 128).rearrange("p (b n q) -> p b n q", b=NBLK, n=3)
    ckT = ra_bf16(2 * 512).rearrange("p (g t) -> p g t", g=2)
    cvt = ra_bf16(4 * 256).rearrange("p (b c) -> p b c", b=4)
    qT = ra_bf16(4 * NTOK).rearrange("p (r t) -> p r t", r=4)
    kT = ra_bf16(NTOK)
    vtok = ra_bf16(NBLK * 256).rearrange("p (b c) -> p b c", b=NBLK)
    kraw = ra_f32(NTOK)
    vraw = [ra_f32(256), ra_f32(256)]
    tmpA = [ra_f32(512), ra_f32(512)]
    tmpB = [ra_f32(512), ra_f32(512)]
    eT = [ra_bf16(512) for _ in range(4)]
    attraw1 = ra_f32(4 * NTOK).rearrange("p (r t) -> p r t", r=4)
    attraw0 = RC[:, 8 * NTOK:16 * NTOK].bitcast(F32).rearrange("p (r t) -> p r t", r=4)
    attraw = [attraw0, attraw1]
    ATTK = [[("MIX", k) for k in range(8, 16)], ["attraw1"]]
    ra_off[0] = 0
    qf = ra_f32(NTOK)
    sigf = [ra_f32(NTOK), ra_f32(NTOK)]
    gsil = ra_f32(NTOK)
    vtokH = ra_bf16(NBLK * 128).rearrange("p (b c) -> p b c", b=NBLK)
    kk = ra_f32(NTOK)
    E1 = ra_f32(NTOK)
    logf = E1
    bc = ra_f32(NTOK)
    d1 = ra_f32(NTOK)
    qmid = ra_bf16(NTOK)
    kmid = ra_bf16(NTOK)
    vTh = kmid
    klast = ra_bf16(NTOK)
    qb = ra_bf16(NTOK)
    decay = ra_f32(NCH)
    klastM = ra_bf16(NCH * 128).rearrange("p (c k) -> p c k", c=NCH)
    aT = [ra_bf16(128), ra_bf16(128)]
    Sall = ra_bf16(NCH * 128).rearrange("p (c v) -> p c v", c=NCH)
    Sf = [ra_f32(128), ra_f32(128)]
    Sfin = ra_f32(NSLOT * 128).rearrange("p (s v) -> p s v", s=NSLOT)
    S0t = ra_f32(128)
    osq = d1[:, 0:256].bitcast(BF16)
    rstdh = d1[:, 256:768]
    th = d1[:, 768:1280]
    xb = RC[:, 0:KC * 512].rearrange("p (k t) -> p k t", k=KC)
    xsq = RC[:, 7 * NTOK:7 * NTOK + KC * 512].rearrange("p (k t) -> p k t", k=KC)
    XBK = [("MIX", k) for k in range(0, 7)]
    XSQK = [("MIX", k) for k in range(7, 14)]
    mean = sb("ln_mean", [128, 512], F32)
    rstd = sb("ln_rstd", [128, 512], F32)
    rtmp = sb("ln_tmp", [128, 512], F32)
    tmp_relu = [sb("relu0", [128, 512], F32), sb("relu1", [128, 512], F32)]

    ps_rr = [0]

    def next_ps(exclude=()):
        while True:
            i = ps_rr[0] % 8
            ps_rr[0] += 1
            if i not in exclude:
                return i

    ring_rr = [0]

    def load_w(src_ap):
        i = ring_rr[0] % NRING
        ring_rr[0] += 1
        dst = RING[i]
        S.add("pool", lambda e, dst=dst, src=src_ap: e.dma_start(
            out=dst[:].rearrange("p k c -> p (k c)"), in_=src),
            writes=[("ring", i)], dma=("ring", i))
        return i

    def mm_group(out_ap, pairs, reads, writes):
        def fn(e, out_ap=out_ap, pairs=pairs):
            ins = first = None
            n = len(pairs)
            for j, (l_, r_) in enumerate(pairs):
                ins = e.matmul(out_ap, l_, r_, start=(j == 0), stop=(j == n - 1))
                if first is None:
                    first = ins
            return (first, ins)
        return S.add("pe", fn, reads=reads, writes=writes)

    def mm1(out_ap, l_, r_, start, stop, reads, writes):
        return S.add("pe", lambda e: e.matmul(out_ap, l_, r_, start=start, stop=stop, skip_group_check=True),
                     reads=reads, writes=writes)

    def act(out, in_, func, reads, writes, scale=None, bias=None):
        kw = {}
        if scale is not None:
            kw["scale"] = scale
        if bias is not None:
            kw["bias"] = bias
        return S.add("act", lambda e: e.activation(out=out, in_=in_, func=func, **kw), reads=reads, writes=writes)

    def tt(eng, out, in0, in1, op, reads, writes):
        return S.add(eng, lambda e: e.tensor_tensor(out=out, in0=in0, in1=in1, op=op), reads=reads, writes=writes)

    def ts(eng, out, in0, s1, s2, op0, op1, reads, writes):
        if op1 is None:
            return S.add(eng, lambda e: e.tensor_scalar(out=out, in0=in0, scalar1=s1, scalar2=None, op0=op0),
                         reads=reads, writes=writes)
        return S.add(eng, lambda e: e.tensor_scalar(out=out, in0=in0, scalar1=s1, scalar2=s2, op0=op0, op1=op1),
                     reads=reads, writes=writes)

    def stt(out, in0, scalar, in1, op0, op1, reads, writes):
        return S.add("dve", lambda e: e.scalar_tensor_tensor(out=out, in0=in0, scalar=scalar, in1=in1,
                                                             op0=op0, op1=op1), reads=reads, writes=writes)

    def vcopy(out, in_, reads, writes):
        return S.add("dve", lambda e: e.tensor_copy(out=out, in_=in_), reads=reads, writes=writes)

    def vmemset(ap, val, writes):
        return S.add("dve", lambda e: e.memset(ap, val), writes=writes)

    def dma(q, out, in_, reads, writes, key):
        return S.add(q, lambda e: e.dma_start(out=out, in_=in_), reads=reads, writes=writes, dma=key)

    def barrier(keys):
        return S.add("dve", lambda e: e.memset(dummy[:, 0:1], 0.0), writes=list(keys))

    XK = [("X", k) for k in range(KC)]
    XMK = [("XM", k) for k in range(KC)]
    MIXK = [("MIX", k) for k in range(KC)]
    ATT_TEMPK = (["cosT", "sinT", "masks", "ckT", "cvt", "kT", "kraw", "vraw0", "vraw1", "attraw1"] +
                 [("qT", r) for r in range(4)] + [("vtok", B) for B in range(NBLK)] +
                 [("tmpA", 0), ("tmpA", 1), ("tmpB", 0), ("tmpB", 1)] + [("eT", i) for i in range(4)])
    HG_TEMPK = ["qf", "sig0", "sig1", "gsil", "vtokH", "kk", "E1", "bc", "d1", "qmid", "kmid", "klast", "qb",
                "decay", "klastM", "aT0", "aT1", "Sall", "Sf0", "Sf1", "Sfin", "S0t"]

    for k4 in range(4):
        dma("sp", X[:, 4 * k4:4 * k4 + 4, :], xT_d[:, 4 * k4:4 * k4 + 4, :], [], XK[4 * k4:4 * k4 + 4], ("xin", k4))
    small = [(cond_f, cond_d, "cond"), (bmod, bmod_d, "bmod"), (lng, lng_d, "lng"), (lnb, lnb_d, "lnb"),
             (attg, attg_d, "attg"), (hgg, hgg_d, "hgg"), (lbl, lbl_d, "lbl"), (esink, sink_d, "esink"),
             (ctxf, ctxf_d, "ctxf"), (cfl, cfl_d, "cfl"), (rmask, rmask_d, "rmask")]
    for t_, d_, nm in small:
        dma("sp", t_[:], d_, [], [nm], ("c", nm))
    dma("pool", cmats[:], cm_d[0:3].rearrange("m p c -> p m c"), [], ["cmats"], ("c", "cmats"))
    vmemset(ones[:], 1.0, ["ones"])
    vmemset(scanm[:], 1.0, ["scanm"])
    vmemset(scanm[:, :, 0:1], 0.0, ["scanm"])
    act(condS[:], cond_f[:], AF.Silu, ["cond"], ["condS"])
    act(esink[:], esink[:], AF.Exp, ["esink"], ["esink"])
    act(lbl[:], lbl[:], AF.Exp, ["lbl"], ["lbl"])
    tt("dve", lbsum[:], lbl[:, :, 0, :], lbl[:, :, 1, :], ALU.add, ["lbl"], ["lbsum"])
    for l_ in range(2, DEPTH):
        tt("dve", lbsum[:], lbsum[:], lbl[:, :, l_, :], ALU.add, ["lbl", "lbsum"], ["lbsum"])
    S.add("dve", lambda e: e.reciprocal(out=lbsum[:], in_=lbsum[:]), reads=["lbsum"], writes=["lbsum"])
    vmemset(lbv[:, :, 0, :], 0.0, ["lbv"])
    for l_ in range(1, DEPTH):
        tt("dve", lbv[:, :, l_, :], lbl[:, :, l_, :], lbsum[:], ALU.mult, ["lbl", "lbsum", "lbv"], ["lbv"])
        if l_ > 1:
            tt("dve", lbv[:, :, l_, :], lbv[:, :, l_, :], lbv[:, :, l_ - 1, :], ALU.add, ["lbv"], ["lbv"])
    ts("dve", oml[:], lbv[:], -1.0, 1.0, ALU.mult, ALU.add, ["lbv"], ["oml"])

    def r4(ap):
        return ap.rearrange("p (r q) -> p r q", r=4)

    def c32(ap):
        return ap.rearrange("p (c s) -> p c s", s=32)

    def layer(l):
        S.epoch = l
        pm = next_ps()
        for ct in range(96):
            ri = load_w(wmod_d[l][ct])
            mm_group(PS[pm][:, 2 * ct:2 * ct + 2],
                     [(RING[ri][:, k, :], condS[:, k, :]) for k in range(KC)],
                     reads=[("ring", ri), "condS"], writes=[("ps", pm)])
        tt("dve", mods[:], PS[pm][:, 0:192].rearrange("p (c j) -> p c j", j=2),
           bmod[:, l, :].unsqueeze(2).to_broadcast([128, 96, 2]), ALU.add,
           [("ps", pm), "bmod"], ["mods"])
        ts("dve", sc1p[:], mods[:, 16:32, :], 1.0, None, ALU.add, None, ["mods"], ["sc1p"])
        ts("dve", sc2p[:], mods[:, 64:80, :], 1.0, None, ALU.add, None, ["mods"], ["sc2p"])
        sh1 = mods[:, 0:16, :]
        g1 = mods[:, 32:48, :]
        sh2 = mods[:, 48:64, :]
        g2 = mods[:, 80:96, :]

        if _STOP[0] == 1:
            raise _StopBuild()
        for k in range(KC):
            for j, (a, b) in enumerate([(0, 1024), (1024, 1280)]):
                if (k + j) % 2 == 0:
                    ts("dve", XM[:, k, a:b], X[:, k, a:b], sc1p[:, k, j:j + 1], sh1[:, k, j:j + 1], ALU.mult, ALU.add,
                       [("X", k), "sc1p", "mods"], [("XM", k)])
                else:
                    act(XM[:, k, a:b], X[:, k, a:b], AF.Identity, [("X", k), "sc1p", "mods"], [("XM", k)],
                        scale=sc1p[:, k, j:j + 1], bias=sh1[:, k, j:j + 1])
        if _STOP[0] == 2:
            raise _StopBuild()
        for k4 in range(4):
            dma("sp", xsp_d[:, 4 * k4:4 * k4 + 4, :], X[:, 4 * k4:4 * k4 + 4, :], XK[4 * k4:4 * k4 + 4], ["xspill"],
                ("xsp", k4))
        barrier(XK + ATT_TEMPK + HG_TEMPK)
        dma("sp", cosT, cos_d, [], ["cosT"], ("c", "cosT"))
        dma("sp", sinT, sin_d, [], ["sinT"], ("c", "sinT"))
        dma("pool", masks, msk_d, [], ["masks"], ("c", "masks"))
        dma("pool", ckT, ckT_d[l], [], ["ckT"], ("c", "ckT"))
        dma("pool", cvt, cv_d[l], [], ["cvt"], ("c", "cvt"))

        if _STOP[0] == 3:
            raise _StopBuild()
        wt = [0]

        def project(ri, a, n, pbank):
            mm_group(PS[pbank][:, 0:n], [(RING[ri][:, k, :], XM[:, k, a:a + n]) for k in range(KC)],
                     reads=[("ring", ri)] + XMK, writes=[("ps", pbank)])

        def rope_pair(dst_fn, keyname, raw_out=None):
            r0 = load_w(win_d[l][wt[0]])
            r1 = load_w(win_d[l][wt[0] + 1])
            wt[0] += 2
            for ti, (a, n) in enumerate(TT):
                p0 = next_ps()
                p1 = next_ps()
                project(r0, a, n, p0)
                project(r1, a, n, p1)
                tA = tmpA[ti % 2]
                tB = tmpB[ti % 2]
                if raw_out is not None:
                    vcopy(raw_out[:, a:a + n], PS[p0][:, 0:n], [("ps", p0)], ["kraw"])
                tt("dve", tA[:, 0:n], PS[p0][:, 0:n], cosT[:, a:a + n], ALU.mult, [("ps", p0), "cosT"], [("tmpA", ti % 2)])
                tt("dve", tB[:, 0:n], PS[p1][:, 0:n], sinT[:, a:a + n], ALU.mult, [("ps", p1), "sinT"], [("tmpB", ti % 2)])
                tt("dve", dst_fn(a, n), tA[:, 0:n], tB[:, 0:n], ALU.add, [("tmpA", ti % 2), ("tmpB", ti % 2)], [keyname])

        for g in range(2):
            if _STOP[0] == 34 and g == 1:
                raise _StopBuild()
            for r in range(4):
                rope_pair(lambda a, n, r=r: qT[:, r, a:a + n], ("qT", r))
            if _STOP[0] == 30:
                raise _StopBuild()
            rope_pair(lambda a, n: kT[:, a:a + n], "kT", raw_out=kraw)
            if _STOP[0] == 31:
                raise _StopBuild()
            dma("sp", okT_d[l, g], kraw, ["kraw"], [("okT", l, g)], ("okT",))
            if _STOP[0] == 32:
                raise _StopBuild()
            if g == 0:
                rv = [load_w(win_d[l][wt[0]]), load_w(win_d[l][wt[0] + 1])]
                wt[0] += 2
                for B in range(NBLK):
                    pv = next_ps()
                    for vg in range(2):
                        mm_group(PS[pv][:, vg * 128:(vg + 1) * 128],
                                 [(XM[:, k, B * 128:(B + 1) * 128], RING[rv[vg]][:, k, :]) for k in range(KC)],
                                 reads=[("ring", rv[vg])] + XMK, writes=[("ps", pv)])
                    act(vtok[:, B, :], PS[pv][:, 0:256], AF.Identity, [("ps", pv)], [("vtok", B)])
                    vcopy(vraw[B % 2], PS[pv][:, 0:256], [("ps", pv)], ["vraw%d" % (B % 2)])
                    dma("sp", ov_d[l, B], vraw[B % 2], ["vraw%d" % (B % 2)], [("ov", l, B)], ("ov", B % 2))
            if _STOP[0] == 33:
                raise _StopBuild()
            for B in range(NBLK):
                kbs = []
                if B - 1 >= 0:
                    kbs.append(("loc", B - 1, 0))
                kbs.append(("loc", B, None))
                if B + 1 < NBLK:
                    kbs.append(("loc", B + 1, 2))
                for j in range(4):
                    kbs.append(("ctx", j, None))
                po = next_ps()
                pd = next_ps(exclude=(po,))
                qrhs = qT[:, :, B * 128:(B + 1) * 128]
                qkeys = [("qT", r) for r in range(4)]
                nkb = len(kbs)
                sbank = {}

                def issue_s(i):
                    kind, j, mi = kbs[i]
                    pb = next_ps(exclude=(po, pd))
                    sbank[i] = pb
                    if kind == "loc":
                        lhs = kT[:, j * 128:(j + 1) * 128]
                        rd = ["kT"]
                    else:
                        lhs = ckT[:, g, j * 128:(j + 1) * 128]
                        rd = ["ckT"]
                    mm_group(r4(PS[pb][:]), [(lhs, qrhs)], reads=rd + qkeys, writes=[("ps", pb)])

                issue_s(0)
                for i in range(nkb):
                    if i + 1 < nkb:
                        issue_s(i + 1)
                    kind, j, mi = kbs[i]
                    pb = sbank[i]
                    et = eT[i % 4]
                    ek = ("eT", i % 4)
                    act(et, PS[pb][:], AF.Exp, [("ps", pb)], [ek], scale=ATT_SCALE)
                    if kind == "loc" and mi is not None:
                        tt("dve", r4(et), r4(et), masks[:, B, mi, :].unsqueeze(1).to_broadcast([128, 4, 128]),
                           ALU.mult, [ek, "masks"], [ek])
                    if kind == "ctx":
                        ts("dve", et, et, ctxf[:, B:B + 1], None, ALU.mult, None, [ek, "ctxf"], [ek])
                    if kind == "loc":
                        vl = vtok[:, j, g * 128:(g + 1) * 128]
                        vrd = [("vtok", j)]
                    else:
                        vl = cvt[:, j, g * 128:(g + 1) * 128]
                        vrd = ["cvt"]
                    mm1(PS[po][:], vl, et, (i == 0), (i == nkb - 1), vrd + [ek], [("ps", po)])
                    mm1(PS[pd][:], ones[:], et, (i == 0), (i == nkb - 1), ["ones", ek], [("ps", pd)])
                tA = tmpA[B % 2]
                tk = ("tmpA", B % 2)
                tt("dve", r4(tA), r4(PS[pd][:]),
                   esink[:, l, 4 * g:4 * g + 4].unsqueeze(2).to_broadcast([128, 4, 128]), ALU.add,
                   [("ps", pd), "esink"], [tk])
                S.add("dve", lambda e, tA=tA: e.reciprocal(out=tA, in_=tA), reads=[tk], writes=[tk])
                tt("dve", attraw[g][:, :, B * 128:(B + 1) * 128], r4(PS[po][:]), r4(tA), ALU.mult,
                   [("ps", po), tk], ATTK[g])
        if _STOP[0] == 4:
            raise _StopBuild()
        for ti, (a, n) in enumerate(TT):
            pss = next_ps()
            for h in range(8):
                o_ = eT[h % 4]
                act(o_[:, 0:n], attraw[h // 4][:, h % 4, a:a + n], AF.Square, ATTK[h // 4], [("eT", h % 4)])
                mm1(PS[pss][:, 0:n], ones[:], o_[:, 0:n], (h == 0), (h == 7), ["ones", ("eT", h % 4)], [("ps", pss)])
            tA = tmpA[ti % 2]
            tk = ("tmpA", ti % 2)
            act(tA[:, 0:n], PS[pss][:, 0:n], AF.Ln, [("ps", pss)], [tk], scale=1.0 / 1024.0, bias=RMS_EPS)
            act(tA[:, 0:n], tA[:, 0:n], AF.Exp, [tk], [tk], scale=-0.5)
            for h in range(8):
                stt(MIX[:, h, a:a + n], attraw[h // 4][:, h % 4, a:a + n], attg[:, l, h:h + 1], tA[:, 0:n],
                    ALU.mult, ALU.mult, ATTK[h // 4] + ["attg", tk], [("MIX", h)])

        if _STOP[0] == 5:
            raise _StopBuild()
        barrier(ATT_TEMPK + HG_TEMPK)
        ident = cmats[:, 0, :]
        for h in range(8):
            rts = [load_w(win_d[l][wt[0] + i]) for i in range(5)]
            wt[0] += 5
            for ti, (a, n) in enumerate(TT):
                pb = [next_ps() for _ in range(5)]
                for i in range(5):
                    project(rts[i], a, n, pb[i])
                act(qf[:, a:a + n], PS[pb[0]][:, 0:n], AF.Silu, [("ps", pb[0])], ["qf"])
                act(sigf[0][:, a:a + n], PS[pb[1]][:, 0:n], AF.Sigmoid, [("ps", pb[1])], ["sig0"])
                act(sigf[1][:, a:a + n], PS[pb[2]][:, 0:n], AF.Sigmoid, [("ps", pb[2])], ["sig1"])
                act(vTh[:, a:a + n], PS[pb[3]][:, 0:n], AF.Identity, [("ps", pb[3])], ["kmid"])
                act(gsil[:, a:a + n], PS[pb[4]][:, 0:n], AF.Silu, [("ps", pb[4])], ["gsil"])
            for B in range(NBLK):
                pt = next_ps()
                ptv = PS[pt][:].bitcast(BF16)[:, 0:128]
                S.add("pe", lambda e, ptv=ptv, B=B: e.transpose(ptv, vTh[:, B * 128:(B + 1) * 128], ident),
                      reads=["kmid", "cmats"], writes=[("ps", pt)])
                act(vtokH[:, B, :], ptv, AF.Identity, [("ps", pt)], ["vtokH"])
            pos = [next_ps() for _ in range(3)]
            for i in range(3):
                vmemset(PS[pos[i]][:], 0.0, [("ps", pos[i])])

            def o_ap(t0, n):
                return PS[pos[t0 // 512]][:, t0 % 512:t0 % 512 + n]

            for dr in range(2):
                fwd = (dr == 0)
                ts("dve", kk, sigf[dr], oml[:, dr, l, h:h + 1], lbv[:, dr, l, h:h + 1], ALU.mult, ALU.add,
                   ["sig%d" % dr, "oml", "lbv"], ["kk"])
                act(logf, kk, AF.Ln, ["kk"], ["E1"])
                ts("dve", kk, kk, -1.0, 1.0, ALU.mult, ALU.add, ["kk", "E1"], ["kk"])
                S.add("dve", lambda e: e.tensor_tensor_scan(out=bc, data0=scanm[:].rearrange("p c s -> p (c s)"),
                                                            data1=logf, initial=0.0, op0=ALU.mult, op1=ALU.add),
                      reads=["scanm", "E1"], writes=["bc"])
                if not fwd:
                    tt("dve", d1, logf, bc, ALU.subtract, ["E1", "bc"], ["d1"])
                    tt("dve", c32(bc), c32(d1), c32(bc)[:, :, 31:32].to_broadcast([128, NCH, 32]), ALU.add,
                       ["d1", "bc"], ["bc"])
                mid = 15 if fwd else 16
                lastp = 31 if fwd else 0
                bmid = c32(bc)[:, :, mid:mid + 1].to_broadcast([128, NCH, 32])
                blast = c32(bc)[:, :, lastp:lastp + 1].to_broadcast([128, NCH, 32])
                tt("dve", c32(d1), c32(bc), bmid, ALU.subtract, ["bc"], ["d1"])
                act(E1, d1, AF.Exp, ["d1"], ["E1"])
                tt("dve", qmid, qf, E1, ALU.mult, ["qf", "E1"], ["qmid"])
                act(E1, d1, AF.Exp, ["d1", "qmid"], ["E1"], scale=-1.0)
                tt("dve", kmid, kk, E1, ALU.mult, ["kk", "E1"], ["kmid"])
                tt("dve", c32(d1), blast, c32(bc), ALU.subtract, ["bc", "kmid"], ["d1"])
                act(E1, d1, AF.Exp, ["d1"], ["E1"])
                tt("dve", klast, kk, E1, ALU.mult, ["kk", "E1"], ["klast"])
                act(E1, bc, AF.Exp, ["bc", "klast"], ["E1"])
                tt("dve", qb, qf, E1, ALU.mult, ["qf", "E1"], ["qb"])
                act(decay.unsqueeze(2), c32(bc)[:, :, lastp:lastp + 1], AF.Exp, ["bc"], ["decay"])
                for B in range(NBLK):
                    pt = next_ps(exclude=pos)
                    ptv = PS[pt][:].bitcast(BF16)[:, 0:128]
                    S.add("pe", lambda e, ptv=ptv, B=B: e.transpose(ptv, klast[:, B * 128:(B + 1) * 128], ident),
                          reads=["klast", "cmats"], writes=[("ps", pt)])
                    for c4 in range(4):
                        act(klastM[:, 4 * B + c4, :], ptv, AF.Identity, [("ps", pt), "rmask"], ["klastM"],
                            scale=rmask[:, c4:c4 + 1])
                mk = cmats[:, 1 if fwd else 2, :]
                for B in range(NBLK):
                    pa = next_ps(exclude=pos)
                    mm_group(PS[pa][:, 0:128], [(kmid[:, B * 128:(B + 1) * 128], qmid[:, B * 128:(B + 1) * 128])],
                             reads=["kmid", "qmid"], writes=[("ps", pa)])
                    tt("dve", aT[B % 2], PS[pa][:, 0:128], mk, ALU.mult, [("ps", pa), "cmats"], ["aT%d" % (B % 2)])
                    mm1(o_ap(B * 128, 128), vtokH[:, B, :], aT[B % 2], False, False,
                        ["vtokH", "aT%d" % (B % 2)], [("ps", pos[(B * 128) // 512])])
                order = list(range(NCH)) if fwd else list(range(NCH - 1, -1, -1))
                cur = 0
                for c in order:
                    slot = c // 8
                    first_in_slot = (c % 8 == 0) if fwd else (c % 8 == 7)
                    if first_in_slot:
                        nxt = Sf[(cur + 1) % 2]
                        nk = "Sf%d" % ((cur + 1) % 2)
                        if slot == 4:
                            vmemset(nxt, 0.0, [nk])
                        elif (fwd and slot == 0) or ((not fwd) and slot == 3):
                            dma("sp", S0t, s0_d[l, dr, h], [], ["S0t"], ("c", "S0t"))
                            vcopy(nxt, S0t, ["S0t"], [nk])
                        else:
                            ts("dve", nxt, Sf[cur], cfl[:, 0:1], None, ALU.mult, None, ["Sf%d" % cur, "cfl"], [nk])
                        cur = (cur + 1) % 2
                    act(Sall[:, c, :], Sf[cur], AF.Identity, ["Sf%d" % cur], ["Sall"])
                    pu = next_ps(exclude=pos)
                    mm_group(PS[pu][:, 0:128], [(klastM[:, c, :], vtokH[:, c // 4, :])], reads=["klastM", "vtokH"],
                             writes=[("ps", pu)])
                    nxt = Sf[(cur + 1) % 2]
                    nk = "Sf%d" % ((cur + 1) % 2)
                    stt(nxt, Sf[cur], decay[:, c:c + 1], PS[pu][:, 0:128], ALU.mult, ALU.add,
                        ["Sf%d" % cur, "decay", ("ps", pu)], [nk])
                    cur = (cur + 1) % 2
                    last_in_slot = (c % 8 == 7) if fwd else (c % 8 == 0)
                    if last_in_slot:
                        vcopy(Sfin[:, slot, :], Sf[cur], ["Sf%d" % cur], ["Sfin"])
                dma("sp", ost_d[l, dr, h], Sfin, ["Sfin"], [("ost", l, dr, h)], ("ost",))
                for c in range(NCH):
                    mm1(o_ap(c * 32, 32), Sall[:, c, :], qb[:, c * 32:(c + 1) * 32], False, False,
                        ["Sall", "qb"], [("ps", pos[(c * 32) // 512])])
            for ti, (a, n) in enumerate(TT):
                act(osq[:, 0:n], PS[pos[ti]][:, 0:n], AF.Square, [("ps", pos[ti]), "d1"], ["d1"])
                pss = next_ps(exclude=pos)
                mm_group(PS[pss][:, 0:n], [(ones[:], osq[:, 0:n])], reads=["ones", "d1"], writes=[("ps", pss)])
                act(rstdh[:, 0:n], PS[pss][:, 0:n], AF.Ln, [("ps", pss), "d1"], ["d1"], scale=1.0 / 128.0, bias=RMS_EPS)
                act(rstdh[:, 0:n], rstdh[:, 0:n], AF.Exp, ["d1"], ["d1"], scale=-0.5)
                tt("dve", th[:, 0:n], PS[pos[ti]][:, 0:n], rstdh[:, 0:n], ALU.mult, [("ps", pos[ti]), "d1"], ["d1"])
                stt(MIX[:, 8 + h, a:a + n], th[:, 0:n], hgg[:, l:l + 1], gsil[:, a:a + n], ALU.mult, ALU.mult,
                    ["d1", "hgg", "gsil"], [("MIX", 8 + h)])
        assert wt[0] == N_IN_TILES

        if _STOP[0] == 6:
            raise _StopBuild()
        barrier(ATT_TEMPK + HG_TEMPK + XK)
        for k4 in range(4):
            dma("sp", X[:, 4 * k4:4 * k4 + 4, :], xsp_d[:, 4 * k4:4 * k4 + 4, :], ["xspill"], XK[4 * k4:4 * k4 + 4],
                ("xin", k4))
        for k in range(KC):
            act(X[:, k, :], X[:, k, :], AF.Identity, [("X", k)], [("X", k)], scale=ALPHA)

        if _STOP[0] == 7:
            raise _StopBuild()
        def resid_accum(oc, ti, a, n, pbank, gate):
            j = 0 if ti < 2 else 1
            stt(X[:, oc, a:a + n], PS[pbank][:, 0:n], gate[:, oc, j:j + 1], X[:, oc, a:a + n], ALU.mult, ALU.add,
                [("ps", pbank), "mods", ("X", oc)], [("X", oc)])

        for oc in range(KC):
            ri = load_w(wo_d[l][oc])
            for ti, (a, n) in enumerate(TT):
                pb = next_ps()
                mm_group(PS[pb][:, 0:n], [(RING[ri][:, k, :], MIX[:, k, a:a + n]) for k in range(KC)],
                         reads=[("ring", ri)] + MIXK, writes=[("ps", pb)])
                resid_accum(oc, ti, a, n, pb, g1)

        if _STOP[0] == 8:
            raise _StopBuild()
        def layer_norm(which, write_xm):
            for ti, (a, n) in enumerate(TT):
                j = 0 if ti < 2 else 1
                for k in range(KC):
                    act(xb[:, k, 0:n], X[:, k, a:a + n], AF.Identity, [("X", k)], XBK)
                    act(xsq[:, k, 0:n], X[:, k, a:a + n], AF.Square, [("X", k)], XSQK)
                p1 = next_ps()
                p2 = next_ps()
                mm_group(PS[p1][:, 0:n], [(ones[:], xb[:, k, 0:n]) for k in range(KC)], reads=["ones"] + XBK,
                         writes=[("ps", p1)])
                mm_group(PS[p2][:, 0:n], [(ones[:], xsq[:, k, 0:n]) for k in range(KC)], reads=["ones"] + XSQK,
                         writes=[("ps", p2)])
                ts("dve", mean[:, 0:n], PS[p1][:, 0:n], 1.0 / D, None, ALU.mult, None, [("ps", p1)], ["mean"])
                tt("dve", rtmp[:, 0:n], mean[:, 0:n], mean[:, 0:n], ALU.mult, ["mean"], ["rtmp"])
                stt(rstd[:, 0:n], PS[p2][:, 0:n], 1.0 / D, rtmp[:, 0:n], ALU.mult, ALU.subtract,
                    [("ps", p2), "rtmp"], ["rstd"])
                act(rstd[:, 0:n], rstd[:, 0:n], AF.Ln, ["rstd"], ["rstd"], bias=LN_EPS)
                act(rstd[:, 0:n], rstd[:, 0:n], AF.Exp, ["rstd"], ["rstd"], scale=-0.5)
                for k in range(KC):
                    tt("dve", X[:, k, a:a + n], X[:, k, a:a + n], mean[:, 0:n], ALU.subtract, [("X", k), "mean"],
                       [("X", k)])
                    tt("dve", X[:, k, a:a + n], X[:, k, a:a + n], rstd[:, 0:n], ALU.mult, [("X", k), "rstd"],
                       [("X", k)])
                    act(X[:, k, a:a + n], X[:, k, a:a + n], AF.Identity, [("X", k), "lng", "lnb"], [("X", k)],
                        scale=lng[:, l, which, k:k + 1], bias=lnb[:, l, which, k:k + 1])
                    if write_xm:
                        ts("dve", XM[:, k, a:a + n], X[:, k, a:a + n], sc2p[:, k, j:j + 1], sh2[:, k, j:j + 1],
                           ALU.mult, ALU.add, [("X", k), "sc2p", "mods"], [("XM", k)])

        layer_norm(0, True)
        for k in range(KC):
            act(X[:, k, :], X[:, k, :], AF.Identity, [("X", k)], [("X", k)], scale=ALPHA)

        if _STOP[0] == 9:
            raise _StopBuild()
        for grp in range(4):
            for jj in range(16):
                ri = load_w(wup_d[l][grp * 16 + jj])
                for ti, (a, n) in enumerate(TT):
                    pb = next_ps()
                    project(ri, a, n, pb)
                    rl = tmp_relu[ti % 2]
                    act(rl[:, 0:n], PS[pb][:, 0:n], AF.Relu, [("ps", pb)], [("relu", ti % 2)])
                    tt("dve", HID[:, jj, a:a + n], rl[:, 0:n], rl[:, 0:n], ALU.mult, [("relu", ti % 2)], [("MIX", jj)])
            for oc in range(KC):
                ri = load_w(wdn_d[l][grp * 16 + oc])
                for ti, (a, n) in enumerate(TT):
                    pb = next_ps()
                    mm_group(PS[pb][:, 0:n], [(RING[ri][:, k, :], HID[:, k, a:a + n]) for k in range(KC)],
                             reads=[("ring", ri)] + MIXK, writes=[("ps", pb)])
                    resid_accum(oc, ti, a, n, pb, g2)
        layer_norm(1, False)

    try:
        for l in range(L):
            layer(l)
    except _StopBuild:
        pass
    for k4 in range(4):
        dma("sp", yT_d[:, 4 * k4:4 * k4 + 4, :], X[:, 4 * k4:4 * k4 + 4, :], XK[4 * k4:4 * k4 + 4], [("yT", k4)],
            ("yout", k4))

    with nc.Block() as block:
        S.emit(nc, stack, block)
    stack.close()
    return nc


def _core_slots(core):
    if core < 2:
        return [("s", core, 256 * s) for s in range(4)] + [("p", 30 + core, 0)]
    return [("p", (core - 2) * 5 + s, 0) for s in range(5)]


def _fm(vec2d):
    a = np.asarray(vec2d, np.float32)
    n = a.shape[-1] // 128
    a = a.reshape(a.shape[:-1] + (n, 128))
    return np.ascontiguousarray(np.moveaxis(a, -1, 0))


def _retile(w):
    K, N = w.shape
    n = N // 128
    return np.ascontiguousarray(w.reshape(K // 128, 128, n, 128).transpose(2, 1, 0, 3)).reshape(n, 128, (K // 128) * 128)


def _win_cols():
    perm = np.arange(128) ^ 32
    ar = np.arange(128)
    cols = []
    for g in range(2):
        for r in range(4):
            h = 4 * g + r
            cols.append(h * 128 + ar)
            cols.append(h * 128 + perm)
        cols.append(1024 + g * 128 + ar)
        cols.append(1024 + g * 128 + perm)
        if g == 0:
            cols.append(1280 + ar)
            cols.append(1280 + 128 + ar)
    for h in range(8):
        for base in (1536, 2560, 3584, 4608, 5632):
            cols.append(base + h * 128 + ar)
    return np.concatenate(cols)


def _rope_tables(slots):
    cosT = np.ones((128, NTOK), np.float32)
    sinT = np.zeros((128, NTOK), np.float32)
    inv = (np.float32(10000.0) ** (-np.arange(32, dtype=np.float32) / np.float32(32))).astype(np.float32)
    for s, (kind, seq, start) in enumerate(slots):
        if kind != "s":
            continue
        t = start + np.arange(256)
        row = (t // 64).astype(np.float32)
        col = (t % 64).astype(np.float32)
        ang_r = (row[None, :] * inv[:, None]).astype(np.float32)
        ang_c = (col[None, :] * inv[:, None]).astype(np.float32)
        sl = slice(256 * s, 256 * (s + 1))
        cr, sr, cc, sc_ = np.cos(ang_r), np.sin(ang_r), np.cos(ang_c), np.sin(ang_c)
        cosT[0:32, sl] = cr
        cosT[32:64, sl] = cr
        cosT[64:96, sl] = cc
        cosT[96:128, sl] = cc
        sinT[0:32, sl] = -sr
        sinT[32:64, sl] = sr
        sinT[64:96, sl] = -sc_
        sinT[96:128, sl] = sc_
    return cosT, sinT


def _masks(slots):
    m = np.zeros((128, NBLK, 3, 128), np.float32)
    ctxflag = np.zeros((128, NBLK), np.float32)
    ar = np.arange(128)
    for B in range(NBLK):
        kind, seq, start = slots[B // 2]
        qpos = start + (B % 2) * 128 + ar
        if kind == "s":
            ctxflag[:, B] = 1.0
        for nb, KB in ((0, B - 1), (1, B), (2, B + 1)):
            if KB < 0 or KB >= NBLK:
                continue
            kkind, kseq, kstart = slots[KB // 2]
            if (kkind, kseq) != (kind, seq):
                continue
            kpos = kstart + (KB % 2) * 128 + ar
            if kind == "s":
                ok = np.abs(kpos[:, None] - qpos[None, :]) <= 128
            else:
                ok = np.ones((128, 128), bool)
            m[:, B, nb, :] = ok.astype(np.float32)
    cfl = np.full((128, 1), 1.0 if slots[0][0] == "s" else 0.0, np.float32)
    return m, ctxflag, cfl


def _const_mats():
    cm = np.zeros((4, 128, 128), np.float32)
    ar = np.arange(128)
    cm[0] = np.eye(128, dtype=np.float32)
    same = (ar[:, None] // 32) == (ar[None, :] // 32)
    cm[1] = (same & (ar[:, None] <= ar[None, :])).astype(np.float32)
    cm[2] = (same & (ar[:, None] >= ar[None, :])).astype(np.float32)
    rowmask = np.zeros((128, 4), np.float32)
    for c in range(4):
        rowmask[32 * c:32 * (c + 1), c] = 1.0
    return cm, rowmask


def _prepare(inp, L, cores):
    f = lambda k: np.asarray(inp[k], np.float32)
    w_in = f("w_in")
    cols = _win_cols()
    shared = {}
    for l in range(L):
        shared[f"wmod{l}"] = _retile(f("w_mod")[l])
        shared[f"win{l}"] = _retile(w_in[l][:, cols])
        shared[f"wo{l}"] = _retile(f("w_o")[l])
        shared[f"wup{l}"] = _retile(f("w_up")[l])
        shared[f"wdn{l}"] = np.ascontiguousarray(
            f("w_down")[l].reshape(4, 16, 128, 16, 128).transpose(0, 3, 2, 1, 4)).reshape(64, 128, 2048)
    shared.update({
        "bmod": _fm(f("b_mod")[:L]),
        "lng": _fm(f("ln_g")[:L]),
        "lnb": _fm(f("ln_b")[:L]),
        "attg": _fm(f("attn_norm_g")[:L]),
        "hgg": np.ascontiguousarray(f("hg_norm_g")[:L].T),
        "lbl": _fm(f("hg_lb_logits")),
        "sink": np.ascontiguousarray(np.broadcast_to(f("attn_sink")[:L][None], (128, L, 8))),
    })
    cm, rowmask = _const_mats()
    shared["cmats"] = cm
    shared["rowmask"] = rowmask
    xp, xs = f("x_prompt"), f("x_sample")
    maps = []
    for core in cores:
        slots = _core_slots(core)
        xs_ = []
        for kind, seq, start in slots:
            xs_.append(xs[seq, start:start + 256] if kind == "s" else xp[seq])
        xc = np.concatenate(xs_, 0)
        d = dict(shared)
        d["xT"] = np.ascontiguousarray(xc.reshape(NTOK, KC, 128).transpose(2, 1, 0))
        condA = f("c")[core] if core < 2 else f("c_ctx")
        condB = f("c_ctx")
        d["cond"] = _fm(np.stack([condA, condB]))
        d["cond"] = np.ascontiguousarray(d["cond"].transpose(0, 2, 1))
        cosT, sinT = _rope_tables(slots)
        d["cosT"], d["sinT"] = cosT, sinT
        m, ctxflag, cfl = _masks(slots)
        d["masks"], d["ctxflag"], d["cflag"] = m, ctxflag, cfl
        if core < 2:
            ck = f("cache_k")[core][:L]
            cv = f("cache_v")[core][:L]
            d["ckT"] = np.ascontiguousarray(ck.transpose(0, 3, 2, 1))
            d["cv"] = np.ascontiguousarray(cv.reshape(L, 4, 128, 256).transpose(0, 2, 1, 3))
            d["s0"] = np.ascontiguousarray(np.stack([f("state_hgrn_fwd")[core][:L], f("state_hgrn_bwd")[core][:L]], 1))
        else:
            d["ckT"] = np.zeros((L, 128, 2, 512), np.float32)
            d["cv"] = np.zeros((L, 128, 4, 256), np.float32)
            d["s0"] = np.zeros((L, 2, 8, 128, 128), np.float32)
        maps.append(d)
    return maps


_PROG = {}


def _run(inp, L=DEPTH, cores=tuple(range(8)), trace=False):
    if L not in _PROG:
        _PROG[L] = build_program(L)
    nc = _PROG[L]
    maps = _prepare(inp, L, cores)
    res = run_bass_kernel_spmd(nc, maps, core_ids=list(range(len(cores))), **({"trace": True} if trace else {}))
    return res


def _gather(results, L, cores):
    BATCH, SEQ, DEC_BATCH, DEC_SEQ = 32, 256, 2, 1024
    y_p = np.zeros((BATCH, SEQ, D), np.float32)
    y_s = np.zeros((DEC_BATCH, DEC_SEQ, D), np.float32)
    nk = np.zeros((BATCH, L, SEQ, 2, 128), np.float32)
    nv = np.zeros((BATCH, L, SEQ, 2, 128), np.float32)
    sf = np.zeros((BATCH, L, 8, 128, 128), np.float32)
    sbw = np.zeros((BATCH, L, 8, 128, 128), np.float32)
    for ci, core in enumerate(cores):
        r = results[ci]
        y = np.asarray(r["yT"]).transpose(2, 1, 0).reshape(NTOK, D)
        okT = np.asarray(r["okT"])
        ov = np.asarray(r["ov"])
        ost = np.asarray(r["ost"])
        for s, (kind, seq, start) in enumerate(_core_slots(core)):
            ys = y[256 * s:256 * (s + 1)]
            if kind == "s":
                y_s[seq, start:start + 256] = ys
            else:
                y_p[seq] = ys
                nk[seq] = okT[:, :, :, 256 * s:256 * (s + 1)].transpose(0, 3, 1, 2)
                nv[seq] = ov[:, 2 * s:2 * s + 2].reshape(L, 256, 2, 128)
                sf[seq] = ost[:, 0, :, :, s, :]
                sbw[seq] = ost[:, 1, :, :, s, :]
    return y_p, y_s, nk, nv, sf, sbw


def kernel(**inputs):
    res = _run(inputs, DEPTH, tuple(range(8)))
    return _gather(res.results, DEPTH, tuple(range(8)))
```

```python
from contextlib import ExitStack
import numpy as np
import concourse.bass as bass
import concourse.mybir as mybir
from concourse.bass_utils import run_bass_kernel_spmd

F32 = mybir.dt.float32
BF16 = mybir.dt.bfloat16
AF = mybir.ActivationFunctionType
ALU = mybir.AluOpType

D = 2048
KC = 16
NTOK = 1280
NBLK = 10
NCH = 40
NSLOT = 5
DEPTH = 4
ALPHA = (2 * DEPTH) ** 0.25
ATT_SCALE = 128 ** -0.5
LN_EPS = 1e-5
RMS_EPS = 1e-6
TT = [(0, 512), (512, 512), (1024, 256)]
NRING = 6
N_IN_TILES = 62
_STOP = [None]


class _StopBuild(Exception):
    pass


class _Op:
    __slots__ = ("eng", "fn", "deps", "sig", "sem", "tick", "dma", "ndma", "epoch", "waits", "vc")


class Sched:
    def __init__(self):
        self.ops = []
        self.lastw = {}
        self.readers = {}
        self.epoch = 0

    def add(self, eng, fn, reads=(), writes=(), dma=None, ndma=1):
        op = _Op()
        op.eng, op.fn, op.dma, op.ndma, op.sig = eng, fn, dma, ndma, False
        op.epoch = self.epoch
        deps = {}
        psr = [r for r in reads if isinstance(r, tuple) and r and r[0] == "ps"]
        if psr:
            reads = [r for r in reads if r not in psr]
            writes = list(writes) + [r for r in psr if r not in writes]

        def consider(d, kind):
            if d is None or d is op:
                return
            if d.dma is None and op.dma is None and d.eng == eng:
                if eng == "pe":
                    return
            deps[id(d)] = d

        for r in reads:
            consider(self.lastw.get(r), "raw")
        for w in writes:
            consider(self.lastw.get(w), "waw")
            for rd in self.readers.get(w, ()):
                consider(rd, "war")
        op.deps = list(deps.values())
        for d in op.deps:
            d.sig = True
        for r in reads:
            self.readers.setdefault(r, []).append(op)
        for w in writes:
            self.lastw[w] = op
            self.readers[w] = []
        self.ops.append(op)
        return op

    def emit(self, nc, stack, block):
        sems = {}

        def get_sem(key):
            if key not in sems:
                sems[key] = stack.enter_context(nc.semaphore("s_" + "_".join(str(k) for k in key)))
            return sems[key]

        counts = {}
        for op in self.ops:
            if op.dma is not None:
                key = ("d", op.dma)
                counts[key] = counts.get(key, 0) + 16 * op.ndma
            else:
                key = ("e", op.eng, op.epoch)
                if op.sig:
                    counts[key] = counts.get(key, 0) + 1
            op.sem = key
            op.tick = counts.get(key, 0)
        known = {}
        for op in self.ops:
            k = known.setdefault(op.eng, {})
            need = []
            for d in sorted(op.deps, key=lambda d_: -d_.tick):
                if k.get(d.sem, 0) < d.tick:
                    need.append((d.sem, d.tick))
                    k[d.sem] = d.tick
                    for s_, t_ in d.vc.items():
                        if k.get(s_, 0) < t_:
                            k[s_] = t_
            op.waits = need
            op.vc = dict(k) if (op.sig or op.dma is not None) else None
        by_eng = {}
        for op in self.ops:
            by_eng.setdefault(op.eng, []).append(op)
        for op in self.ops:
            if op.sig or op.dma is not None:
                get_sem(op.sem)

        def run(engname, e):
            for op in by_eng.get(engname, []):
                waits = op.waits
                for sem_, tick_ in waits[1:]:
                    e.wait_ge(get_sem(sem_), tick_)
                ins = op.fn(e)
                if isinstance(ins, tuple):
                    first, last = ins
                elif isinstance(ins, list):
                    first, last = ins[0], ins
                else:
                    first = last = ins
                if waits:
                    first.wait_op(get_sem(waits[0][0]), waits[0][1], "sem-ge")
                if op.dma is not None:
                    lst = last if isinstance(last, list) else [last]
                    assert len(lst) == op.ndma, (len(lst), op.ndma)
                    for i_ in lst:
                        i_.then_inc(get_sem(op.sem), 16)
                elif op.sig:
                    last.then_inc(get_sem(op.sem), 1)
            done = {}
            for op in by_eng.get(engname, []):
                if op.dma is not None:
                    done[op.sem] = max(done.get(op.sem, 0), op.tick)
            for sem_, tick_ in done.items():
                e.wait_ge(get_sem(sem_), tick_)

        @block.tensor
        def _(e):
            run("pe", e)

        @block.scalar
        def _(e):
            run("act", e)

        @block.vector
        def _(e):
            run("dve", e)

        @block.gpsimd
        def _(e):
            run("pool", e)

        @block.sync
        def _(e):
            run("sp", e)


def build_program(n_layers=DEPTH):
    L = n_layers
    nc = bass.Bass("TRN2", target_bir_lowering=False)

    def din(name, shape):
        return nc.dram_tensor(name, list(shape), F32, kind="ExternalInput").ap()

    def dout(name, shape):
        return nc.dram_tensor(name, list(shape), F32, kind="ExternalOutput").ap()

    xT_d = din("xT", [128, KC, NTOK])
    cond_d = din("cond", [128, KC, 2])
    bmod_d = din("bmod", [128, L, 96])
    lng_d = din("lng", [128, L, 2, KC])
    lnb_d = din("lnb", [128, L, 2, KC])
    attg_d = din("attg", [128, L, 8])
    hgg_d = din("hgg", [128, L])
    lbl_d = din("lbl", [128, 2, DEPTH, 8])
    sink_d = din("sink", [128, L, 8])
    cos_d = din("cosT", [128, NTOK])
    sin_d = din("sinT", [128, NTOK])
    msk_d = din("masks", [128, NBLK, 3, 128])
    ctxf_d = din("ctxflag", [128, NBLK])
    cfl_d = din("cflag", [128, 1])
    ckT_d = din("ckT", [L, 128, 2, 512])
    cv_d = din("cv", [L, 128, 4, 256])
    s0_d = din("s0", [L, 2, 8, 128, 128])
    cm_d = din("cmats", [4, 128, 128])
    rmask_d = din("rowmask", [128, 4])
    wmod_d = [din(f"wmod{l}", [96, 128, 2048]) for l in range(L)]
    win_d = [din(f"win{l}", [N_IN_TILES, 128, 2048]) for l in range(L)]
    wo_d = [din(f"wo{l}", [16, 128, 2048]) for l in range(L)]
    wup_d = [din(f"wup{l}", [64, 128, 2048]) for l in range(L)]
    wdn_d = [din(f"wdn{l}", [64, 128, 2048]) for l in range(L)]

    yT_d = dout("yT", [128, KC, NTOK])
    okT_d = dout("okT", [L, 2, 128, NTOK])
    ov_d = dout("ov", [L, NBLK, 128, 256])
    ost_d = dout("ost", [L, 2, 8, 128, NSLOT, 128])
    xsp_d = nc.dram_tensor("xspill", [128, KC, NTOK], F32, kind="Internal").ap()

    S = Sched()
    stack = ExitStack()

    def sb(name, shape, dt):
        return stack.enter_context(nc.sbuf_tensor(name, list(shape), dt))

    RA = sb("RA", [128, KC * NTOK], F32)
    XM = sb("XM", [128, KC, NTOK], BF16)
    RC = sb("RC", [128, KC * NTOK], BF16)
    RING = [sb(f"ring{i}", [128, KC, 128], BF16) for i in range(NRING)]
    X = RA[:].rearrange("p (k t) -> p k t", k=KC)
    MIX = RC[:].rearrange("p (k t) -> p k t", k=KC)
    HID = MIX

    cond_f = sb("cond_f", [128, KC, 2], F32)
    condS = sb("condS", [128, KC, 2], BF16)
    bmod = sb("bmod_s", [128, L, 96], F32)
    lng = sb("lng_s", [128, L, 2, KC], F32)
    lnb = sb("lnb_s", [128, L, 2, KC], F32)
    attg = sb("attg_s", [128, L, 8], F32)
    hgg = sb("hgg_s", [128, L], F32)
    lbl = sb("lbl_s", [128, 2, DEPTH, 8], F32)
    lbv = sb("lbv", [128, 2, DEPTH, 8], F32)
    oml = sb("oml", [128, 2, DEPTH, 8], F32)
    lbsum = sb("lbsum", [128, 2, 8], F32)
    esink = sb("esink", [128, L, 8], F32)
    ctxf = sb("ctxf", [128, NBLK], F32)
    cfl = sb("cfl", [128, 1], F32)
    rmask = sb("rmask", [128, 4], F32)
    cmats = sb("cmats_s", [128, 3, 128], BF16)
    ones = sb("ones", [128, 128], BF16)
    rmaskF = sb("rmaskF", [128, 4, 128], BF16)
    scanm = sb("scanm", [128, NCH, 32], BF16)
    mods2 = [sb("mods0", [128, 96, 2], F32), sb("mods1", [128, 96, 2], F32)]
    sc1p2 = [sb("sc1p0", [128, KC, 2], F32), sb("sc1p1", [128, KC, 2], F32)]
    sc2p2 = [sb("sc2p0", [128, KC, 2], F32), sb("sc2p1", [128, KC, 2], F32)]
    dummy = sb("dummy_t", [128, 8], F32)

    PS = [stack.enter_context(nc.psum_tensor(f"ps{i}", [128, 512], F32)) for i in range(8)]

    ra_off = [0]

    def ra_f32(nwords):
        o = ra_off[0]
        ra_off[0] += nwords
        assert ra_off[0] <= KC * NTOK, ra_off[0]
        return RA[:, o:o + nwords]

    def ra_bf16(nel):
        assert nel % 2 == 0
        return ra_f32(nel // 2).bitcast(BF16)

    cosT = ra_f32(NTOK)
    sinT = ra_f32(NTOK)
    masks = ra_bf16(NBLK * 3 * 128).rearrange("p (b n q) -> p b n q", b=NBLK, n=3)
    ckT = ra_bf16(2 * 512).rearrange("p (g t) -> p g t", g=2)
    cvt = ra_bf16(4 * 256).rearrange("p (b c) -> p b c", b=4)
    qT = ra_bf16(4 * NTOK).rearrange("p (r t) -> p r t", r=4)
    kT = ra_bf16(NTOK)
    vtok = ra_bf16(NBLK * 256).rearrange("p (b c) -> p b c", b=NBLK)
    kraw = ra_f32(NTOK)
    vraw = [ra_f32(256), ra_f32(256)]
    tmpA = [ra_f32(512), ra_f32(512)]
    tmpB = [ra_f32(512), ra_f32(512)]
    eT = [ra_bf16(512) for _ in range(4)]
    attraw1 = ra_f32(4 * NTOK).rearrange("p (r t) -> p r t", r=4)
    attraw0 = RC[:, 8 * NTOK:16 * NTOK].bitcast(F32).rearrange("p (r t) -> p r t", r=4)
    attraw = [attraw0, attraw1]
    ATTK = [[("MIX", k) for k in range(8, 16)], ["attraw1"]]
    ra_off[0] = 0
    qf = ra_f32(NTOK)
    sigf = [ra_f32(NTOK), ra_f32(NTOK)]
    gsil = ra_f32(NTOK)
    vtokH = ra_bf16(NBLK * 128).rearrange("p (b c) -> p b c", b=NBLK)
    kk = ra_f32(NTOK)
    E1 = ra_f32(NTOK)
    logf = E1
    bc = ra_f32(NTOK)
    d1 = ra_f32(NTOK)
    qmid = ra_bf16(NTOK)
    kmid = ra_bf16(NTOK)
    vTh = kmid
    klast = ra_bf16(NTOK)
    qb = ra_bf16(NTOK)
    decay = ra_f32(NCH)
    klastM = ra_bf16(NCH * 128).rearrange("p (c k) -> p c k", c=NCH)
    aT = [ra_bf16(128), ra_bf16(128)]
    Sall = ra_bf16(NCH * 128).rearrange("p (c v) -> p c v", c=NCH)
    Sf = [ra_f32(128), ra_f32(128)]
    Sfin = ra_f32(NSLOT * 128).rearrange("p (s v) -> p s v", s=NSLOT)
    S0t = ra_f32(128)
    osq = d1[:, 0:256].bitcast(BF16)
    rstdh = d1[:, 256:768]
    th = d1[:, 768:1280]
    xb = RC[:, 0:KC * 512].rearrange("p (k t) -> p k t", k=KC)
    xsq = RC[:, 7 * NTOK:7 * NTOK + KC * 512].rearrange("p (k t) -> p k t", k=KC)
    XBK = [("MIX", k) for k in range(0, 7)]
    XSQK = [("MIX", k) for k in range(7, 14)]
    mean = sb("ln_mean", [128, 512], F32)
    rstd = sb("ln_rstd", [128, 512], F32)
    rtmp = sb("ln_tmp", [128, 512], F32)
    tmp_relu = [sb("relu0", [128, 512], F32), sb("relu1", [128, 512], F32)]

    ps_rr = [0]

    def next_ps(exclude=()):
        while True:
            i = ps_rr[0] % 8
            ps_rr[0] += 1
            if i not in exclude:
                return i

    ring_rr = [0]

    def load_w(src_ap):
        i = ring_rr[0] % NRING
        ring_rr[0] += 1
        dst = RING[i]
        S.add("pool", lambda e, dst=dst, src=src_ap: e.dma_start(
            out=dst[:].rearrange("p k c -> p (k c)"), in_=src),
            writes=[("ring", i)], dma=("ring", i))
        return i

    def mm_group(out_ap, pairs, reads, writes):
        def fn(e, out_ap=out_ap, pairs=pairs):
            ins = first = None
            n = len(pairs)
            for j, (l_, r_) in enumerate(pairs):
                ins = e.matmul(out_ap, l_, r_, start=(j == 0), stop=(j == n - 1))
                if first is None:
                    first = ins
            return (first, ins)
        return S.add("pe", fn, reads=reads, writes=writes)

    def mm1(out_ap, l_, r_, start, stop, reads, writes):
        return S.add("pe", lambda e: e.matmul(out_ap, l_, r_, start=start, stop=stop, skip_group_check=True),
                     reads=reads, writes=writes)

    def act(out, in_, func, reads, writes, scale=None, bias=None):
        kw = {}
        if scale is not None:
            kw["scale"] = scale
        if bias is not None:
            kw["bias"] = bias
        return S.add("act", lambda e: e.activation(out=out, in_=in_, func=func, **kw), reads=reads, writes=writes)

    def tt(eng, out, in0, in1, op, reads, writes):
        return S.add(eng, lambda e: e.tensor_tensor(out=out, in0=in0, in1=in1, op=op), reads=reads, writes=writes)

    def ts(eng, out, in0, s1, s2, op0, op1, reads, writes):
        if op1 is None:
            return S.add(eng, lambda e: e.tensor_scalar(out=out, in0=in0, scalar1=s1, scalar2=None, op0=op0),
                         reads=reads, writes=writes)
        return S.add(eng, lambda e: e.tensor_scalar(out=out, in0=in0, scalar1=s1, scalar2=s2, op0=op0, op1=op1),
                     reads=reads, writes=writes)

    def stt(out, in0, scalar, in1, op0, op1, reads, writes):
        return S.add("dve", lambda e: e.scalar_tensor_tensor(out=out, in0=in0, scalar=scalar, in1=in1,
                                                             op0=op0, op1=op1), reads=reads, writes=writes)

    def vcopy(out, in_, reads, writes):
        return S.add("dve", lambda e: e.tensor_copy(out=out, in_=in_), reads=reads, writes=writes)

    def vmemset(ap, val, writes):
        return S.add("dve", lambda e: e.memset(ap, val), writes=writes)

    def dma(q, out, in_, reads, writes, key):
        return S.add(q, lambda e: e.dma_start(out=out, in_=in_), reads=reads, writes=writes, dma=key)

    def barrier(keys):
        return S.add("dve", lambda e: e.memset(dummy[:, 0:1], 0.0), writes=list(keys))

    XK = [("X", k) for k in range(KC)]
    XMK = [("XM", k) for k in range(KC)]
    MIXK = [("MIX", k) for k in range(KC)]
    ATT_TEMPK = (["cosT", "sinT", "masks", "ckT", "cvt", "kT", "kraw", "vraw0", "vraw1", "attraw1"] +
                 [("qT", r) for r in range(4)] + [("vtok", B) for B in range(NBLK)] +
                 [("tmpA", 0), ("tmpA", 1), ("tmpB", 0), ("tmpB", 1)] + [("eT", i) for i in range(4)])
    HG_TEMPK = ["qf", "sig0", "sig1", "gsil", "vtokH", "kk", "E1", "bc", "d1", "qmid", "kmid", "klast", "qb",
                "decay", "klastM", "aT0", "aT1", "Sall", "Sf0", "Sf1", "Sfin", "S0t"]

    for k4 in range(4):
        dma("sp", X[:, 4 * k4:4 * k4 + 4, :], xT_d[:, 4 * k4:4 * k4 + 4, :], [], XK[4 * k4:4 * k4 + 4], ("xin", k4))
    small = [(cond_f, cond_d, "cond"), (bmod, bmod_d, "bmod"), (lng, lng_d, "lng"), (lnb, lnb_d, "lnb"),
             (attg, attg_d, "attg"), (hgg, hgg_d, "hgg"), (lbl, lbl_d, "lbl"), (esink, sink_d, "esink"),
             (ctxf, ctxf_d, "ctxf"), (cfl, cfl_d, "cfl"), (rmask, rmask_d, "rmask")]
    for t_, d_, nm in small:
        dma("sp", t_[:], d_, [], [nm], ("c", nm))
    dma("pool", cmats[:], cm_d[0:3].rearrange("m p c -> p m c"), [], ["cmats"], ("c", "cmats"))
    vmemset(ones[:], 1.0, ["ones"])
    vcopy(rmaskF[:], rmask[:].unsqueeze(2).to_broadcast([128, 4, 128]), ["rmask"], ["rmaskF"])
    vmemset(scanm[:], 1.0, ["scanm"])
    vmemset(scanm[:, :, 0:1], 0.0, ["scanm"])
    act(condS[:], cond_f[:], AF.Silu, ["cond"], ["condS"])
    act(esink[:], esink[:], AF.Exp, ["esink"], ["esink"])
    act(lbl[:], lbl[:], AF.Exp, ["lbl"], ["lbl"])
    tt("dve", lbsum[:], lbl[:, :, 0, :], lbl[:, :, 1, :], ALU.add, ["lbl"], ["lbsum"])
    for l_ in range(2, DEPTH):
        tt("dve", lbsum[:], lbsum[:], lbl[:, :, l_, :], ALU.add, ["lbl", "lbsum"], ["lbsum"])
    S.add("dve", lambda e: e.reciprocal(out=lbsum[:], in_=lbsum[:]), reads=["lbsum"], writes=["lbsum"])
    vmemset(lbv[:, :, 0, :], 0.0, ["lbv"])
    for l_ in range(1, DEPTH):
        tt("dve", lbv[:, :, l_, :], lbl[:, :, l_, :], lbsum[:], ALU.mult, ["lbl", "lbsum", "lbv"], ["lbv"])
        if l_ > 1:
            tt("dve", lbv[:, :, l_, :], lbv[:, :, l_, :], lbv[:, :, l_ - 1, :], ALU.add, ["lbv"], ["lbv"])
    ts("dve", oml[:], lbv[:], -1.0, 1.0, ALU.mult, ALU.add, ["lbv"], ["oml"])

    def r4(ap):
        return ap.rearrange("p (r q) -> p r q", r=4)

    def c32(ap):
        return ap.rearrange("p (c s) -> p c s", s=32)

    def mods_tile(l, ct, pm):
        ri = load_w(wmod_d[l][ct])
        mm_group(PS[pm][:, 2 * ct:2 * ct + 2],
                 [(RING[ri][:, k, :], condS[:, k, :]) for k in range(KC)],
                 reads=[("ring", ri), "condS"], writes=[("ps", pm)])

    def mods_finish(l, pm):
        p_ = l % 2
        tt("dve", mods2[p_][:], PS[pm][:, 0:192].rearrange("p (c j) -> p c j", j=2),
           bmod[:, l, :].unsqueeze(2).to_broadcast([128, 96, 2]), ALU.add,
           [("ps", pm), "bmod"], [("mods", p_)])
        ts("dve", sc1p2[p_][:], mods2[p_][:, 16:32, :], 1.0, None, ALU.add, None, [("mods", p_)], [("sc1p", p_)])
        ts("dve", sc2p2[p_][:], mods2[p_][:, 64:80, :], 1.0, None, ALU.add, None, [("mods", p_)], [("sc2p", p_)])

    def layer(l):
        S.epoch = l
        if l == 0:
            pm = next_ps()
            for ct in range(96):
                mods_tile(0, ct, pm)
            mods_finish(0, pm)
        mods, sc1p, sc2p = mods2[l % 2], sc1p2[l % 2], sc2p2[l % 2]
        MK, S1K, S2K = ("mods", l % 2), ("sc1p", l % 2), ("sc2p", l % 2)
        sh1 = mods[:, 0:16, :]
        g1 = mods[:, 32:48, :]
        sh2 = mods[:, 48:64, :]
        g2 = mods[:, 80:96, :]

        if _STOP[0] == 1:
            raise _StopBuild()
        for k in range(KC):
            for j, (a, b) in enumerate([(0, 1024), (1024, 1280)]):
                if (k + j) % 2 == 0:
                    ts("dve", XM[:, k, a:b], X[:, k, a:b], sc1p[:, k, j:j + 1], sh1[:, k, j:j + 1], ALU.mult, ALU.add,
                       [("X", k), S1K, MK], [("XM", k)])
                else:
                    act(XM[:, k, a:b], X[:, k, a:b], AF.Identity, [("X", k), S1K, MK], [("XM", k)],
                        scale=sc1p[:, k, j:j + 1], bias=sh1[:, k, j:j + 1])
        if _STOP[0] == 2:
            raise _StopBuild()
        for k4 in range(4):
            dma("sp", xsp_d[:, 4 * k4:4 * k4 + 4, :], X[:, 4 * k4:4 * k4 + 4, :], XK[4 * k4:4 * k4 + 4], ["xspill"],
                ("xsp", k4))
        barrier(XK + ATT_TEMPK + HG_TEMPK)
        dma("sp", cosT, cos_d, [], ["cosT"], ("c", "cosT"))
        dma("sp", sinT, sin_d, [], ["sinT"], ("c", "sinT"))
        dma("pool", masks, msk_d, [], ["masks"], ("c", "masks"))
        dma("pool", ckT, ckT_d[l], [], ["ckT"], ("c", "ckT"))
        dma("pool", cvt, cv_d[l], [], ["cvt"], ("c", "cvt"))

        if _STOP[0] == 3:
            raise _StopBuild()
        wt = [0]

        def project(ri, a, n, pbank):
            mm_group(PS[pbank][:, 0:n], [(RING[ri][:, k, :], XM[:, k, a:a + n]) for k in range(KC)],
                     reads=[("ring", ri)] + XMK, writes=[("ps", pbank)])

        def rope_pair(dst_fn, keyname, raw_out=None):
            r0 = load_w(win_d[l][wt[0]])
            r1 = load_w(win_d[l][wt[0] + 1])
            wt[0] += 2
            for ti, (a, n) in enumerate(TT):
                p0 = next_ps()
                p1 = next_ps()
                project(r0, a, n, p0)
                project(r1, a, n, p1)
                tA = tmpA[ti % 2]
                tB = tmpB[ti % 2]
                if raw_out is not None:
                    vcopy(raw_out[:, a:a + n], PS[p0][:, 0:n], [("ps", p0)], ["kraw"])
                tt("dve", tA[:, 0:n], PS[p0][:, 0:n], cosT[:, a:a + n], ALU.mult, [("ps", p0), "cosT"], [("tmpA", ti % 2)])
                tt("dve", tB[:, 0:n], PS[p1][:, 0:n], sinT[:, a:a + n], ALU.mult, [("ps", p1), "sinT"], [("tmpB", ti % 2)])
                tt("pool", dst_fn(a, n), tA[:, 0:n], tB[:, 0:n], ALU.add, [("tmpA", ti % 2), ("tmpB", ti % 2)], [keyname])

        for g in range(2):
            if _STOP[0] == 34 and g == 1:
                raise _StopBuild()
            for r in range(4):
                rope_pair(lambda a, n, r=r: qT[:, r, a:a + n], ("qT", r))
            if _STOP[0] == 30:
                raise _StopBuild()
            rope_pair(lambda a, n: kT[:, a:a + n], "kT", raw_out=kraw)
            if _STOP[0] == 31:
                raise _StopBuild()
            dma("sp", okT_d[l, g], kraw, ["kraw"], [("okT", l, g)], ("okT",))
            if _STOP[0] == 32:
                raise _StopBuild()
            if g == 0:
                rv = [load_w(win_d[l][wt[0]]), load_w(win_d[l][wt[0] + 1])]
                wt[0] += 2
                for B in range(NBLK):
                    pv = next_ps()
                    for vg in range(2):
                        mm_group(PS[pv][:, vg * 128:(vg + 1) * 128],
                                 [(XM[:, k, B * 128:(B + 1) * 128], RING[rv[vg]][:, k, :]) for k in range(KC)],
                                 reads=[("ring", rv[vg])] + XMK, writes=[("ps", pv)])
                    act(vtok[:, B, :], PS[pv][:, 0:256], AF.Identity, [("ps", pv)], [("vtok", B)])
                    vcopy(vraw[B % 2], PS[pv][:, 0:256], [("ps", pv)], ["vraw%d" % (B % 2)])
                    dma("sp", ov_d[l, B], vraw[B % 2], ["vraw%d" % (B % 2)], [("ov", l, B)], ("ov", B % 2))
            if _STOP[0] == 33:
                raise _StopBuild()
            for B in range(NBLK):
                kbs = []
                if B - 1 >= 0:
                    kbs.append(("loc", B - 1, 0))
                kbs.append(("loc", B, None))
                if B + 1 < NBLK:
                    kbs.append(("loc", B + 1, 2))
                for j in range(4):
                    kbs.append(("ctx", j, None))
                po = next_ps()
                pd = next_ps(exclude=(po,))
                qrhs = qT[:, :, B * 128:(B + 1) * 128]
                qkeys = [("qT", r) for r in range(4)]
                nkb = len(kbs)
                sbank = {}

                def issue_s(i):
                    kind, j, mi = kbs[i]
                    pb = next_ps(exclude=(po, pd))
                    sbank[i] = pb
                    if kind == "loc":
                        lhs = kT[:, j * 128:(j + 1) * 128]
                        rd = ["kT"]
                    else:
                        lhs = ckT[:, g, j * 128:(j + 1) * 128]
                        rd = ["ckT"]
                    mm_group(r4(PS[pb][:]), [(lhs, qrhs)], reads=rd + qkeys, writes=[("ps", pb)])

                issue_s(0)
                for i in range(nkb):
                    if i + 1 < nkb:
                        issue_s(i + 1)
                    kind, j, mi = kbs[i]
                    pb = sbank[i]
                    et = eT[i % 4]
                    ek = ("eT", i % 4)
                    act(et, PS[pb][:], AF.Exp, [("ps", pb)], [ek], scale=ATT_SCALE)
                    if kind == "loc" and mi is not None:
                        tt("dve", r4(et), r4(et), masks[:, B, mi, :].unsqueeze(1).to_broadcast([128, 4, 128]),
                           ALU.mult, [ek, "masks"], [ek])
                    if kind == "ctx":
                        ts("dve", et, et, ctxf[:, B:B + 1], None, ALU.mult, None, [ek, "ctxf"], [ek])
                    if kind == "loc":
                        vl = vtok[:, j, g * 128:(g + 1) * 128]
                        vrd = [("vtok", j)]
                    else:
                        vl = cvt[:, j, g * 128:(g + 1) * 128]
                        vrd = ["cvt"]
                    mm1(PS[po][:], vl, et, (i == 0), (i == nkb - 1), vrd + [ek], [("ps", po)])
                    mm1(PS[pd][:], ones[:], et, (i == 0), (i == nkb - 1), ["ones", ek], [("ps", pd)])
                tA = tmpA[B % 2]
                tk = ("tmpA", B % 2)
                tt("dve", r4(tA), r4(PS[pd][:]),
                   esink[:, l, 4 * g:4 * g + 4].unsqueeze(2).to_broadcast([128, 4, 128]), ALU.add,
                   [("ps", pd), "esink"], [tk])
                S.add("dve", lambda e, tA=tA: e.reciprocal(out=tA, in_=tA), reads=[tk], writes=[tk])
                tt("dve", attraw[g][:, :, B * 128:(B + 1) * 128], r4(PS[po][:]), r4(tA), ALU.mult,
                   [("ps", po), tk], ATTK[g])
        if _STOP[0] == 4:
            raise _StopBuild()
        for ti, (a, n) in enumerate(TT):
            pss = next_ps()
            for h in range(8):
                o_ = eT[h % 4]
                act(o_[:, 0:n], attraw[h // 4][:, h % 4, a:a + n], AF.Square, ATTK[h // 4], [("eT", h % 4)])
                mm1(PS[pss][:, 0:n], ones[:], o_[:, 0:n], (h == 0), (h == 7), ["ones", ("eT", h % 4)], [("ps", pss)])
            tA = tmpA[ti % 2]
            tk = ("tmpA", ti % 2)
            act(tA[:, 0:n], PS[pss][:, 0:n], AF.Ln, [("ps", pss)], [tk], scale=1.0 / 1024.0, bias=RMS_EPS)
            act(tA[:, 0:n], tA[:, 0:n], AF.Exp, [tk], [tk], scale=-0.5)
            for h in range(8):
                stt(MIX[:, h, a:a + n], attraw[h // 4][:, h % 4, a:a + n], attg[:, l, h:h + 1], tA[:, 0:n],
                    ALU.mult, ALU.mult, ATTK[h // 4] + ["attg", tk], [("MIX", h)])

        if _STOP[0] == 5:
            raise _StopBuild()
        barrier(ATT_TEMPK + HG_TEMPK)
        ident = cmats[:, 0, :]
        for h in range(8):
            rts = [load_w(win_d[l][wt[0] + i]) for i in range(5)]
            wt[0] += 5
            for ti, (a, n) in enumerate(TT):
                pb = [next_ps() for _ in range(5)]
                for i in range(5):
                    project(rts[i], a, n, pb[i])
                act(qf[:, a:a + n], PS[pb[0]][:, 0:n], AF.Silu, [("ps", pb[0])], ["qf"])
                act(sigf[0][:, a:a + n], PS[pb[1]][:, 0:n], AF.Sigmoid, [("ps", pb[1])], ["sig0"])
                act(sigf[1][:, a:a + n], PS[pb[2]][:, 0:n], AF.Sigmoid, [("ps", pb[2])], ["sig1"])
                act(vTh[:, a:a + n], PS[pb[3]][:, 0:n], AF.Identity, [("ps", pb[3])], ["kmid"])
                act(gsil[:, a:a + n], PS[pb[4]][:, 0:n], AF.Silu, [("ps", pb[4])], ["gsil"])
            for B in range(NBLK):
                pt = next_ps()
                ptv = PS[pt][:].bitcast(BF16)[:, 0:128]
                S.add("pe", lambda e, ptv=ptv, B=B: e.transpose(ptv, vTh[:, B * 128:(B + 1) * 128], ident),
                      reads=["kmid", "cmats"], writes=[("ps", pt)])
                act(vtokH[:, B, :], ptv, AF.Identity, [("ps", pt)], ["vtokH"])
            pos = [next_ps() for _ in range(3)]
            for i in range(3):
                vmemset(PS[pos[i]][:], 0.0, [("ps", pos[i])])

            def o_ap(t0, n):
                return PS[pos[t0 // 512]][:, t0 % 512:t0 % 512 + n]

            for dr in range(2):
                fwd = (dr == 0)
                ts("dve", kk, sigf[dr], oml[:, dr, l, h:h + 1], lbv[:, dr, l, h:h + 1], ALU.mult, ALU.add,
                   ["sig%d" % dr, "oml", "lbv"], ["kk"])
                act(logf, kk, AF.Ln, ["kk"], ["E1"])
                ts("dve", kk, kk, -1.0, 1.0, ALU.mult, ALU.add, ["kk", "E1"], ["kk"])
                S.add("dve", lambda e: e.tensor_tensor_scan(out=bc, data0=scanm[:].rearrange("p c s -> p (c s)"),
                                                            data1=logf, initial=0.0, op0=ALU.mult, op1=ALU.add),
                      reads=["scanm", "E1"], writes=["bc"])
                if not fwd:
                    tt("dve", d1, logf, bc, ALU.subtract, ["E1", "bc"], ["d1"])
                    tt("dve", c32(bc), c32(d1), c32(bc)[:, :, 31:32].to_broadcast([128, NCH, 32]), ALU.add,
                       ["d1", "bc"], ["bc"])
                mid = 15 if fwd else 16
                lastp = 31 if fwd else 0
                bmid = c32(bc)[:, :, mid:mid + 1].to_broadcast([128, NCH, 32])
                blast = c32(bc)[:, :, lastp:lastp + 1].to_broadcast([128, NCH, 32])
                tt("dve", c32(d1), c32(bc), bmid, ALU.subtract, ["bc"], ["d1"])
                act(E1, d1, AF.Exp, ["d1"], ["E1"])
                tt("dve", qmid, qf, E1, ALU.mult, ["qf", "E1"], ["qmid"])
                act(E1, d1, AF.Exp, ["d1", "qmid"], ["E1"], scale=-1.0)
                tt("dve", kmid, kk, E1, ALU.mult, ["kk", "E1"], ["kmid"])
                tt("dve", c32(d1), blast, c32(bc), ALU.subtract, ["bc", "kmid"], ["d1"])
                act(E1, d1, AF.Exp, ["d1"], ["E1"])
                tt("dve", klast, kk, E1, ALU.mult, ["kk", "E1"], ["klast"])
                act(E1, bc, AF.Exp, ["bc", "klast"], ["E1"])
                tt("dve", qb, qf, E1, ALU.mult, ["qf", "E1"], ["qb"])
                act(decay.unsqueeze(2), c32(bc)[:, :, lastp:lastp + 1], AF.Exp, ["bc"], ["decay"])
                for B in range(NBLK):
                    pt = next_ps(exclude=pos)
                    ptv = PS[pt][:].bitcast(BF16)[:, 0:128]
                    S.add("pe", lambda e, ptv=ptv, B=B: e.transpose(ptv, klast[:, B * 128:(B + 1) * 128], ident),
                          reads=["klast", "cmats"], writes=[("ps", pt)])
                    tt("dve", klastM[:, 4 * B:4 * B + 4, :], ptv.unsqueeze(1).to_broadcast([128, 4, 128]), rmaskF[:],
                       ALU.mult, [("ps", pt), "rmaskF"], ["klastM"])
                mk = cmats[:, 1 if fwd else 2, :]
                for B in range(NBLK):
                    pa = next_ps(exclude=pos)
                    mm_group(PS[pa][:, 0:128], [(kmid[:, B * 128:(B + 1) * 128], qmid[:, B * 128:(B + 1) * 128])],
                             reads=["kmid", "qmid"], writes=[("ps", pa)])
                    tt("dve", aT[B % 2], PS[pa][:, 0:128], mk, ALU.mult, [("ps", pa), "cmats"], ["aT%d" % (B % 2)])
                    mm1(o_ap(B * 128, 128), vtokH[:, B, :], aT[B % 2], False, False,
                        ["vtokH", "aT%d" % (B % 2)], [("ps", pos[(B * 128) // 512])])
                order = list(range(NCH)) if fwd else list(range(NCH - 1, -1, -1))
                cur = 0
                for c in order:
                    slot = c // 8
                    first_in_slot = (c % 8 == 0) if fwd else (c % 8 == 7)
                    if first_in_slot:
                        nxt = Sf[(cur + 1) % 2]
                        nk = "Sf%d" % ((cur + 1) % 2)
                        if slot == 4:
                            vmemset(nxt, 0.0, [nk])
                        elif (fwd and slot == 0) or ((not fwd) and slot == 3):
                            dma("sp", S0t, s0_d[l, dr, h], [], ["S0t"], ("c", "S0t"))
                            vcopy(nxt, S0t, ["S0t"], [nk])
                        else:
                            ts("dve", nxt, Sf[cur], cfl[:, 0:1], None, ALU.mult, None, ["Sf%d" % cur, "cfl"], [nk])
                        cur = (cur + 1) % 2
                    act(Sall[:, c, :], Sf[cur], AF.Identity, ["Sf%d" % cur], ["Sall"])
                    pu = next_ps(exclude=pos)
                    mm_group(PS[pu][:, 0:128], [(klastM[:, c, :], vtokH[:, c // 4, :])], reads=["klastM", "vtokH"],
                             writes=[("ps", pu)])
                    nxt = Sf[(cur + 1) % 2]
                    nk = "Sf%d" % ((cur + 1) % 2)
                    stt(nxt, Sf[cur], decay[:, c:c + 1], PS[pu][:, 0:128], ALU.mult, ALU.add,
                        ["Sf%d" % cur, "decay", ("ps", pu)], [nk])
                    cur = (cur + 1) % 2
                    last_in_slot = (c % 8 == 7) if fwd else (c % 8 == 0)
                    if last_in_slot:
                        vcopy(Sfin[:, slot, :], Sf[cur], ["Sf%d" % cur], ["Sfin"])
                dma("sp", ost_d[l, dr, h], Sfin, ["Sfin"], [("ost", l, dr, h)], ("ost",))
                for c in range(NCH):
                    mm1(o_ap(c * 32, 32), Sall[:, c, :], qb[:, c * 32:(c + 1) * 32], False, False,
                        ["Sall", "qb"], [("ps", pos[(c * 32) // 512])])
            for ti, (a, n) in enumerate(TT):
                act(osq[:, 0:n], PS[pos[ti]][:, 0:n], AF.Square, [("ps", pos[ti]), "d1"], ["d1"])
                pss = next_ps(exclude=pos)
                mm_group(PS[pss][:, 0:n], [(ones[:], osq[:, 0:n])], reads=["ones", "d1"], writes=[("ps", pss)])
                act(rstdh[:, 0:n], PS[pss][:, 0:n], AF.Ln, [("ps", pss), "d1"], ["d1"], scale=1.0 / 128.0, bias=RMS_EPS)
                act(rstdh[:, 0:n], rstdh[:, 0:n], AF.Exp, ["d1"], ["d1"], scale=-0.5)
                tt("dve", th[:, 0:n], PS[pos[ti]][:, 0:n], rstdh[:, 0:n], ALU.mult, [("ps", pos[ti]), "d1"], ["d1"])
                stt(MIX[:, 8 + h, a:a + n], th[:, 0:n], hgg[:, l:l + 1], gsil[:, a:a + n], ALU.mult, ALU.mult,
                    ["d1", "hgg", "gsil"], [("MIX", 8 + h)])
        assert wt[0] == N_IN_TILES

        if _STOP[0] == 6:
            raise _StopBuild()
        barrier(ATT_TEMPK + HG_TEMPK + XK)
        for k4 in range(4):
            dma("sp", X[:, 4 * k4:4 * k4 + 4, :], xsp_d[:, 4 * k4:4 * k4 + 4, :], ["xspill"], XK[4 * k4:4 * k4 + 4],
                ("xin", k4))
        for k4 in range(4):
            act(X[:, 4 * k4:4 * k4 + 4, :], X[:, 4 * k4:4 * k4 + 4, :], AF.Identity, XK[4 * k4:4 * k4 + 4],
                XK[4 * k4:4 * k4 + 4], scale=ALPHA)

        if _STOP[0] == 7:
            raise _StopBuild()
        def resid_accum(oc, ti, a, n, pbank, gate):
            j = 0 if ti < 2 else 1
            stt(X[:, oc, a:a + n], PS[pbank][:, 0:n], gate[:, oc, j:j + 1], X[:, oc, a:a + n], ALU.mult, ALU.add,
                [("ps", pbank), MK, ("X", oc)], [("X", oc)])

        for oc in range(KC):
            ri = load_w(wo_d[l][oc])
            for ti, (a, n) in enumerate(TT):
                pb = next_ps()
                mm_group(PS[pb][:, 0:n], [(RING[ri][:, k, :], MIX[:, k, a:a + n]) for k in range(KC)],
                         reads=[("ring", ri)] + MIXK, writes=[("ps", pb)])
                resid_accum(oc, ti, a, n, pb, g1)

        if _STOP[0] == 8:
            raise _StopBuild()
        def layer_norm(which, write_xm):
            for ti, (a, n) in enumerate(TT):
                j = 0 if ti < 2 else 1
                act(xb[:, :, 0:n], X[:, :, a:a + n], AF.Identity, XK, XBK)
                act(xsq[:, :, 0:n], X[:, :, a:a + n], AF.Square, XK, XSQK)
                p1 = next_ps()
                p2 = next_ps()
                mm_group(PS[p1][:, 0:n], [(ones[:], xb[:, k, 0:n]) for k in range(KC)], reads=["ones"] + XBK,
                         writes=[("ps", p1)])
                mm_group(PS[p2][:, 0:n], [(ones[:], xsq[:, k, 0:n]) for k in range(KC)], reads=["ones"] + XSQK,
                         writes=[("ps", p2)])
                ts("dve", mean[:, 0:n], PS[p1][:, 0:n], 1.0 / D, None, ALU.mult, None, [("ps", p1)], ["mean"])
                tt("dve", rtmp[:, 0:n], mean[:, 0:n], mean[:, 0:n], ALU.mult, ["mean"], ["rtmp"])
                stt(rstd[:, 0:n], PS[p2][:, 0:n], 1.0 / D, rtmp[:, 0:n], ALU.mult, ALU.subtract,
                    [("ps", p2), "rtmp"], ["rstd"])
                act(rstd[:, 0:n], rstd[:, 0:n], AF.Ln, ["rstd"], ["rstd"], bias=LN_EPS)
                act(rstd[:, 0:n], rstd[:, 0:n], AF.Exp, ["rstd"], ["rstd"], scale=-0.5)
                for k in range(KC):
                    tt("pool", X[:, k, a:a + n], X[:, k, a:a + n], mean[:, 0:n], ALU.subtract, [("X", k), "mean"],
                       [("X", k)])
                    tt("dve", X[:, k, a:a + n], X[:, k, a:a + n], rstd[:, 0:n], ALU.mult, [("X", k), "rstd"],
                       [("X", k)])
                    act(X[:, k, a:a + n], X[:, k, a:a + n], AF.Identity, [("X", k), "lng", "lnb"], [("X", k)],
                        scale=lng[:, l, which, k:k + 1], bias=lnb[:, l, which, k:k + 1])
                    if write_xm:
                        ts("dve", XM[:, k, a:a + n], X[:, k, a:a + n], sc2p[:, k, j:j + 1], sh2[:, k, j:j + 1],
                           ALU.mult, ALU.add, [("X", k), S2K, MK], [("XM", k)])

        layer_norm(0, True)
        for k4 in range(4):
            act(X[:, 4 * k4:4 * k4 + 4, :], X[:, 4 * k4:4 * k4 + 4, :], AF.Identity, XK[4 * k4:4 * k4 + 4],
                XK[4 * k4:4 * k4 + 4], scale=ALPHA)

        if _STOP[0] == 9:
            raise _StopBuild()
        nxt = (l + 1 < L)
        pmn = next_ps() if nxt else None
        excl = (pmn,) if nxt else ()
        mct = [0]

        def mods_step(nt):
            if not nxt:
                return
            for _ in range(nt):
                if mct[0] < 96:
                    mods_tile(l + 1, mct[0], pmn)
                    mct[0] += 1

        for grp in range(4):
            for jj in range(16):
                ri = load_w(wup_d[l][grp * 16 + jj])
                for ti, (a, n) in enumerate(TT):
                    pb = next_ps(exclude=excl)
                    project(ri, a, n, pb)
                    rl = tmp_relu[ti % 2]
                    act(rl[:, 0:n], PS[pb][:, 0:n], AF.Relu, [("ps", pb)], [("relu", ti % 2)])
                    tt("dve", HID[:, jj, a:a + n], rl[:, 0:n], rl[:, 0:n], ALU.mult, [("relu", ti % 2)], [("MIX", jj)])
                mods_step(1)
            for oc in range(KC):
                ri = load_w(wdn_d[l][grp * 16 + oc])
                for ti, (a, n) in enumerate(TT):
                    pb = next_ps(exclude=excl)
                    mm_group(PS[pb][:, 0:n], [(RING[ri][:, k, :], HID[:, k, a:a + n]) for k in range(KC)],
                             reads=[("ring", ri)] + MIXK, writes=[("ps", pb)])
                    resid_accum(oc, ti, a, n, pb, g2)
                mods_step(1)
        if nxt:
            mods_step(96)
            mods_finish(l + 1, pmn)
        layer_norm(1, False)

    try:
        for l in range(L):
            layer(l)
    except _StopBuild:
        pass
    for k4 in range(4):
        dma("sp", yT_d[:, 4 * k4:4 * k4 + 4, :], X[:, 4 * k4:4 * k4 + 4, :], XK[4 * k4:4 * k4 + 4], [("yT", k4)],
            ("yout", k4))

    with nc.Block() as block:
        S.emit(nc, stack, block)
    stack.close()
    return nc


def _core_slots(core):
    if core < 2:
        return [("s", core, 256 * s) for s in range(4)] + [("p", 30 + core, 0)]
    return [("p", (core - 2) * 5 + s, 0) for s in range(5)]


def _fm(vec2d):
    a = np.asarray(vec2d, np.float32)
    n = a.shape[-1] // 128
    a = a.reshape(a.shape[:-1] + (n, 128))
    return np.ascontiguousarray(np.moveaxis(a, -1, 0))


def _retile(w):
    K, N = w.shape
    n = N // 128
    return np.ascontiguousarray(w.reshape(K // 128, 128, n, 128).transpose(2, 1, 0, 3)).reshape(n, 128, (K // 128) * 128)


def _win_cols():
    perm = np.arange(128) ^ 32
    ar = np.arange(128)
    cols = []
    for g in range(2):
        for r in range(4):
            h = 4 * g + r
            cols.append(h * 128 + ar)
            cols.append(h * 128 + perm)
        cols.append(1024 + g * 128 + ar)
        cols.append(1024 + g * 128 + perm)
        if g == 0:
            cols.append(1280 + ar)
            cols.append(1280 + 128 + ar)
    for h in range(8):
        for base in (1536, 2560, 3584, 4608, 5632):
            cols.append(base + h * 128 + ar)
    return np.concatenate(cols)


def _rope_tables(slots):
    cosT = np.ones((128, NTOK), np.float32)
    sinT = np.zeros((128, NTOK), np.float32)
    inv = (np.float32(10000.0) ** (-np.arange(32, dtype=np.float32) / np.float32(32))).astype(np.float32)
    for s, (kind, seq, start) in enumerate(slots):
        if kind != "s":
            continue
        t = start + np.arange(256)
        row = (t // 64).astype(np.float32)
        col = (t % 64).astype(np.float32)
        ang_r = (row[None, :] * inv[:, None]).astype(np.float32)
        ang_c = (col[None, :] * inv[:, None]).astype(np.float32)
        sl = slice(256 * s, 256 * (s + 1))
        cr, sr, cc, sc_ = np.cos(ang_r), np.sin(ang_r), np.cos(ang_c), np.sin(ang_c)
        cosT[0:32, sl] = cr
        cosT[32:64, sl] = cr
        cosT[64:96, sl] = cc
        cosT[96:128, sl] = cc
        sinT[0:32, sl] = -sr
        sinT[32:64, sl] = sr
        sinT[64:96, sl] = -sc_
        sinT[96:128, sl] = sc_
    return cosT, sinT


def _masks(slots):
    m = np.zeros((128, NBLK, 3, 128), np.float32)
    ctxflag = np.zeros((128, NBLK), np.float32)
    ar = np.arange(128)
    for B in range(NBLK):
        kind, seq, start = slots[B // 2]
        qpos = start + (B % 2) * 128 + ar
        if kind == "s":
            ctxflag[:, B] = 1.0
        for nb, KB in ((0, B - 1), (1, B), (2, B + 1)):
            if KB < 0 or KB >= NBLK:
                continue
            kkind, kseq, kstart = slots[KB // 2]
            if (kkind, kseq) != (kind, seq):
                continue
            kpos = kstart + (KB % 2) * 128 + ar
            if kind == "s":
                ok = np.abs(kpos[:, None] - qpos[None, :]) <= 128
            else:
                ok = np.ones((128, 128), bool)
            m[:, B, nb, :] = ok.astype(np.float32)
    cfl = np.full((128, 1), 1.0 if slots[0][0] == "s" else 0.0, np.float32)
    return m, ctxflag, cfl


def _const_mats():
    cm = np.zeros((4, 128, 128), np.float32)
    ar = np.arange(128)
    cm[0] = np.eye(128, dtype=np.float32)
    same = (ar[:, None] // 32) == (ar[None, :] // 32)
    cm[1] = (same & (ar[:, None] <= ar[None, :])).astype(np.float32)
    cm[2] = (same & (ar[:, None] >= ar[None, :])).astype(np.float32)
    rowmask = np.zeros((128, 4), np.float32)
    for c in range(4):
        rowmask[32 * c:32 * (c + 1), c] = 1.0
    return cm, rowmask


def _prepare(inp, L, cores):
    f = lambda k: np.asarray(inp[k], np.float32)
    w_in = f("w_in")
    cols = _win_cols()
    shared = {}
    for l in range(L):
        shared[f"wmod{l}"] = _retile(f("w_mod")[l])
        shared[f"win{l}"] = _retile(w_in[l][:, cols])
        shared[f"wo{l}"] = _retile(f("w_o")[l])
        shared[f"wup{l}"] = _retile(f("w_up")[l])
        shared[f"wdn{l}"] = np.ascontiguousarray(
            f("w_down")[l].reshape(4, 16, 128, 16, 128).transpose(0, 3, 2, 1, 4)).reshape(64, 128, 2048)
    shared.update({
        "bmod": _fm(f("b_mod")[:L]),
        "lng": _fm(f("ln_g")[:L]),
        "lnb": _fm(f("ln_b")[:L]),
        "attg": _fm(f("attn_norm_g")[:L]),
        "hgg": np.ascontiguousarray(f("hg_norm_g")[:L].T),
        "lbl": _fm(f("hg_lb_logits")),
        "sink": np.ascontiguousarray(np.broadcast_to(f("attn_sink")[:L][None], (128, L, 8))),
    })
    cm, rowmask = _const_mats()
    shared["cmats"] = cm
    shared["rowmask"] = rowmask
    xp, xs = f("x_prompt"), f("x_sample")
    maps = []
    for core in cores:
        slots = _core_slots(core)
        xs_ = []
        for kind, seq, start in slots:
            xs_.append(xs[seq, start:start + 256] if kind == "s" else xp[seq])
        xc = np.concatenate(xs_, 0)
        d = dict(shared)
        d["xT"] = np.ascontiguousarray(xc.reshape(NTOK, KC, 128).transpose(2, 1, 0))
        condA = f("c")[core] if core < 2 else f("c_ctx")
        condB = f("c_ctx")
        d["cond"] = _fm(np.stack([condA, condB]))
        d["cond"] = np.ascontiguousarray(d["cond"].transpose(0, 2, 1))
        cosT, sinT = _rope_tables(slots)
        d["cosT"], d["sinT"] = cosT, sinT
        m, ctxflag, cfl = _masks(slots)
        d["masks"], d["ctxflag"], d["cflag"] = m, ctxflag, cfl
        if core < 2:
            ck = f("cache_k")[core][:L]
            cv = f("cache_v")[core][:L]
            d["ckT"] = np.ascontiguousarray(ck.transpose(0, 3, 2, 1))
            d["cv"] = np.ascontiguousarray(cv.reshape(L, 4, 128, 256).transpose(0, 2, 1, 3))
            d["s0"] = np.ascontiguousarray(np.stack([f("state_hgrn_fwd")[core][:L], f("state_hgrn_bwd")[core][:L]], 1))
        else:
            d["ckT"] = np.zeros((L, 128, 2, 512), np.float32)
            d["cv"] = np.zeros((L, 128, 4, 256), np.float32)
            d["s0"] = np.zeros((L, 2, 8, 128, 128), np.float32)
        maps.append(d)
    return maps


_PROG = {}


def _run(inp, L=DEPTH, cores=tuple(range(8)), trace=False):
    if L not in _PROG:
        _PROG[L] = build_program(L)
    nc = _PROG[L]
    maps = _prepare(inp, L, cores)
    res = run_bass_kernel_spmd(nc, maps, core_ids=list(range(len(cores))), **({"trace": True} if trace else {}))
    return res


def _gather(results, L, cores):
    BATCH, SEQ, DEC_BATCH, DEC_SEQ = 32, 256, 2, 1024
    y_p = np.zeros((BATCH, SEQ, D), np.float32)
    y_s = np.zeros((DEC_BATCH, DEC_SEQ, D), np.float32)
    nk = np.zeros((BATCH, L, SEQ, 2, 128), np.float32)
    nv = np.zeros((BATCH, L, SEQ, 2, 128), np.float32)
    sf = np.zeros((BATCH, L, 8, 128, 128), np.float32)
    sbw = np.zeros((BATCH, L, 8, 128, 128), np.float32)
    for ci, core in enumerate(cores):
        r = results[ci]
        y = np.asarray(r["yT"]).transpose(2, 1, 0).reshape(NTOK, D)
        okT = np.asarray(r["okT"])
        ov = np.asarray(r["ov"])
        ost = np.asarray(r["ost"])
        for s, (kind, seq, start) in enumerate(_core_slots(core)):
            ys = y[256 * s:256 * (s + 1)]
            if kind == "s":
                y_s[seq, start:start + 256] = ys
            else:
                y_p[seq] = ys
                nk[seq] = okT[:, :, :, 256 * s:256 * (s + 1)].transpose(0, 3, 1, 2)
                nv[seq] = ov[:, 2 * s:2 * s + 2].reshape(L, 256, 2, 128)
                sf[seq] = ost[:, 0, :, :, s, :]
                sbw[seq] = ost[:, 1, :, :, s, :]
    return y_p, y_s, nk, nv, sf, sbw


def kernel(**inputs):
    res = _run(inputs, DEPTH, tuple(range(8)))
    return _gather(res.results, DEPTH, tuple(range(8)))
```

```python
from contextlib import ExitStack
import numpy as np
import concourse.bass as bass
import concourse.mybir as mybir
from concourse.bass_utils import run_bass_kernel_spmd

F32 = mybir.dt.float32
BF16 = mybir.dt.bfloat16
AF = mybir.ActivationFunctionType
ALU = mybir.AluOpType

D = 2048
KC = 16
NTOK = 1280
NBLK = 10
NCH = 40
NSLOT = 5
DEPTH = 4
ALPHA = (2 * DEPTH) ** 0.25
ATT_SCALE = 128 ** -0.5
LN_EPS = 1e-5
RMS_EPS = 1e-6
TT = [(0, 512), (512, 512), (1024, 256)]
NRING = 6
N_IN_TILES = 62
_STOP = [None]


class _StopBuild(Exception):
    pass


class _Op:
    __slots__ = ("eng", "fn", "deps", "sig", "sem", "tick", "dma", "ndma", "epoch", "waits", "vc")


class Sched:
    def __init__(self):
        self.ops = []
        self.lastw = {}
        self.readers = {}
        self.epoch = 0

    def add(self, eng, fn, reads=(), writes=(), dma=None, ndma=1):
        op = _Op()
        op.eng, op.fn, op.dma, op.ndma, op.sig = eng, fn, dma, ndma, False
        op.epoch = self.epoch
        deps = {}
        psr = [r for r in reads if isinstance(r, tuple) and r and r[0] == "ps"]
        if psr:
            reads = [r for r in reads if r not in psr]
            writes = list(writes) + [r for r in psr if r not in writes]

        def consider(d, kind):
            if d is None or d is op:
                return
            if d.dma is None and op.dma is None and d.eng == eng:
                if eng == "pe":
                    return
            deps[id(d)] = d

        for r in reads:
            consider(self.lastw.get(r), "raw")
        for w in writes:
            consider(self.lastw.get(w), "waw")
            for rd in self.readers.get(w, ()):
                consider(rd, "war")
        op.deps = list(deps.values())
        for d in op.deps:
            d.sig = True
        for r in reads:
            self.readers.setdefault(r, []).append(op)
        for w in writes:
            self.lastw[w] = op
            self.readers[w] = []
        self.ops.append(op)
        return op

    def emit(self, nc, stack, block):
        sems = {}

        def get_sem(key):
            if key not in sems:
                sems[key] = stack.enter_context(nc.semaphore("s_" + "_".join(str(k) for k in key)))
            return sems[key]

        counts = {}
        for op in self.ops:
            if op.dma is not None:
                key = ("d", op.dma)
                counts[key] = counts.get(key, 0) + 16 * op.ndma
            else:
                key = ("e", op.eng, op.epoch)
                if op.sig:
                    counts[key] = counts.get(key, 0) + 1
            op.sem = key
            op.tick = counts.get(key, 0)
        known = {}
        for op in self.ops:
            k = known.setdefault(op.eng, {})
            need = []
            for d in sorted(op.deps, key=lambda d_: -d_.tick):
                if k.get(d.sem, 0) < d.tick:
                    need.append((d.sem, d.tick))
                    k[d.sem] = d.tick
                    for s_, t_ in d.vc.items():
                        if k.get(s_, 0) < t_:
                            k[s_] = t_
            op.waits = need
            op.vc = dict(k) if (op.sig or op.dma is not None) else None
        by_eng = {}
        for op in self.ops:
            by_eng.setdefault(op.eng, []).append(op)
        for op in self.ops:
            if op.sig or op.dma is not None:
                get_sem(op.sem)

        def run(engname, e):
            for op in by_eng.get(engname, []):
                waits = op.waits
                for sem_, tick_ in waits[1:]:
                    e.wait_ge(get_sem(sem_), tick_)
                ins = op.fn(e)
                if isinstance(ins, tuple):
                    first, last = ins
                elif isinstance(ins, list):
                    first, last = ins[0], ins
                else:
                    first = last = ins
                if waits:
                    first.wait_op(get_sem(waits[0][0]), waits[0][1], "sem-ge")
                if op.dma is not None:
                    lst = last if isinstance(last, list) else [last]
                    assert len(lst) == op.ndma, (len(lst), op.ndma)
                    for i_ in lst:
                        i_.then_inc(get_sem(op.sem), 16)
                elif op.sig:
                    last.then_inc(get_sem(op.sem), 1)
            done = {}
            for op in by_eng.get(engname, []):
                if op.dma is not None:
                    done[op.sem] = max(done.get(op.sem, 0), op.tick)
            for sem_, tick_ in done.items():
                e.wait_ge(get_sem(sem_), tick_)

        @block.tensor
        def _(e):
            run("pe", e)

        @block.scalar
        def _(e):
            run("act", e)

        @block.vector
        def _(e):
            run("dve", e)

        @block.gpsimd
        def _(e):
            run("pool", e)

        @block.sync
        def _(e):
            run("sp", e)


def build_program(n_layers=DEPTH):
    L = n_layers
    nc = bass.Bass("TRN2", target_bir_lowering=False)

    def din(name, shape):
        return nc.dram_tensor(name, list(shape), F32, kind="ExternalInput").ap()

    def dout(name, shape):
        return nc.dram_tensor(name, list(shape), F32, kind="ExternalOutput").ap()

    xT_d = din("xT", [128, KC, NTOK])
    cond_d = din("cond", [128, KC, 2])
    bmod_d = din("bmod", [128, L, 96])
    lng_d = din("lng", [128, L, 2, KC])
    lnb_d = din("lnb", [128, L, 2, KC])
    attg_d = din("attg", [128, L, 8])
    hgg_d = din("hgg", [128, L])
    lbl_d = din("lbl", [128, 2, DEPTH, 8])
    sink_d = din("sink", [128, L, 8])
    cos_d = din("cosT", [128, NTOK])
    sin_d = din("sinT", [128, NTOK])
    msk_d = din("masks", [128, NBLK, 3, 128])
    ctxf_d = din("ctxflag", [128, NBLK])
    cfl_d = din("cflag", [128, 1])
    ckT_d = din("ckT", [L, 128, 2, 512])
    cv_d = din("cv", [L, 128, 4, 256])
    s0_d = din("s0", [L, 2, 8, 128, 128])
    cm_d = din("cmats", [4, 128, 128])
    rmask_d = din("rowmask", [128, 4])
    wmod_d = [din(f"wmod{l}", [96, 128, 2048]) for l in range(L)]
    win_d = [din(f"win{l}", [N_IN_TILES, 128, 2048]) for l in range(L)]
    wo_d = [din(f"wo{l}", [16, 128, 2048]) for l in range(L)]
    wup_d = [din(f"wup{l}", [64, 128, 2048]) for l in range(L)]
    wdn_d = [din(f"wdn{l}", [64, 128, 2048]) for l in range(L)]

    yT_d = dout("yT", [128, KC, NTOK])
    okT_d = dout("okT", [L, 2, 128, NTOK])
    ov_d = dout("ov", [L, NBLK, 128, 256])
    ost_d = dout("ost", [L, 2, 8, 128, NSLOT, 128])
    xsp_d = nc.dram_tensor("xspill", [128, KC, NTOK], F32, kind="Internal").ap()

    S = Sched()
    stack = ExitStack()

    def sb(name, shape, dt):
        return stack.enter_context(nc.sbuf_tensor(name, list(shape), dt))

    RA = sb("RA", [128, KC * NTOK], F32)
    XM = sb("XM", [128, KC, NTOK], BF16)
    RC = sb("RC", [128, KC * NTOK], BF16)
    RING = [sb(f"ring{i}", [128, KC, 128], BF16) for i in range(NRING)]
    X = RA[:].rearrange("p (k t) -> p k t", k=KC)
    MIX = RC[:].rearrange("p (k t) -> p k t", k=KC)
    HID = MIX

    cond_f = sb("cond_f", [128, KC, 2], F32)
    condS = sb("condS", [128, KC, 2], BF16)
    bmod = sb("bmod_s", [128, L, 96], F32)
    lng = sb("lng_s", [128, L, 2, KC], F32)
    lnb = sb("lnb_s", [128, L, 2, KC], F32)
    attg = sb("attg_s", [128, L, 8], F32)
    hgg = sb("hgg_s", [128, L], F32)
    lbl = sb("lbl_s", [128, 2, DEPTH, 8], F32)
    lbv = sb("lbv", [128, 2, DEPTH, 8], F32)
    oml = sb("oml", [128, 2, DEPTH, 8], F32)
    lbsum = sb("lbsum", [128, 2, 8], F32)
    esink = sb("esink", [128, L, 8], F32)
    ctxf = sb("ctxf", [128, NBLK], F32)
    cfl = sb("cfl", [128, 1], F32)
    rmask = sb("rmask", [128, 4], F32)
    cmats = sb("cmats_s", [128, 3, 128], BF16)
    ones = sb("ones", [128, 128], BF16)
    rmaskF = sb("rmaskF", [128, 4, 128], BF16)
    scanm = sb("scanm", [128, NCH, 32], BF16)
    mods2 = [sb("mods0", [128, 96, 2], F32), sb("mods1", [128, 96, 2], F32)]
    sc1p2 = [sb("sc1p0", [128, KC, 2], F32), sb("sc1p1", [128, KC, 2], F32)]
    sc2p2 = [sb("sc2p0", [128, KC, 2], F32), sb("sc2p1", [128, KC, 2], F32)]
    dummy = sb("dummy_t", [128, 8], F32)

    PS = [stack.enter_context(nc.psum_tensor(f"ps{i}", [128, 512], F32)) for i in range(8)]

    ra_off = [0]

    def ra_f32(nwords):
        o = ra_off[0]
        ra_off[0] += nwords
        assert ra_off[0] <= KC * NTOK, ra_off[0]
        return RA[:, o:o + nwords]

    def ra_bf16(nel):
        assert nel % 2 == 0
        return ra_f32(nel // 2).bitcast(BF16)

    cosT = ra_f32(NTOK)
    sinT = ra_f32(NTOK)
    masks = ra_bf16(NBLK * 3 * 128).rearrange("p (b n q) -> p b n q", b=NBLK, n=3)
    ckT = ra_bf16(2 * 512).rearrange("p (g t) -> p g t", g=2)
    cvt = ra_bf16(4 * 256).rearrange("p (b c) -> p b c", b=4)
    qT = ra_bf16(4 * NTOK).rearrange("p (r t) -> p r t", r=4)
    kT = ra_bf16(NTOK)
    vtok = ra_bf16(NBLK * 256).rearrange("p (b c) -> p b c", b=NBLK)
    kraw = ra_f32(NTOK)
    vraw = [ra_f32(256), ra_f32(256)]
    tmpA = [ra_f32(512), ra_f32(512)]
    tmpB = [ra_f32(512), ra_f32(512)]
    eT = [ra_bf16(512) for _ in range(4)]
    attraw1 = ra_f32(4 * NTOK).rearrange("p (r t) -> p r t", r=4)
    attraw0 = RC[:, 8 * NTOK:16 * NTOK].bitcast(F32).rearrange("p (r t) -> p r t", r=4)
    attraw = [attraw0, attraw1]
    ATTK = [[("MIX", k) for k in range(8, 16)], ["attraw1"]]
    ra_off[0] = 0
    qf = ra_f32(NTOK)
    sigf = [ra_f32(NTOK), ra_f32(NTOK)]
    gsil = ra_f32(NTOK)
    vtokH = ra_bf16(NBLK * 128).rearrange("p (b c) -> p b c", b=NBLK)
    kk = ra_f32(NTOK)
    E1 = ra_f32(NTOK)
    logf = E1
    bc = ra_f32(NTOK)
    d1 = ra_f32(NTOK)
    qmid = ra_bf16(NTOK)
    kmid = ra_bf16(NTOK)
    vTh = kmid
    klast = ra_bf16(NTOK)
    qb = ra_bf16(NTOK)
    decay = ra_f32(NCH)
    klastM = ra_bf16(NCH * 128).rearrange("p (c k) -> p c k", c=NCH)
    aT = [ra_bf16(128), ra_bf16(128)]
    Sall = ra_bf16(NCH * 128).rearrange("p (c v) -> p c v", c=NCH)
    Sf = [ra_f32(128), ra_f32(128)]
    Sfin = ra_f32(NSLOT * 128).rearrange("p (s v) -> p s v", s=NSLOT)
    S0t = ra_f32(128)
    osq = d1[:, 0:256].bitcast(BF16)
    rstdh = d1[:, 256:768]
    th = d1[:, 768:1280]
    xb = RC[:, 0:KC * 512].rearrange("p (k t) -> p k t", k=KC)
    xsq = RC[:, 7 * NTOK:7 * NTOK + KC * 512].rearrange("p (k t) -> p k t", k=KC)
    XBK = [("MIX", k) for k in range(0, 7)]
    XSQK = [("MIX", k) for k in range(7, 14)]
    mean = sb("ln_mean", [128, 512], F32)
    rstd = sb("ln_rstd", [128, 512], F32)
    rtmp = sb("ln_tmp", [128, 512], F32)
    tmp_relu = [sb("relu0", [128, 512], F32), sb("relu1", [128, 512], F32)]

    ps_rr = [0]

    def next_ps(exclude=()):
        while True:
            i = ps_rr[0] % 8
            ps_rr[0] += 1
            if i not in exclude:
                return i

    ring_rr = [0]

    def load_w(src_ap):
        i = ring_rr[0] % NRING
        ring_rr[0] += 1
        dst = RING[i]
        S.add("pool", lambda e, dst=dst, src=src_ap: e.dma_start(
            out=dst[:].rearrange("p k c -> p (k c)"), in_=src),
            writes=[("ring", i)], dma=("ring", i))
        return i

    def mm_group(out_ap, pairs, reads, writes):
        def fn(e, out_ap=out_ap, pairs=pairs):
            ins = first = None
            n = len(pairs)
            for j, (l_, r_) in enumerate(pairs):
                ins = e.matmul(out_ap, l_, r_, start=(j == 0), stop=(j == n - 1))
                if first is None:
                    first = ins
            return (first, ins)
        return S.add("pe", fn, reads=reads, writes=writes)

    def mm1(out_ap, l_, r_, start, stop, reads, writes):
        return S.add("pe", lambda e: e.matmul(out_ap, l_, r_, start=start, stop=stop, skip_group_check=True),
                     reads=reads, writes=writes)

    def act(out, in_, func, reads, writes, scale=None, bias=None):
        kw = {}
        if scale is not None:
            kw["scale"] = scale
        if bias is not None:
            kw["bias"] = bias
        return S.add("act", lambda e: e.activation(out=out, in_=in_, func=func, **kw), reads=reads, writes=writes)

    def tt(eng, out, in0, in1, op, reads, writes):
        return S.add(eng, lambda e: e.tensor_tensor(out=out, in0=in0, in1=in1, op=op), reads=reads, writes=writes)

    def ts(eng, out, in0, s1, s2, op0, op1, reads, writes):
        if op1 is None:
            return S.add(eng, lambda e: e.tensor_scalar(out=out, in0=in0, scalar1=s1, scalar2=None, op0=op0),
                         reads=reads, writes=writes)
        return S.add(eng, lambda e: e.tensor_scalar(out=out, in0=in0, scalar1=s1, scalar2=s2, op0=op0, op1=op1),
                     reads=reads, writes=writes)

    def stt(out, in0, scalar, in1, op0, op1, reads, writes):
        return S.add("dve", lambda e: e.scalar_tensor_tensor(out=out, in0=in0, scalar=scalar, in1=in1,
                                                             op0=op0, op1=op1), reads=reads, writes=writes)

    def vcopy(out, in_, reads, writes):
        return S.add("dve", lambda e: e.tensor_copy(out=out, in_=in_), reads=reads, writes=writes)

    def vmemset(ap, val, writes):
        return S.add("dve", lambda e: e.memset(ap, val), writes=writes)

    def dma(q, out, in_, reads, writes, key):
        return S.add(q, lambda e: e.dma_start(out=out, in_=in_), reads=reads, writes=writes, dma=key)

    def barrier(keys):
        return S.add("dve", lambda e: e.memset(dummy[:, 0:1], 0.0), writes=list(keys))

    XK = [("X", k) for k in range(KC)]
    XMK = [("XM", k) for k in range(KC)]
    MIXK = [("MIX", k) for k in range(KC)]
    ATT_TEMPK = (["cosT", "sinT", "masks", "ckT", "cvt", "kT", "kraw", "vraw0", "vraw1", "attraw1"] +
                 [("qT", r) for r in range(4)] + [("vtok", B) for B in range(NBLK)] +
                 [("tmpA", 0), ("tmpA", 1), ("tmpB", 0), ("tmpB", 1)] + [("eT", i) for i in range(4)])
    HG_TEMPK = ["qf", "sig0", "sig1", "gsil", "vtokH", "kk", "E1", "bc", "d1", "qmid", "kmid", "klast", "qb",
                "decay", "klastM", "aT0", "aT1", "Sall", "Sf0", "Sf1", "Sfin", "S0t"]

    for k4 in range(4):
        dma("sp", X[:, 4 * k4:4 * k4 + 4, :], xT_d[:, 4 * k4:4 * k4 + 4, :], [], XK[4 * k4:4 * k4 + 4], ("xin", k4))
    small = [(cond_f, cond_d, "cond"), (bmod, bmod_d, "bmod"), (lng, lng_d, "lng"), (lnb, lnb_d, "lnb"),
             (attg, attg_d, "attg"), (hgg, hgg_d, "hgg"), (lbl, lbl_d, "lbl"), (esink, sink_d, "esink"),
             (ctxf, ctxf_d, "ctxf"), (cfl, cfl_d, "cfl"), (rmask, rmask_d, "rmask")]
    for t_, d_, nm in small:
        dma("sp", t_[:], d_, [], [nm], ("c", nm))
    dma("pool", cmats[:], cm_d[0:3].rearrange("m p c -> p m c"), [], ["cmats"], ("c", "cmats"))
    vmemset(ones[:], 1.0, ["ones"])
    vcopy(rmaskF[:], rmask[:].unsqueeze(2).to_broadcast([128, 4, 128]), ["rmask"], ["rmaskF"])
    vmemset(scanm[:], 1.0, ["scanm"])
    vmemset(scanm[:, :, 0:1], 0.0, ["scanm"])
    act(condS[:], cond_f[:], AF.Silu, ["cond"], ["condS"])
    act(esink[:], esink[:], AF.Exp, ["esink"], ["esink"])
    act(lbl[:], lbl[:], AF.Exp, ["lbl"], ["lbl"])
    tt("dve", lbsum[:], lbl[:, :, 0, :], lbl[:, :, 1, :], ALU.add, ["lbl"], ["lbsum"])
    for l_ in range(2, DEPTH):
        tt("dve", lbsum[:], lbsum[:], lbl[:, :, l_, :], ALU.add, ["lbl", "lbsum"], ["lbsum"])
    S.add("dve", lambda e: e.reciprocal(out=lbsum[:], in_=lbsum[:]), reads=["lbsum"], writes=["lbsum"])
    vmemset(lbv[:, :, 0, :], 0.0, ["lbv"])
    for l_ in range(1, DEPTH):
        tt("dve", lbv[:, :, l_, :], lbl[:, :, l_, :], lbsum[:], ALU.mult, ["lbl", "lbsum", "lbv"], ["lbv"])
        if l_ > 1:
            tt("dve", lbv[:, :, l_, :], lbv[:, :, l_, :], lbv[:, :, l_ - 1, :], ALU.add, ["lbv"], ["lbv"])
    ts("dve", oml[:], lbv[:], -1.0, 1.0, ALU.mult, ALU.add, ["lbv"], ["oml"])

    def r4(ap):
        return ap.rearrange("p (r q) -> p r q", r=4)

    def c32(ap):
        return ap.rearrange("p (c s) -> p c s", s=32)

    def mods_tile(l, ct, pm):
        ri = load_w(wmod_d[l][ct])
        mm_group(PS[pm][:, 2 * ct:2 * ct + 2],
                 [(RING[ri][:, k, :], condS[:, k, :]) for k in range(KC)],
                 reads=[("ring", ri), "condS"], writes=[("ps", pm)])

    def mods_finish(l, pm):
        p_ = l % 2
        tt("dve", mods2[p_][:], PS[pm][:, 0:192].rearrange("p (c j) -> p c j", j=2),
           bmod[:, l, :].unsqueeze(2).to_broadcast([128, 96, 2]), ALU.add,
           [("ps", pm), "bmod"], [("mods", p_)])
        ts("dve", sc1p2[p_][:], mods2[p_][:, 16:32, :], 1.0, None, ALU.add, None, [("mods", p_)], [("sc1p", p_)])
        ts("dve", sc2p2[p_][:], mods2[p_][:, 64:80, :], 1.0, None, ALU.add, None, [("mods", p_)], [("sc2p", p_)])

    def layer(l):
        S.epoch = l
        if l == 0:
            pm = next_ps()
            for ct in range(96):
                mods_tile(0, ct, pm)
            mods_finish(0, pm)
        mods, sc1p, sc2p = mods2[l % 2], sc1p2[l % 2], sc2p2[l % 2]
        MK, S1K, S2K = ("mods", l % 2), ("sc1p", l % 2), ("sc2p", l % 2)
        sh1 = mods[:, 0:16, :]
        g1 = mods[:, 32:48, :]
        sh2 = mods[:, 48:64, :]
        g2 = mods[:, 80:96, :]

        if _STOP[0] == 1:
            raise _StopBuild()
        for k in range(KC):
            for j, (a, b) in enumerate([(0, 1024), (1024, 1280)]):
                if (k + j) % 2 == 0:
                    ts("dve", XM[:, k, a:b], X[:, k, a:b], sc1p[:, k, j:j + 1], sh1[:, k, j:j + 1], ALU.mult, ALU.add,
                       [("X", k), S1K, MK], [("XM", k)])
                else:
                    act(XM[:, k, a:b], X[:, k, a:b], AF.Identity, [("X", k), S1K, MK], [("XM", k)],
                        scale=sc1p[:, k, j:j + 1], bias=sh1[:, k, j:j + 1])
        if _STOP[0] == 2:
            raise _StopBuild()
        for k4 in range(4):
            dma("sp", xsp_d[:, 4 * k4:4 * k4 + 4, :], X[:, 4 * k4:4 * k4 + 4, :], XK[4 * k4:4 * k4 + 4], ["xspill"],
                ("xsp", k4))
        barrier(XK + ATT_TEMPK + HG_TEMPK)
        dma("sp", cosT, cos_d, [], ["cosT"], ("c", "cosT"))
        dma("sp", sinT, sin_d, [], ["sinT"], ("c", "sinT"))
        dma("pool", masks, msk_d, [], ["masks"], ("c", "masks"))
        dma("pool", ckT, ckT_d[l], [], ["ckT"], ("c", "ckT"))
        dma("pool", cvt, cv_d[l], [], ["cvt"], ("c", "cvt"))

        if _STOP[0] == 3:
            raise _StopBuild()
        wt = [0]

        def project(ri, a, n, pbank):
            mm_group(PS[pbank][:, 0:n], [(RING[ri][:, k, :], XM[:, k, a:a + n]) for k in range(KC)],
                     reads=[("ring", ri)] + XMK, writes=[("ps", pbank)])

        def rope_pair(dst_fn, keyname, raw_out=None):
            r0 = load_w(win_d[l][wt[0]])
            r1 = load_w(win_d[l][wt[0] + 1])
            wt[0] += 2
            for ti, (a, n) in enumerate(TT):
                p0 = next_ps()
                p1 = next_ps()
                project(r0, a, n, p0)
                project(r1, a, n, p1)
                tA = tmpA[ti % 2]
                tB = tmpB[ti % 2]
                if raw_out is not None:
                    vcopy(raw_out[:, a:a + n], PS[p0][:, 0:n], [("ps", p0)], ["kraw"])
                tt("dve", tA[:, 0:n], PS[p0][:, 0:n], cosT[:, a:a + n], ALU.mult, [("ps", p0), "cosT"], [("tmpA", ti % 2)])
                tt("dve", tB[:, 0:n], PS[p1][:, 0:n], sinT[:, a:a + n], ALU.mult, [("ps", p1), "sinT"], [("tmpB", ti % 2)])
                tt("pool", dst_fn(a, n), tA[:, 0:n], tB[:, 0:n], ALU.add, [("tmpA", ti % 2), ("tmpB", ti % 2)], [keyname])

        for g in range(2):
            if _STOP[0] == 34 and g == 1:
                raise _StopBuild()
            for r in range(4):
                rope_pair(lambda a, n, r=r: qT[:, r, a:a + n], ("qT", r))
            if _STOP[0] == 30:
                raise _StopBuild()
            rope_pair(lambda a, n: kT[:, a:a + n], "kT", raw_out=kraw)
            if _STOP[0] == 31:
                raise _StopBuild()
            dma("sp", okT_d[l, g], kraw, ["kraw"], [("okT", l, g)], ("okT",))
            if _STOP[0] == 32:
                raise _StopBuild()
            if g == 0:
                rv = [load_w(win_d[l][wt[0]]), load_w(win_d[l][wt[0] + 1])]
                wt[0] += 2
                for B in range(NBLK):
                    pv = next_ps()
                    for vg in range(2):
                        mm_group(PS[pv][:, vg * 128:(vg + 1) * 128],
                                 [(XM[:, k, B * 128:(B + 1) * 128], RING[rv[vg]][:, k, :]) for k in range(KC)],
                                 reads=[("ring", rv[vg])] + XMK, writes=[("ps", pv)])
                    act(vtok[:, B, :], PS[pv][:, 0:256], AF.Identity, [("ps", pv)], [("vtok", B)])
                    vcopy(vraw[B % 2], PS[pv][:, 0:256], [("ps", pv)], ["vraw%d" % (B % 2)])
                    dma("sp", ov_d[l, B], vraw[B % 2], ["vraw%d" % (B % 2)], [("ov", l, B)], ("ov", B % 2))
            if _STOP[0] == 33:
                raise _StopBuild()
            for B in range(NBLK):
                kbs = []
                if B - 1 >= 0:
                    kbs.append(("loc", B - 1, 0))
                kbs.append(("loc", B, None))
                if B + 1 < NBLK:
                    kbs.append(("loc", B + 1, 2))
                for j in range(4):
                    kbs.append(("ctx", j, None))
                po = next_ps()
                pd = next_ps(exclude=(po,))
                qrhs = qT[:, :, B * 128:(B + 1) * 128]
                qkeys = [("qT", r) for r in range(4)]
                nkb = len(kbs)
                sbank = {}

                def issue_s(i):
                    kind, j, mi = kbs[i]
                    pb = next_ps(exclude=(po, pd))
                    sbank[i] = pb
                    if kind == "loc":
                        lhs = kT[:, j * 128:(j + 1) * 128]
                        rd = ["kT"]
                    else:
                        lhs = ckT[:, g, j * 128:(j + 1) * 128]
                        rd = ["ckT"]
                    mm_group(r4(PS[pb][:]), [(lhs, qrhs)], reads=rd + qkeys, writes=[("ps", pb)])

                issue_s(0)
                for i in range(nkb):
                    if i + 1 < nkb:
                        issue_s(i + 1)
                    kind, j, mi = kbs[i]
                    pb = sbank[i]
                    et = eT[i % 4]
                    ek = ("eT", i % 4)
                    act(et, PS[pb][:], AF.Exp, [("ps", pb)], [ek], scale=ATT_SCALE)
                    if kind == "loc" and mi is not None:
                        tt("dve", r4(et), r4(et), masks[:, B, mi, :].unsqueeze(1).to_broadcast([128, 4, 128]),
                           ALU.mult, [ek, "masks"], [ek])
                    if kind == "ctx":
                        act(et, et, AF.Identity, [ek, "ctxf"], [ek], scale=ctxf[:, B:B + 1])
                    if kind == "loc":
                        vl = vtok[:, j, g * 128:(g + 1) * 128]
                        vrd = [("vtok", j)]
                    else:
                        vl = cvt[:, j, g * 128:(g + 1) * 128]
                        vrd = ["cvt"]
                    mm1(PS[po][:], vl, et, (i == 0), (i == nkb - 1), vrd + [ek], [("ps", po)])
                    mm1(PS[pd][:], ones[:], et, (i == 0), (i == nkb - 1), ["ones", ek], [("ps", pd)])
                tA = tmpA[B % 2]
                tk = ("tmpA", B % 2)
                tt("dve", r4(tA), r4(PS[pd][:]),
                   esink[:, l, 4 * g:4 * g + 4].unsqueeze(2).to_broadcast([128, 4, 128]), ALU.add,
                   [("ps", pd), "esink"], [tk])
                S.add("dve", lambda e, tA=tA: e.reciprocal(out=tA, in_=tA), reads=[tk], writes=[tk])
                tt("dve", attraw[g][:, :, B * 128:(B + 1) * 128], r4(PS[po][:]), r4(tA), ALU.mult,
                   [("ps", po), tk], ATTK[g])
        if _STOP[0] == 4:
            raise _StopBuild()
        for ti, (a, n) in enumerate(TT):
            pss = next_ps()
            for h in range(8):
                o_ = eT[h % 4]
                act(o_[:, 0:n], attraw[h // 4][:, h % 4, a:a + n], AF.Square, ATTK[h // 4], [("eT", h % 4)])
                mm1(PS[pss][:, 0:n], ones[:], o_[:, 0:n], (h == 0), (h == 7), ["ones", ("eT", h % 4)], [("ps", pss)])
            tA = tmpA[ti % 2]
            tk = ("tmpA", ti % 2)
            act(tA[:, 0:n], PS[pss][:, 0:n], AF.Ln, [("ps", pss)], [tk], scale=1.0 / 1024.0, bias=RMS_EPS)
            act(tA[:, 0:n], tA[:, 0:n], AF.Exp, [tk], [tk], scale=-0.5)
            for h in range(8):
                stt(MIX[:, h, a:a + n], attraw[h // 4][:, h % 4, a:a + n], attg[:, l, h:h + 1], tA[:, 0:n],
                    ALU.mult, ALU.mult, ATTK[h // 4] + ["attg", tk], [("MIX", h)])

        if _STOP[0] == 5:
            raise _StopBuild()
        barrier(ATT_TEMPK + HG_TEMPK)
        ident = cmats[:, 0, :]
        for h in range(8):
            rts = [load_w(win_d[l][wt[0] + i]) for i in range(5)]
            wt[0] += 5
            for ti, (a, n) in enumerate(TT):
                pb = [next_ps() for _ in range(5)]
                for i in range(5):
                    project(rts[i], a, n, pb[i])
                act(qf[:, a:a + n], PS[pb[0]][:, 0:n], AF.Silu, [("ps", pb[0])], ["qf"])
                act(sigf[0][:, a:a + n], PS[pb[1]][:, 0:n], AF.Sigmoid, [("ps", pb[1])], ["sig0"])
                act(sigf[1][:, a:a + n], PS[pb[2]][:, 0:n], AF.Sigmoid, [("ps", pb[2])], ["sig1"])
                act(vTh[:, a:a + n], PS[pb[3]][:, 0:n], AF.Identity, [("ps", pb[3])], ["kmid"])
                act(gsil[:, a:a + n], PS[pb[4]][:, 0:n], AF.Silu, [("ps", pb[4])], ["gsil"])
            for B in range(NBLK):
                pt = next_ps()
                ptv = PS[pt][:].bitcast(BF16)[:, 0:128]
                S.add("pe", lambda e, ptv=ptv, B=B: e.transpose(ptv, vTh[:, B * 128:(B + 1) * 128], ident),
                      reads=["kmid", "cmats"], writes=[("ps", pt)])
                act(vtokH[:, B, :], ptv, AF.Identity, [("ps", pt)], ["vtokH"])
            pos = [next_ps() for _ in range(3)]
            for i in range(3):
                vmemset(PS[pos[i]][:], 0.0, [("ps", pos[i])])

            def o_ap(t0, n):
                return PS[pos[t0 // 512]][:, t0 % 512:t0 % 512 + n]

            for dr in range(2):
                fwd = (dr == 0)
                act(kk, sigf[dr], AF.Identity, ["sig%d" % dr, "oml", "lbv"], ["kk"],
                    scale=oml[:, dr, l, h:h + 1], bias=lbv[:, dr, l, h:h + 1])
                act(logf, kk, AF.Ln, ["kk"], ["E1"])
                act(kk, kk, AF.Identity, ["kk", "E1"], ["kk"], scale=-1.0, bias=1.0)
                S.add("dve", lambda e: e.tensor_tensor_scan(out=bc, data0=scanm[:].rearrange("p c s -> p (c s)"),
                                                            data1=logf, initial=0.0, op0=ALU.mult, op1=ALU.add),
                      reads=["scanm", "E1"], writes=["bc"])
                if not fwd:
                    tt("dve", d1, logf, bc, ALU.subtract, ["E1", "bc"], ["d1"])
                    tt("dve", c32(bc), c32(d1), c32(bc)[:, :, 31:32].to_broadcast([128, NCH, 32]), ALU.add,
                       ["d1", "bc"], ["bc"])
                mid = 15 if fwd else 16
                lastp = 31 if fwd else 0
                bmid = c32(bc)[:, :, mid:mid + 1].to_broadcast([128, NCH, 32])
                blast = c32(bc)[:, :, lastp:lastp + 1].to_broadcast([128, NCH, 32])
                tt("dve", c32(d1), c32(bc), bmid, ALU.subtract, ["bc"], ["d1"])
                act(E1, d1, AF.Exp, ["d1"], ["E1"])
                tt("dve", qmid, qf, E1, ALU.mult, ["qf", "E1"], ["qmid"])
                act(E1, d1, AF.Exp, ["d1", "qmid"], ["E1"], scale=-1.0)
                tt("dve", kmid, kk, E1, ALU.mult, ["kk", "E1"], ["kmid"])
                tt("dve", c32(d1), blast, c32(bc), ALU.subtract, ["bc", "kmid"], ["d1"])
                act(E1, d1, AF.Exp, ["d1"], ["E1"])
                tt("dve", klast, kk, E1, ALU.mult, ["kk", "E1"], ["klast"])
                act(E1, bc, AF.Exp, ["bc", "klast"], ["E1"])
                tt("dve", qb, qf, E1, ALU.mult, ["qf", "E1"], ["qb"])
                act(decay.unsqueeze(2), c32(bc)[:, :, lastp:lastp + 1], AF.Exp, ["bc"], ["decay"])
                for B in range(NBLK):
                    pt = next_ps(exclude=pos)
                    ptv = PS[pt][:].bitcast(BF16)[:, 0:128]
                    S.add("pe", lambda e, ptv=ptv, B=B: e.transpose(ptv, klast[:, B * 128:(B + 1) * 128], ident),
                          reads=["klast", "cmats"], writes=[("ps", pt)])
                    tt("dve", klastM[:, 4 * B:4 * B + 4, :], ptv.unsqueeze(1).to_broadcast([128, 4, 128]), rmaskF[:],
                       ALU.mult, [("ps", pt), "rmaskF"], ["klastM"])
                mk = cmats[:, 1 if fwd else 2, :]
                for B in range(NBLK):
                    pa = next_ps(exclude=pos)
                    mm_group(PS[pa][:, 0:128], [(kmid[:, B * 128:(B + 1) * 128], qmid[:, B * 128:(B + 1) * 128])],
                             reads=["kmid", "qmid"], writes=[("ps", pa)])
                    tt("dve", aT[B % 2], PS[pa][:, 0:128], mk, ALU.mult, [("ps", pa), "cmats"], ["aT%d" % (B % 2)])
                    mm1(o_ap(B * 128, 128), vtokH[:, B, :], aT[B % 2], False, False,
                        ["vtokH", "aT%d" % (B % 2)], [("ps", pos[(B * 128) // 512])])
                order = list(range(NCH)) if fwd else list(range(NCH - 1, -1, -1))
                cur = 0
                for c in order:
                    slot = c // 8
                    first_in_slot = (c % 8 == 0) if fwd else (c % 8 == 7)
                    if first_in_slot:
                        nxt = Sf[(cur + 1) % 2]
                        nk = "Sf%d" % ((cur + 1) % 2)
                        if slot == 4:
                            vmemset(nxt, 0.0, [nk])
                        elif (fwd and slot == 0) or ((not fwd) and slot == 3):
                            dma("sp", S0t, s0_d[l, dr, h], [], ["S0t"], ("c", "S0t"))
                            vcopy(nxt, S0t, ["S0t"], [nk])
                        else:
                            ts("dve", nxt, Sf[cur], cfl[:, 0:1], None, ALU.mult, None, ["Sf%d" % cur, "cfl"], [nk])
                        cur = (cur + 1) % 2
                    act(Sall[:, c, :], Sf[cur], AF.Identity, ["Sf%d" % cur], ["Sall"])
                    pu = next_ps(exclude=pos)
                    mm_group(PS[pu][:, 0:128], [(klastM[:, c, :], vtokH[:, c // 4, :])], reads=["klastM", "vtokH"],
                             writes=[("ps", pu)])
                    nxt = Sf[(cur + 1) % 2]
                    nk = "Sf%d" % ((cur + 1) % 2)
                    stt(nxt, Sf[cur], decay[:, c:c + 1], PS[pu][:, 0:128], ALU.mult, ALU.add,
                        ["Sf%d" % cur, "decay", ("ps", pu)], [nk])
                    cur = (cur + 1) % 2
                    last_in_slot = (c % 8 == 7) if fwd else (c % 8 == 0)
                    if last_in_slot:
                        act(Sfin[:, slot, :], Sf[cur], AF.Identity, ["Sf%d" % cur], ["Sfin"])
                dma("sp", ost_d[l, dr, h], Sfin, ["Sfin"], [("ost", l, dr, h)], ("ost",))
                for c in range(NCH):
                    mm1(o_ap(c * 32, 32), Sall[:, c, :], qb[:, c * 32:(c + 1) * 32], False, False,
                        ["Sall", "qb"], [("ps", pos[(c * 32) // 512])])
            for ti, (a, n) in enumerate(TT):
                act(osq[:, 0:n], PS[pos[ti]][:, 0:n], AF.Square, [("ps", pos[ti]), "d1"], ["d1"])
                pss = next_ps(exclude=pos)
                mm_group(PS[pss][:, 0:n], [(ones[:], osq[:, 0:n])], reads=["ones", "d1"], writes=[("ps", pss)])
                act(rstdh[:, 0:n], PS[pss][:, 0:n], AF.Ln, [("ps", pss), "d1"], ["d1"], scale=1.0 / 128.0, bias=RMS_EPS)
                act(rstdh[:, 0:n], rstdh[:, 0:n], AF.Exp, ["d1"], ["d1"], scale=-0.5)
                tt("dve", th[:, 0:n], PS[pos[ti]][:, 0:n], rstdh[:, 0:n], ALU.mult, [("ps", pos[ti]), "d1"], ["d1"])
                stt(MIX[:, 8 + h, a:a + n], th[:, 0:n], hgg[:, l:l + 1], gsil[:, a:a + n], ALU.mult, ALU.mult,
                    ["d1", "hgg", "gsil"], [("MIX", 8 + h)])
        assert wt[0] == N_IN_TILES

        if _STOP[0] == 6:
            raise _StopBuild()
        barrier(ATT_TEMPK + HG_TEMPK + XK)
        for k4 in range(4):
            dma("sp", X[:, 4 * k4:4 * k4 + 4, :], xsp_d[:, 4 * k4:4 * k4 + 4, :], ["xspill"], XK[4 * k4:4 * k4 + 4],
                ("xin", k4))
        for k4 in range(4):
            act(X[:, 4 * k4:4 * k4 + 4, :], X[:, 4 * k4:4 * k4 + 4, :], AF.Identity, XK[4 * k4:4 * k4 + 4],
                XK[4 * k4:4 * k4 + 4], scale=ALPHA)

        if _STOP[0] == 7:
            raise _StopBuild()
        def resid_accum(oc, ti, a, n, pbank, gate):
            j = 0 if ti < 2 else 1
            stt(X[:, oc, a:a + n], PS[pbank][:, 0:n], gate[:, oc, j:j + 1], X[:, oc, a:a + n], ALU.mult, ALU.add,
                [("ps", pbank), MK, ("X", oc)], [("X", oc)])

        for oc in range(KC):
            ri = load_w(wo_d[l][oc])
            for ti, (a, n) in enumerate(TT):
                pb = next_ps()
                mm_group(PS[pb][:, 0:n], [(RING[ri][:, k, :], MIX[:, k, a:a + n]) for k in range(KC)],
                         reads=[("ring", ri)] + MIXK, writes=[("ps", pb)])
                resid_accum(oc, ti, a, n, pb, g1)

        if _STOP[0] == 8:
            raise _StopBuild()
        def layer_norm(which, write_xm):
            for ti, (a, n) in enumerate(TT):
                j = 0 if ti < 2 else 1
                act(xb[:, :, 0:n], X[:, :, a:a + n], AF.Identity, XK, XBK)
                act(xsq[:, :, 0:n], X[:, :, a:a + n], AF.Square, XK, XSQK)
                p1 = next_ps()
                p2 = next_ps()
                mm_group(PS[p1][:, 0:n], [(ones[:], xb[:, k, 0:n]) for k in range(KC)], reads=["ones"] + XBK,
                         writes=[("ps", p1)])
                mm_group(PS[p2][:, 0:n], [(ones[:], xsq[:, k, 0:n]) for k in range(KC)], reads=["ones"] + XSQK,
                         writes=[("ps", p2)])
                act(mean[:, 0:n], PS[p1][:, 0:n], AF.Identity, [("ps", p1)], ["mean"], scale=1.0 / D)
                tt("dve", rtmp[:, 0:n], mean[:, 0:n], mean[:, 0:n], ALU.mult, ["mean"], ["rtmp"])
                stt(rstd[:, 0:n], PS[p2][:, 0:n], 1.0 / D, rtmp[:, 0:n], ALU.mult, ALU.subtract,
                    [("ps", p2), "rtmp"], ["rstd"])
                act(rstd[:, 0:n], rstd[:, 0:n], AF.Ln, ["rstd"], ["rstd"], bias=LN_EPS)
                act(rstd[:, 0:n], rstd[:, 0:n], AF.Exp, ["rstd"], ["rstd"], scale=-0.5)
                for k in range(KC):
                    tt("pool", X[:, k, a:a + n], X[:, k, a:a + n], mean[:, 0:n], ALU.subtract, [("X", k), "mean"],
                       [("X", k)])
                    tt("dve", X[:, k, a:a + n], X[:, k, a:a + n], rstd[:, 0:n], ALU.mult, [("X", k), "rstd"],
                       [("X", k)])
                    act(X[:, k, a:a + n], X[:, k, a:a + n], AF.Identity, [("X", k), "lng", "lnb"], [("X", k)],
                        scale=lng[:, l, which, k:k + 1], bias=lnb[:, l, which, k:k + 1])
                    if write_xm:
                        act(XM[:, k, a:a + n], X[:, k, a:a + n], AF.Identity, [("X", k), S2K, MK], [("XM", k)],
                            scale=sc2p[:, k, j:j + 1], bias=sh2[:, k, j:j + 1])

        layer_norm(0, True)
        for k4 in range(4):
            act(X[:, 4 * k4:4 * k4 + 4, :], X[:, 4 * k4:4 * k4 + 4, :], AF.Identity, XK[4 * k4:4 * k4 + 4],
                XK[4 * k4:4 * k4 + 4], scale=ALPHA)

        if _STOP[0] == 9:
            raise _StopBuild()
        nxt = (l + 1 < L)
        pmn = next_ps() if nxt else None
        excl = (pmn,) if nxt else ()
        mct = [0]

        def mods_step(nt):
            if not nxt:
                return
            for _ in range(nt):
                if mct[0] < 96:
                    mods_tile(l + 1, mct[0], pmn)
                    mct[0] += 1

        for grp in range(4):
            for jj in range(16):
                ri = load_w(wup_d[l][grp * 16 + jj])
                for ti, (a, n) in enumerate(TT):
                    pb = next_ps(exclude=excl)
                    project(ri, a, n, pb)
                    rl = tmp_relu[ti % 2]
                    act(rl[:, 0:n], PS[pb][:, 0:n], AF.Relu, [("ps", pb)], [("relu", ti % 2)])
                    act(HID[:, jj, a:a + n], rl[:, 0:n], AF.Square, [("relu", ti % 2)], [("MIX", jj)])
                mods_step(1)
            for oc in range(KC):
                ri = load_w(wdn_d[l][grp * 16 + oc])
                for ti, (a, n) in enumerate(TT):
                    pb = next_ps(exclude=excl)
                    mm_group(PS[pb][:, 0:n], [(RING[ri][:, k, :], HID[:, k, a:a + n]) for k in range(KC)],
                             reads=[("ring", ri)] + MIXK, writes=[("ps", pb)])
                    resid_accum(oc, ti, a, n, pb, g2)
                mods_step(1)
        if nxt:
            mods_step(96)
            mods_finish(l + 1, pmn)
        layer_norm(1, False)

    try:
        for l in range(L):
            layer(l)
    except _StopBuild:
        pass
    for k4 in range(4):
        dma("sp", yT_d[:, 4 * k4:4 * k4 + 4, :], X[:, 4 * k4:4 * k4 + 4, :], XK[4 * k4:4 * k4 + 4], [("yT", k4)],
            ("yout", k4))

    with nc.Block() as block:
        S.emit(nc, stack, block)
    stack.close()
    return nc


def _core_slots(core):
    if core < 2:
        return [("s", core, 256 * s) for s in range(4)] + [("p", 30 + core, 0)]
    return [("p", (core - 2) * 5 + s, 0) for s in range(5)]


def _fm(vec2d):
    a = np.asarray(vec2d, np.float32)
    n = a.shape[-1] // 128
    a = a.reshape(a.shape[:-1] + (n, 128))
    return np.ascontiguousarray(np.moveaxis(a, -1, 0))


def _retile(w):
    K, N = w.shape
    n = N // 128
    return np.ascontiguousarray(w.reshape(K // 128, 128, n, 128).transpose(2, 1, 0, 3)).reshape(n, 128, (K // 128) * 128)


def _win_cols():
    perm = np.arange(128) ^ 32
    ar = np.arange(128)
    cols = []
    for g in range(2):
        for r in range(4):
            h = 4 * g + r
            cols.append(h * 128 + ar)
            cols.append(h * 128 + perm)
        cols.append(1024 + g * 128 + ar)
        cols.append(1024 + g * 128 + perm)
        if g == 0:
            cols.append(1280 + ar)
            cols.append(1280 + 128 + ar)
    for h in range(8):
        for base in (1536, 2560, 3584, 4608, 5632):
            cols.append(base + h * 128 + ar)
    return np.concatenate(cols)


def _rope_tables(slots):
    cosT = np.ones((128, NTOK), np.float32)
    sinT = np.zeros((128, NTOK), np.float32)
    inv = (np.float32(10000.0) ** (-np.arange(32, dtype=np.float32) / np.float32(32))).astype(np.float32)
    for s, (kind, seq, start) in enumerate(slots):
        if kind != "s":
            continue
        t = start + np.arange(256)
        row = (t // 64).astype(np.float32)
        col = (t % 64).astype(np.float32)
        ang_r = (row[None, :] * inv[:, None]).astype(np.float32)
        ang_c = (col[None, :] * inv[:, None]).astype(np.float32)
        sl = slice(256 * s, 256 * (s + 1))
        cr, sr, cc, sc_ = np.cos(ang_r), np.sin(ang_r), np.cos(ang_c), np.sin(ang_c)
        cosT[0:32, sl] = cr
        cosT[32:64, sl] = cr
        cosT[64:96, sl] = cc
        cosT[96:128, sl] = cc
        sinT[0:32, sl] = -sr
        sinT[32:64, sl] = sr
        sinT[64:96, sl] = -sc_
        sinT[96:128, sl] = sc_
    return cosT, sinT


def _masks(slots):
    m = np.zeros((128, NBLK, 3, 128), np.float32)
    ctxflag = np.zeros((128, NBLK), np.float32)
    ar = np.arange(128)
    for B in range(NBLK):
        kind, seq, start = slots[B // 2]
        qpos = start + (B % 2) * 128 + ar
        if kind == "s":
            ctxflag[:, B] = 1.0
        for nb, KB in ((0, B - 1), (1, B), (2, B + 1)):
            if KB < 0 or KB >= NBLK:
                continue
            kkind, kseq, kstart = slots[KB // 2]
            if (kkind, kseq) != (kind, seq):
                continue
            kpos = kstart + (KB % 2) * 128 + ar
            if kind == "s":
                ok = np.abs(kpos[:, None] - qpos[None, :]) <= 128
            else:
                ok = np.ones((128, 128), bool)
            m[:, B, nb, :] = ok.astype(np.float32)
    cfl = np.full((128, 1), 1.0 if slots[0][0] == "s" else 0.0, np.float32)
    return m, ctxflag, cfl


def _const_mats():
    cm = np.zeros((4, 128, 128), np.float32)
    ar = np.arange(128)
    cm[0] = np.eye(128, dtype=np.float32)
    same = (ar[:, None] // 32) == (ar[None, :] // 32)
    cm[1] = (same & (ar[:, None] <= ar[None, :])).astype(np.float32)
    cm[2] = (same & (ar[:, None] >= ar[None, :])).astype(np.float32)
    rowmask = np.zeros((128, 4), np.float32)
    for c in range(4):
        rowmask[32 * c:32 * (c + 1), c] = 1.0
    return cm, rowmask


def _prepare(inp, L, cores):
    f = lambda k: np.asarray(inp[k], np.float32)
    w_in = f("w_in")
    cols = _win_cols()
    shared = {}
    for l in range(L):
        shared[f"wmod{l}"] = _retile(f("w_mod")[l])
        shared[f"win{l}"] = _retile(w_in[l][:, cols])
        shared[f"wo{l}"] = _retile(f("w_o")[l])
        shared[f"wup{l}"] = _retile(f("w_up")[l])
        shared[f"wdn{l}"] = np.ascontiguousarray(
            f("w_down")[l].reshape(4, 16, 128, 16, 128).transpose(0, 3, 2, 1, 4)).reshape(64, 128, 2048)
    shared.update({
        "bmod": _fm(f("b_mod")[:L]),
        "lng": _fm(f("ln_g")[:L]),
        "lnb": _fm(f("ln_b")[:L]),
        "attg": _fm(f("attn_norm_g")[:L]),
        "hgg": np.ascontiguousarray(f("hg_norm_g")[:L].T),
        "lbl": _fm(f("hg_lb_logits")),
        "sink": np.ascontiguousarray(np.broadcast_to(f("attn_sink")[:L][None], (128, L, 8))),
    })
    cm, rowmask = _const_mats()
    shared["cmats"] = cm
    shared["rowmask"] = rowmask
    xp, xs = f("x_prompt"), f("x_sample")
    maps = []
    for core in cores:
        slots = _core_slots(core)
        xs_ = []
        for kind, seq, start in slots:
            xs_.append(xs[seq, start:start + 256] if kind == "s" else xp[seq])
        xc = np.concatenate(xs_, 0)
        d = dict(shared)
        d["xT"] = np.ascontiguousarray(xc.reshape(NTOK, KC, 128).transpose(2, 1, 0))
        condA = f("c")[core] if core < 2 else f("c_ctx")
        condB = f("c_ctx")
        d["cond"] = _fm(np.stack([condA, condB]))
        d["cond"] = np.ascontiguousarray(d["cond"].transpose(0, 2, 1))
        cosT, sinT = _rope_tables(slots)
        d["cosT"], d["sinT"] = cosT, sinT
        m, ctxflag, cfl = _masks(slots)
        d["masks"], d["ctxflag"], d["cflag"] = m, ctxflag, cfl
        if core < 2:
            ck = f("cache_k")[core][:L]
            cv = f("cache_v")[core][:L]
            d["ckT"] = np.ascontiguousarray(ck.transpose(0, 3, 2, 1))
            d["cv"] = np.ascontiguousarray(cv.reshape(L, 4, 128, 256).transpose(0, 2, 1, 3))
            d["s0"] = np.ascontiguousarray(np.stack([f("state_hgrn_fwd")[core][:L], f("state_hgrn_bwd")[core][:L]], 1))
        else:
            d["ckT"] = np.zeros((L, 128, 2, 512), np.float32)
            d["cv"] = np.zeros((L, 128, 4, 256), np.float32)
            d["s0"] = np.zeros((L, 2, 8, 128, 128), np.float32)
        maps.append(d)
    return maps


_PROG = {}


def _run(inp, L=DEPTH, cores=tuple(range(8)), trace=False):
    if L not in _PROG:
        _PROG[L] = build_program(L)
    nc = _PROG[L]
    maps = _prepare(inp, L, cores)
    res = run_bass_kernel_spmd(nc, maps, core_ids=list(range(len(cores))), **({"trace": True} if trace else {}))
    return res


def _gather(results, L, cores):
    BATCH, SEQ, DEC_BATCH, DEC_SEQ = 32, 256, 2, 1024
    y_p = np.zeros((BATCH, SEQ, D), np.float32)
    y_s = np.zeros((DEC_BATCH, DEC_SEQ, D), np.float32)
    nk = np.zeros((BATCH, L, SEQ, 2, 128), np.float32)
    nv = np.zeros((BATCH, L, SEQ, 2, 128), np.float32)
    sf = np.zeros((BATCH, L, 8, 128, 128), np.float32)
    sbw = np.zeros((BATCH, L, 8, 128, 128), np.float32)
    for ci, core in enumerate(cores):
        r = results[ci]
        y = np.asarray(r["yT"]).transpose(2, 1, 0).reshape(NTOK, D)
        okT = np.asarray(r["okT"])
        ov = np.asarray(r["ov"])
        ost = np.asarray(r["ost"])
        for s, (kind, seq, start) in enumerate(_core_slots(core)):
            ys = y[256 * s:256 * (s + 1)]
            if kind == "s":
                y_s[seq, start:start + 256] = ys
            else:
                y_p[seq] = ys
                nk[seq] = okT[:, :, :, 256 * s:256 * (s + 1)].transpose(0, 3, 1, 2)
                nv[seq] = ov[:, 2 * s:2 * s + 2].reshape(L, 256, 2, 128)
                sf[seq] = ost[:, 0, :, :, s, :]
                sbw[seq] = ost[:, 1, :, :, s, :]
    return y_p, y_s, nk, nv, sf, sbw


def kernel(**inputs):
    res = _run(inputs, DEPTH, tuple(range(8)))
    return _gather(res.results, DEPTH, tuple(range(8)))
```

```python
from contextlib import ExitStack
import numpy as np
import concourse.bass as bass
import concourse.mybir as mybir
from concourse.bass_utils import run_bass_kernel_spmd

F32 = mybir.dt.float32
BF16 = mybir.dt.bfloat16
AF = mybir.ActivationFunctionType
ALU = mybir.AluOpType

D = 2048
KC = 16
NTOK = 1280
NBLK = 10
NCH = 40
NSLOT = 5
DEPTH = 4
ALPHA = (2 * DEPTH) ** 0.25
ATT_SCALE = 128 ** -0.5
LN_EPS = 1e-5
RMS_EPS = 1e-6
TT = [(0, 512), (512, 512), (1024, 256)]
NRING = 6
N_IN_TILES = 62
_STOP = [None]


class _StopBuild(Exception):
    pass


class _Op:
    __slots__ = ("eng", "fn", "deps", "sig", "sem", "tick", "dma", "ndma", "epoch", "waits", "vc")


class Sched:
    def __init__(self):
        self.ops = []
        self.lastw = {}
        self.readers = {}
        self.epoch = 0

    def add(self, eng, fn, reads=(), writes=(), dma=None, ndma=1):
        op = _Op()
        op.eng, op.fn, op.dma, op.ndma, op.sig = eng, fn, dma, ndma, False
        op.epoch = self.epoch
        deps = {}
        psr = [r for r in reads if isinstance(r, tuple) and r and r[0] == "ps"]
        if psr:
            reads = [r for r in reads if r not in psr]
            writes = list(writes) + [r for r in psr if r not in writes]

        def consider(d, kind):
            if d is None or d is op:
                return
            if d.dma is None and op.dma is None and d.eng == eng:
                if eng == "pe":
                    return
            deps[id(d)] = d

        for r in reads:
            consider(self.lastw.get(r), "raw")
        for w in writes:
            consider(self.lastw.get(w), "waw")
            for rd in self.readers.get(w, ()):
                consider(rd, "war")
        op.deps = list(deps.values())
        for d in op.deps:
            d.sig = True
        for r in reads:
            self.readers.setdefault(r, []).append(op)
        for w in writes:
            self.lastw[w] = op
            self.readers[w] = []
        self.ops.append(op)
        return op

    def emit(self, nc, stack, block):
        sems = {}

        def get_sem(key):
            if key not in sems:
                sems[key] = stack.enter_context(nc.semaphore("s_" + "_".join(str(k) for k in key)))
            return sems[key]

        counts = {}
        for op in self.ops:
            if op.dma is not None:
                key = ("d", op.dma)
                counts[key] = counts.get(key, 0) + 16 * op.ndma
            else:
                key = ("e", op.eng, op.epoch)
                if op.sig:
                    counts[key] = counts.get(key, 0) + 1
            op.sem = key
            op.tick = counts.get(key, 0)
        known = {}
        for op in self.ops:
            k = known.setdefault(op.eng, {})
            need = []
            for d in sorted(op.deps, key=lambda d_: -d_.tick):
                if k.get(d.sem, 0) < d.tick:
                    need.append((d.sem, d.tick))
                    k[d.sem] = d.tick
                    for s_, t_ in d.vc.items():
                        if k.get(s_, 0) < t_:
                            k[s_] = t_
            op.waits = need
            op.vc = dict(k) if (op.sig or op.dma is not None) else None
        by_eng = {}
        for op in self.ops:
            by_eng.setdefault(op.eng, []).append(op)
        for op in self.ops:
            if op.sig or op.dma is not None:
                get_sem(op.sem)

        def run(engname, e):
            for op in by_eng.get(engname, []):
                waits = op.waits
                for sem_, tick_ in waits[1:]:
                    e.wait_ge(get_sem(sem_), tick_)
                ins = op.fn(e)
                if isinstance(ins, tuple):
                    first, last = ins
                elif isinstance(ins, list):
                    first, last = ins[0], ins
                else:
                    first = last = ins
                if waits:
                    first.wait_op(get_sem(waits[0][0]), waits[0][1], "sem-ge")
                if op.dma is not None:
                    lst = last if isinstance(last, list) else [last]
                    assert len(lst) == op.ndma, (len(lst), op.ndma)
                    for i_ in lst:
                        i_.then_inc(get_sem(op.sem), 16)
                elif op.sig:
                    last.then_inc(get_sem(op.sem), 1)
            done = {}
            for op in by_eng.get(engname, []):
                if op.dma is not None:
                    done[op.sem] = max(done.get(op.sem, 0), op.tick)
            for sem_, tick_ in done.items():
                e.wait_ge(get_sem(sem_), tick_)

        @block.tensor
        def _(e):
            run("pe", e)

        @block.scalar
        def _(e):
            run("act", e)

        @block.vector
        def _(e):
            run("dve", e)

        @block.gpsimd
        def _(e):
            run("pool", e)

        @block.sync
        def _(e):
            run("sp", e)


def build_program(n_layers=DEPTH):
    L = n_layers
    nc = bass.Bass("TRN2", target_bir_lowering=False)

    def din(name, shape):
        return nc.dram_tensor(name, list(shape), F32, kind="ExternalInput").ap()

    def dout(name, shape):
        return nc.dram_tensor(name, list(shape), F32, kind="ExternalOutput").ap()

    xT_d = din("xT", [128, KC, NTOK])
    cond_d = din("cond", [128, KC, 2])
    bmod_d = din("bmod", [128, L, 96])
    lng_d = din("lng", [128, L, 2, KC])
    lnb_d = din("lnb", [128, L, 2, KC])
    attg_d = din("attg", [128, L, 8])
    hgg_d = din("hgg", [128, L])
    lbl_d = din("lbl", [128, 2, DEPTH, 8])
    sink_d = din("sink", [128, L, 8])
    cos_d = din("cosT", [128, NTOK])
    sin_d = din("sinT", [128, NTOK])
    msk_d = din("masks", [128, NBLK, 3, 128])
    ctxf_d = din("ctxflag", [128, NBLK])
    cfl_d = din("cflag", [128, 1])
    ckT_d = din("ckT", [L, 128, 2, 512])
    cv_d = din("cv", [L, 128, 4, 256])
    s0_d = din("s0", [L, 2, 8, 128, 128])
    cm_d = din("cmats", [4, 128, 128])
    rmask_d = din("rowmask", [128, 4])
    wmod_d = [din(f"wmod{l}", [96, 128, 2048]) for l in range(L)]
    win_d = [din(f"win{l}", [N_IN_TILES, 128, 2048]) for l in range(L)]
    wo_d = [din(f"wo{l}", [16, 128, 2048]) for l in range(L)]
    wup_d = [din(f"wup{l}", [64, 128, 2048]) for l in range(L)]
    wdn_d = [din(f"wdn{l}", [64, 128, 2048]) for l in range(L)]

    yT_d = dout("yT", [128, KC, NTOK])
    okT_d = dout("okT", [L, 2, 128, NTOK])
    ov_d = dout("ov", [L, NBLK, 128, 256])
    ost_d = dout("ost", [L, 2, 8, 128, NSLOT, 128])
    xsp_d = nc.dram_tensor("xspill", [128, KC, NTOK], F32, kind="Internal").ap()

    S = Sched()
    stack = ExitStack()

    def sb(name, shape, dt):
        return stack.enter_context(nc.sbuf_tensor(name, list(shape), dt))

    RA = sb("RA", [128, KC * NTOK], F32)
    XM = sb("XM", [128, KC, NTOK], BF16)
    RC = sb("RC", [128, KC * NTOK], BF16)
    RING = [sb(f"ring{i}", [128, KC, 128], BF16) for i in range(NRING)]
    X = RA[:].rearrange("p (k t) -> p k t", k=KC)
    MIX = RC[:].rearrange("p (k t) -> p k t", k=KC)
    HID = MIX

    cond_f = sb("cond_f", [128, KC, 2], F32)
    condS = sb("condS", [128, KC, 2], BF16)
    bmod = sb("bmod_s", [128, L, 96], F32)
    lng = sb("lng_s", [128, L, 2, KC], F32)
    lnb = sb("lnb_s", [128, L, 2, KC], F32)
    attg = sb("attg_s", [128, L, 8], F32)
    hgg = sb("hgg_s", [128, L], F32)
    lbl = sb("lbl_s", [128, 2, DEPTH, 8], F32)
    lbv = sb("lbv", [128, 2, DEPTH, 8], F32)
    oml = sb("oml", [128, 2, DEPTH, 8], F32)
    lbsum = sb("lbsum", [128, 2, 8], F32)
    esink = sb("esink", [128, L, 8], F32)
    ctxf = sb("ctxf", [128, NBLK], F32)
    cfl = sb("cfl", [128, 1], F32)
    rmask = sb("rmask", [128, 4], F32)
    cmats = sb("cmats_s", [128, 3, 128], BF16)
    ones = sb("ones", [128, 128], BF16)
    rmaskF = sb("rmaskF", [128, 4, 128], BF16)
    scanm = sb("scanm", [128, NCH, 32], BF16)
    mods2 = [sb("mods0", [128, 96, 2], F32), sb("mods1", [128, 96, 2], F32)]
    sc1p2 = [sb("sc1p0", [128, KC, 2], F32), sb("sc1p1", [128, KC, 2], F32)]
    sc2p2 = [sb("sc2p0", [128, KC, 2], F32), sb("sc2p1", [128, KC, 2], F32)]
    dummy = sb("dummy_t", [128, 8], F32)

    PS = [stack.enter_context(nc.psum_tensor(f"ps{i}", [128, 512], F32)) for i in range(8)]

    ra_off = [0]

    def ra_f32(nwords):
        o = ra_off[0]
        ra_off[0] += nwords
        assert ra_off[0] <= KC * NTOK, ra_off[0]
        return RA[:, o:o + nwords]

    def ra_bf16(nel):
        assert nel % 2 == 0
        return ra_f32(nel // 2).bitcast(BF16)

    cosT = ra_f32(NTOK)
    sinT = ra_f32(NTOK)
    masks = ra_bf16(NBLK * 3 * 128).rearrange("p (b n q) -> p b n q", b=NBLK, n=3)
    ckT = ra_bf16(2 * 512).rearrange("p (g t) -> p g t", g=2)
    cvt = ra_bf16(4 * 256).rearrange("p (b c) -> p b c", b=4)
    qT = ra_bf16(4 * NTOK).rearrange("p (r t) -> p r t", r=4)
    kT = ra_bf16(NTOK)
    vtok = ra_bf16(NBLK * 256).rearrange("p (b c) -> p b c", b=NBLK)
    kraw = ra_f32(NTOK)
    vraw = [ra_f32(256), ra_f32(256)]
    tmpA = [ra_f32(512), ra_f32(512)]
    tmpB = [ra_f32(512), ra_f32(512)]
    eT = [ra_bf16(512) for _ in range(4)]
    attraw1 = ra_f32(4 * NTOK).rearrange("p (r t) -> p r t", r=4)
    attraw0 = RC[:, 8 * NTOK:16 * NTOK].bitcast(F32).rearrange("p (r t) -> p r t", r=4)
    attraw = [attraw0, attraw1]
    ATTK = [[("MIX", k) for k in range(8, 16)], ["attraw1"]]
    ra_off[0] = 0
    qf = ra_f32(NTOK)
    sigf = [ra_f32(NTOK), ra_f32(NTOK)]
    gsil = ra_f32(NTOK)
    vtokH = ra_bf16(NBLK * 128).rearrange("p (b c) -> p b c", b=NBLK)
    kk = ra_f32(NTOK)
    E1 = ra_f32(NTOK)
    logf = E1
    bc = ra_f32(NTOK)
    d1 = ra_f32(NTOK)
    qmid = ra_bf16(NTOK)
    kmid = ra_bf16(NTOK)
    vTh = kmid
    klast = ra_bf16(NTOK)
    qb = ra_bf16(NTOK)
    decay = ra_f32(NCH)
    klastM = ra_bf16(NCH * 128).rearrange("p (c k) -> p c k", c=NCH)
    aT = [ra_bf16(128), ra_bf16(128)]
    Sall = ra_bf16(NCH * 128).rearrange("p (c v) -> p c v", c=NCH)
    Sf = [ra_f32(128), ra_f32(128)]
    Sfin = ra_f32(NSLOT * 128).rearrange("p (s v) -> p s v", s=NSLOT)
    S0t = ra_f32(128)
    osq = d1[:, 0:256].bitcast(BF16)
    rstdh = d1[:, 256:768]
    th = d1[:, 768:1280]
    xb = RC[:, 0:KC * 512].rearrange("p (k t) -> p k t", k=KC)
    xsq = RC[:, 7 * NTOK:7 * NTOK + KC * 512].rearrange("p (k t) -> p k t", k=KC)
    XBK = [("MIX", k) for k in range(0, 7)]
    XSQK = [("MIX", k) for k in range(7, 14)]
    mean = sb("ln_mean", [128, 512], F32)
    rstd = sb("ln_rstd", [128, 512], F32)
    rtmp = sb("ln_tmp", [128, 512], F32)
    tmp_relu = [sb("relu0", [128, 512], F32), sb("relu1", [128, 512], F32)]

    ps_rr = [0]

    def next_ps(exclude=()):
        while True:
            i = ps_rr[0] % 8
            ps_rr[0] += 1
            if i not in exclude:
                return i

    ring_rr = [0]

    def load_w(src_ap):
        i = ring_rr[0] % NRING
        ring_rr[0] += 1
        dst = RING[i]
        S.add("pool", lambda e, dst=dst, src=src_ap: e.dma_start(
            out=dst[:].rearrange("p k c -> p (k c)"), in_=src),
            writes=[("ring", i)], dma=("ring", i))
        return i

    def mm_group(out_ap, pairs, reads, writes):
        def fn(e, out_ap=out_ap, pairs=pairs):
            ins = first = None
            n = len(pairs)
            for j, (l_, r_) in enumerate(pairs):
                ins = e.matmul(out_ap, l_, r_, start=(j == 0), stop=(j == n - 1))
                if first is None:
                    first = ins
            return (first, ins)
        return S.add("pe", fn, reads=reads, writes=writes)

    def mm1(out_ap, l_, r_, start, stop, reads, writes):
        return S.add("pe", lambda e: e.matmul(out_ap, l_, r_, start=start, stop=stop, skip_group_check=True),
                     reads=reads, writes=writes)

    def act(out, in_, func, reads, writes, scale=None, bias=None):
        kw = {}
        if scale is not None:
            kw["scale"] = scale
        if bias is not None:
            kw["bias"] = bias
        return S.add("act", lambda e: e.activation(out=out, in_=in_, func=func, **kw), reads=reads, writes=writes)

    def tt(eng, out, in0, in1, op, reads, writes):
        return S.add(eng, lambda e: e.tensor_tensor(out=out, in0=in0, in1=in1, op=op), reads=reads, writes=writes)

    def ts(eng, out, in0, s1, s2, op0, op1, reads, writes):
        if op1 is None:
            return S.add(eng, lambda e: e.tensor_scalar(out=out, in0=in0, scalar1=s1, scalar2=None, op0=op0),
                         reads=reads, writes=writes)
        return S.add(eng, lambda e: e.tensor_scalar(out=out, in0=in0, scalar1=s1, scalar2=s2, op0=op0, op1=op1),
                     reads=reads, writes=writes)

    def stt(out, in0, scalar, in1, op0, op1, reads, writes):
        return S.add("dve", lambda e: e.scalar_tensor_tensor(out=out, in0=in0, scalar=scalar, in1=in1,
                                                             op0=op0, op1=op1), reads=reads, writes=writes)

    def vcopy(out, in_, reads, writes):
        return S.add("dve", lambda e: e.tensor_copy(out=out, in_=in_), reads=reads, writes=writes)

    def vmemset(ap, val, writes):
        return S.add("dve", lambda e: e.memset(ap, val), writes=writes)

    def dma(q, out, in_, reads, writes, key):
        return S.add(q, lambda e: e.dma_start(out=out, in_=in_), reads=reads, writes=writes, dma=key)

    def barrier(keys):
        return S.add("dve", lambda e: e.memset(dummy[:, 0:1], 0.0), writes=list(keys))

    XK = [("X", k) for k in range(KC)]
    XMK = [("XM", k) for k in range(KC)]
    MIXK = [("MIX", k) for k in range(KC)]
    ATT_TEMPK = (["cosT", "sinT", "masks", "ckT", "cvt", "kT", "kraw", "vraw0", "vraw1", "attraw1"] +
                 [("qT", r) for r in range(4)] + [("vtok", B) for B in range(NBLK)] +
                 [("tmpA", 0), ("tmpA", 1), ("tmpB", 0), ("tmpB", 1)] + [("eT", i) for i in range(4)])
    HG_TEMPK = ["qf", "sig0", "sig1", "gsil", "vtokH", "kk", "E1", "bc", "d1", "qmid", "kmid", "klast", "qb",
                "decay", "klastM", "aT0", "aT1", "Sall", "Sf0", "Sf1", "Sfin", "S0t"]

    for k4 in range(4):
        dma("sp", X[:, 4 * k4:4 * k4 + 4, :], xT_d[:, 4 * k4:4 * k4 + 4, :], [], XK[4 * k4:4 * k4 + 4], ("xin", k4))
    small = [(cond_f, cond_d, "cond"), (bmod, bmod_d, "bmod"), (lng, lng_d, "lng"), (lnb, lnb_d, "lnb"),
             (attg, attg_d, "attg"), (hgg, hgg_d, "hgg"), (lbl, lbl_d, "lbl"), (esink, sink_d, "esink"),
             (ctxf, ctxf_d, "ctxf"), (cfl, cfl_d, "cfl"), (rmask, rmask_d, "rmask")]
    for t_, d_, nm in small:
        dma("sp", t_[:], d_, [], [nm], ("c", nm))
    dma("pool", cmats[:], cm_d[0:3].rearrange("m p c -> p m c"), [], ["cmats"], ("c", "cmats"))
    vmemset(ones[:], 1.0, ["ones"])
    vcopy(rmaskF[:], rmask[:].unsqueeze(2).to_broadcast([128, 4, 128]), ["rmask"], ["rmaskF"])
    vmemset(scanm[:], 1.0, ["scanm"])
    vmemset(scanm[:, :, 0:1], 0.0, ["scanm"])
    act(condS[:], cond_f[:], AF.Silu, ["cond"], ["condS"])
    act(esink[:], esink[:], AF.Exp, ["esink"], ["esink"])
    act(lbl[:], lbl[:], AF.Exp, ["lbl"], ["lbl"])
    tt("dve", lbsum[:], lbl[:, :, 0, :], lbl[:, :, 1, :], ALU.add, ["lbl"], ["lbsum"])
    for l_ in range(2, DEPTH):
        tt("dve", lbsum[:], lbsum[:], lbl[:, :, l_, :], ALU.add, ["lbl", "lbsum"], ["lbsum"])
    S.add("dve", lambda e: e.reciprocal(out=lbsum[:], in_=lbsum[:]), reads=["lbsum"], writes=["lbsum"])
    vmemset(lbv[:, :, 0, :], 0.0, ["lbv"])
    for l_ in range(1, DEPTH):
        tt("dve", lbv[:, :, l_, :], lbl[:, :, l_, :], lbsum[:], ALU.mult, ["lbl", "lbsum", "lbv"], ["lbv"])
        if l_ > 1:
            tt("dve", lbv[:, :, l_, :], lbv[:, :, l_, :], lbv[:, :, l_ - 1, :], ALU.add, ["lbv"], ["lbv"])
    ts("dve", oml[:], lbv[:], -1.0, 1.0, ALU.mult, ALU.add, ["lbv"], ["oml"])

    def r4(ap):
        return ap.rearrange("p (r q) -> p r q", r=4)

    def c32(ap):
        return ap.rearrange("p (c s) -> p c s", s=32)

    def mods_tile(l, ct, pm):
        ri = load_w(wmod_d[l][ct])
        mm_group(PS[pm][:, 2 * ct:2 * ct + 2],
                 [(RING[ri][:, k, :], condS[:, k, :]) for k in range(KC)],
                 reads=[("ring", ri), "condS"], writes=[("ps", pm)])

    def mods_finish(l, pm):
        p_ = l % 2
        tt("dve", mods2[p_][:], PS[pm][:, 0:192].rearrange("p (c j) -> p c j", j=2),
           bmod[:, l, :].unsqueeze(2).to_broadcast([128, 96, 2]), ALU.add,
           [("ps", pm), "bmod"], [("mods", p_)])
        ts("dve", sc1p2[p_][:], mods2[p_][:, 16:32, :], 1.0, None, ALU.add, None, [("mods", p_)], [("sc1p", p_)])
        ts("dve", sc2p2[p_][:], mods2[p_][:, 64:80, :], 1.0, None, ALU.add, None, [("mods", p_)], [("sc2p", p_)])

    def layer(l):
        S.epoch = l
        if l == 0:
            pm = next_ps()
            for ct in range(96):
                mods_tile(0, ct, pm)
            mods_finish(0, pm)
        mods, sc1p, sc2p = mods2[l % 2], sc1p2[l % 2], sc2p2[l % 2]
        MK, S1K, S2K = ("mods", l % 2), ("sc1p", l % 2), ("sc2p", l % 2)
        sh1 = mods[:, 0:16, :]
        g1 = mods[:, 32:48, :]
        sh2 = mods[:, 48:64, :]
        g2 = mods[:, 80:96, :]

        if _STOP[0] == 1:
            raise _StopBuild()
        for k in range(KC):
            for j, (a, b) in enumerate([(0, 1024), (1024, 1280)]):
                if (k + j) % 2 == 0:
                    ts("dve", XM[:, k, a:b], X[:, k, a:b], sc1p[:, k, j:j + 1], sh1[:, k, j:j + 1], ALU.mult, ALU.add,
                       [("X", k), S1K, MK], [("XM", k)])
                else:
                    act(XM[:, k, a:b], X[:, k, a:b], AF.Identity, [("X", k), S1K, MK], [("XM", k)],
                        scale=sc1p[:, k, j:j + 1], bias=sh1[:, k, j:j + 1])
        if _STOP[0] == 2:
            raise _StopBuild()
        for k4 in range(4):
            dma("sp", xsp_d[:, 4 * k4:4 * k4 + 4, :], X[:, 4 * k4:4 * k4 + 4, :], XK[4 * k4:4 * k4 + 4], ["xspill"],
                ("xsp", k4))
        barrier(XK + ATT_TEMPK + HG_TEMPK)
        dma("sp", cosT, cos_d, [], ["cosT"], ("c", "cosT"))
        dma("sp", sinT, sin_d, [], ["sinT"], ("c", "sinT"))
        dma("pool", masks, msk_d, [], ["masks"], ("c", "masks"))
        dma("pool", ckT, ckT_d[l], [], ["ckT"], ("c", "ckT"))
        dma("pool", cvt, cv_d[l], [], ["cvt"], ("c", "cvt"))

        if _STOP[0] == 3:
            raise _StopBuild()
        wt = [0]

        def project(ri, a, n, pbank):
            mm_group(PS[pbank][:, 0:n], [(RING[ri][:, k, :], XM[:, k, a:a + n]) for k in range(KC)],
                     reads=[("ring", ri)] + XMK, writes=[("ps", pbank)])

        def rope_pair(dst_fn, keyname, raw_out=None):
            r0 = load_w(win_d[l][wt[0]])
            r1 = load_w(win_d[l][wt[0] + 1])
            wt[0] += 2
            for ti, (a, n) in enumerate(TT):
                p0 = next_ps()
                p1 = next_ps()
                project(r0, a, n, p0)
                project(r1, a, n, p1)
                tA = tmpA[ti % 2]
                tB = tmpB[ti % 2]
                if raw_out is not None:
                    vcopy(raw_out[:, a:a + n], PS[p0][:, 0:n], [("ps", p0)], ["kraw"])
                tt("dve", tA[:, 0:n], PS[p0][:, 0:n], cosT[:, a:a + n], ALU.mult, [("ps", p0), "cosT"], [("tmpA", ti % 2)])
                tt("dve", tB[:, 0:n], PS[p1][:, 0:n], sinT[:, a:a + n], ALU.mult, [("ps", p1), "sinT"], [("tmpB", ti % 2)])
                tt("pool", dst_fn(a, n), tA[:, 0:n], tB[:, 0:n], ALU.add, [("tmpA", ti % 2), ("tmpB", ti % 2)], [keyname])

        for g in range(2):
            if _STOP[0] == 34 and g == 1:
                raise _StopBuild()
            for r in range(4):
                rope_pair(lambda a, n, r=r: qT[:, r, a:a + n], ("qT", r))
            if _STOP[0] == 30:
                raise _StopBuild()
            rope_pair(lambda a, n: kT[:, a:a + n], "kT", raw_out=kraw)
            if _STOP[0] == 31:
                raise _StopBuild()
            dma("sp", okT_d[l, g], kraw, ["kraw"], [("okT", l, g)], ("okT",))
            if _STOP[0] == 32:
                raise _StopBuild()
            if g == 0:
                rv = [load_w(win_d[l][wt[0]]), load_w(win_d[l][wt[0] + 1])]
                wt[0] += 2
                for B in range(NBLK):
                    pv = next_ps()
                    for vg in range(2):
                        mm_group(PS[pv][:, vg * 128:(vg + 1) * 128],
                                 [(XM[:, k, B * 128:(B + 1) * 128], RING[rv[vg]][:, k, :]) for k in range(KC)],
                                 reads=[("ring", rv[vg])] + XMK, writes=[("ps", pv)])
                    act(vtok[:, B, :], PS[pv][:, 0:256], AF.Identity, [("ps", pv)], [("vtok", B)])
                    vcopy(vraw[B % 2], PS[pv][:, 0:256], [("ps", pv)], ["vraw%d" % (B % 2)])
                    dma("sp", ov_d[l, B], vraw[B % 2], ["vraw%d" % (B % 2)], [("ov", l, B)], ("ov", B % 2))
            if _STOP[0] == 33:
                raise _StopBuild()
            for B in range(NBLK):
                kbs = []
                if B - 1 >= 0:
                    kbs.append(("loc", B - 1, 0))
                kbs.append(("loc", B, None))
                if B + 1 < NBLK:
                    kbs.append(("loc", B + 1, 2))
                for j in range(4):
                    kbs.append(("ctx", j, None))
                po = next_ps()
                pd = next_ps(exclude=(po,))
                qrhs = qT[:, :, B * 128:(B + 1) * 128]
                qkeys = [("qT", r) for r in range(4)]
                nkb = len(kbs)
                sbank = {}

                def issue_s(i):
                    kind, j, mi = kbs[i]
                    pb = next_ps(exclude=(po, pd))
                    sbank[i] = pb
                    if kind == "loc":
                        lhs = kT[:, j * 128:(j + 1) * 128]
                        rd = ["kT"]
                    else:
                        lhs = ckT[:, g, j * 128:(j + 1) * 128]
                        rd = ["ckT"]
                    mm_group(r4(PS[pb][:]), [(lhs, qrhs)], reads=rd + qkeys, writes=[("ps", pb)])

                issue_s(0)
                for i in range(nkb):
                    if i + 1 < nkb:
                        issue_s(i + 1)
                    kind, j, mi = kbs[i]
                    pb = sbank[i]
                    et = eT[i % 4]
                    ek = ("eT", i % 4)
                    act(et, PS[pb][:], AF.Exp, [("ps", pb)], [ek], scale=ATT_SCALE)
                    if kind == "loc" and mi is not None:
                        tt("dve", r4(et), r4(et), masks[:, B, mi, :].unsqueeze(1).to_broadcast([128, 4, 128]),
                           ALU.mult, [ek, "masks"], [ek])
                    if kind == "ctx":
                        act(et, et, AF.Identity, [ek, "ctxf"], [ek], scale=ctxf[:, B:B + 1])
                    if kind == "loc":
                        vl = vtok[:, j, g * 128:(g + 1) * 128]
                        vrd = [("vtok", j)]
                    else:
                        vl = cvt[:, j, g * 128:(g + 1) * 128]
                        vrd = ["cvt"]
                    mm1(PS[po][:], vl, et, (i == 0), (i == nkb - 1), vrd + [ek], [("ps", po)])
                    mm1(PS[pd][:], ones[:], et, (i == 0), (i == nkb - 1), ["ones", ek], [("ps", pd)])
                tA = tmpA[B % 2]
                tk = ("tmpA", B % 2)
                tt("dve", r4(tA), r4(PS[pd][:]),
                   esink[:, l, 4 * g:4 * g + 4].unsqueeze(2).to_broadcast([128, 4, 128]), ALU.add,
                   [("ps", pd), "esink"], [tk])
                S.add("dve", lambda e, tA=tA: e.reciprocal(out=tA, in_=tA), reads=[tk], writes=[tk])
                tt("dve", attraw[g][:, :, B * 128:(B + 1) * 128], r4(PS[po][:]), r4(tA), ALU.mult,
                   [("ps", po), tk], ATTK[g])
        if _STOP[0] == 4:
            raise _StopBuild()
        for ti, (a, n) in enumerate(TT):
            pss = next_ps()
            for h in range(8):
                o_ = eT[h % 4]
                act(o_[:, 0:n], attraw[h // 4][:, h % 4, a:a + n], AF.Square, ATTK[h // 4], [("eT", h % 4)])
                mm1(PS[pss][:, 0:n], ones[:], o_[:, 0:n], (h == 0), (h == 7), ["ones", ("eT", h % 4)], [("ps", pss)])
            tA = tmpA[ti % 2]
            tk = ("tmpA", ti % 2)
            act(tA[:, 0:n], PS[pss][:, 0:n], AF.Ln, [("ps", pss)], [tk], scale=1.0 / 1024.0, bias=RMS_EPS)
            act(tA[:, 0:n], tA[:, 0:n], AF.Exp, [tk], [tk], scale=-0.5)
            for h in range(8):
                stt(MIX[:, h, a:a + n], attraw[h // 4][:, h % 4, a:a + n], attg[:, l, h:h + 1], tA[:, 0:n],
                    ALU.mult, ALU.mult, ATTK[h // 4] + ["attg", tk], [("MIX", h)])

        if _STOP[0] == 5:
            raise _StopBuild()
        barrier(ATT_TEMPK + HG_TEMPK)
        ident = cmats[:, 0, :]
        for h in range(8):
            rts = [load_w(win_d[l][wt[0] + i]) for i in range(5)]
            wt[0] += 5
            for ti, (a, n) in enumerate(TT):
                pb = [next_ps() for _ in range(5)]
                for i in range(5):
                    project(rts[i], a, n, pb[i])
                act(qf[:, a:a + n], PS[pb[0]][:, 0:n], AF.Silu, [("ps", pb[0])], ["qf"])
                act(sigf[0][:, a:a + n], PS[pb[1]][:, 0:n], AF.Sigmoid, [("ps", pb[1])], ["sig0"])
                act(sigf[1][:, a:a + n], PS[pb[2]][:, 0:n], AF.Sigmoid, [("ps", pb[2])], ["sig1"])
                act(vTh[:, a:a + n], PS[pb[3]][:, 0:n], AF.Identity, [("ps", pb[3])], ["kmid"])
                act(gsil[:, a:a + n], PS[pb[4]][:, 0:n], AF.Silu, [("ps", pb[4])], ["gsil"])
            for B in range(NBLK):
                pt = next_ps()
                ptv = PS[pt][:].bitcast(BF16)[:, 0:128]
                S.add("pe", lambda e, ptv=ptv, B=B: e.transpose(ptv, vTh[:, B * 128:(B + 1) * 128], ident),
                      reads=["kmid", "cmats"], writes=[("ps", pt)])
                act(vtokH[:, B, :], ptv, AF.Identity, [("ps", pt)], ["vtokH"])
            pos = [next_ps() for _ in range(3)]
            for i in range(3):
                vmemset(PS[pos[i]][:], 0.0, [("ps", pos[i])])

            def o_ap(t0, n):
                return PS[pos[t0 // 512]][:, t0 % 512:t0 % 512 + n]

            def prepA_ops(dr):
                fwd = (dr == 0)
                mid = 15 if fwd else 16
                lastp = 31 if fwd else 0
                bmid = c32(bc)[:, :, mid:mid + 1].to_broadcast([128, NCH, 32])
                blast = c32(bc)[:, :, lastp:lastp + 1].to_broadcast([128, NCH, 32])
                ops = []
                ops.append(lambda: act(kk, sigf[dr], AF.Identity, ["sig%d" % dr, "oml", "lbv"], ["kk"],
                                       scale=oml[:, dr, l, h:h + 1], bias=lbv[:, dr, l, h:h + 1]))
                ops.append(lambda: act(logf, kk, AF.Ln, ["kk"], ["E1"]))
                ops.append(lambda: act(kk, kk, AF.Identity, ["kk", "E1"], ["kk"], scale=-1.0, bias=1.0))
                ops.append(lambda: S.add("dve", lambda e: e.tensor_tensor_scan(
                    out=bc, data0=scanm[:].rearrange("p c s -> p (c s)"), data1=logf, initial=0.0,
                    op0=ALU.mult, op1=ALU.add), reads=["scanm", "E1"], writes=["bc"]))
                if not fwd:
                    ops.append(lambda: tt("dve", d1, logf, bc, ALU.subtract, ["E1", "bc"], ["d1"]))
                    ops.append(lambda: tt("dve", c32(bc), c32(d1), c32(bc)[:, :, 31:32].to_broadcast([128, NCH, 32]),
                                          ALU.add, ["d1", "bc"], ["bc"]))
                ops.append(lambda: tt("dve", c32(d1), c32(bc), bmid, ALU.subtract, ["bc"], ["d1"]))
                ops.append(lambda: act(E1, d1, AF.Exp, ["d1"], ["E1"]))
                ops.append(lambda: tt("dve", qmid, qf, E1, ALU.mult, ["qf", "E1"], ["qmid"]))
                ops.append(lambda: act(E1, d1, AF.Exp, ["d1", "qmid"], ["E1"], scale=-1.0))
                ops.append(lambda: tt("dve", kmid, kk, E1, ALU.mult, ["kk", "E1"], ["kmid"]))
                ops.append(lambda: tt("dve", c32(d1), blast, c32(bc), ALU.subtract, ["bc", "kmid"], ["d1"]))
                ops.append(lambda: act(E1, d1, AF.Exp, ["d1"], ["E1"]))
                ops.append(lambda: tt("dve", klast, kk, E1, ALU.mult, ["kk", "E1"], ["klast"]))
                return ops

            def prepB(dr):
                fwd = (dr == 0)
                lastp = 31 if fwd else 0
                act(E1, bc, AF.Exp, ["bc", "klast"], ["E1"])
                tt("dve", qb, qf, E1, ALU.mult, ["qf", "E1"], ["qb"])
                act(decay.unsqueeze(2), c32(bc)[:, :, lastp:lastp + 1], AF.Exp, ["bc"], ["decay"])
                for B in range(NBLK):
                    pt = next_ps(exclude=pos)
                    ptv = PS[pt][:].bitcast(BF16)[:, 0:128]
                    S.add("pe", lambda e, ptv=ptv, B=B: e.transpose(ptv, klast[:, B * 128:(B + 1) * 128], ident),
                          reads=["klast", "cmats"], writes=[("ps", pt)])
                    tt("dve", klastM[:, 4 * B:4 * B + 4, :], ptv.unsqueeze(1).to_broadcast([128, 4, 128]), rmaskF[:],
                       ALU.mult, [("ps", pt), "rmaskF"], ["klastM"])
                mk = cmats[:, 1 if fwd else 2, :]
                for B in range(NBLK):
                    pa = next_ps(exclude=pos)
                    mm_group(PS[pa][:, 0:128], [(kmid[:, B * 128:(B + 1) * 128], qmid[:, B * 128:(B + 1) * 128])],
                             reads=["kmid", "qmid"], writes=[("ps", pa)])
                    tt("dve", aT[B % 2], PS[pa][:, 0:128], mk, ALU.mult, [("ps", pa), "cmats"], ["aT%d" % (B % 2)])
                    mm1(o_ap(B * 128, 128), vtokH[:, B, :], aT[B % 2], False, False,
                        ["vtokH", "aT%d" % (B % 2)], [("ps", pos[(B * 128) // 512])])

            def chain(dr, filler):
                fwd = (dr == 0)
                filler = list(filler)
                order = list(range(NCH)) if fwd else list(range(NCH - 1, -1, -1))
                cur = 0
                for idx, c in enumerate(order):
                    slot = c // 8
                    first_in_slot = (c % 8 == 0) if fwd else (c % 8 == 7)
                    if first_in_slot:
                        nxt = Sf[(cur + 1) % 2]
                        nk = "Sf%d" % ((cur + 1) % 2)
                        if slot == 4:
                            vmemset(nxt, 0.0, [nk])
                        elif (fwd and slot == 0) or ((not fwd) and slot == 3):
                            dma("sp", S0t, s0_d[l, dr, h], [], ["S0t"], ("c", "S0t"))
                            vcopy(nxt, S0t, ["S0t"], [nk])
                        else:
                            ts("dve", nxt, Sf[cur], cfl[:, 0:1], None, ALU.mult, None, ["Sf%d" % cur, "cfl"], [nk])
                        cur = (cur + 1) % 2
                    act(Sall[:, c, :], Sf[cur], AF.Identity, ["Sf%d" % cur], ["Sall"])
                    pu = next_ps(exclude=pos)
                    mm_group(PS[pu][:, 0:128], [(klastM[:, c, :], vtokH[:, c // 4, :])], reads=["klastM", "vtokH"],
                             writes=[("ps", pu)])
                    nxt = Sf[(cur + 1) % 2]
                    nk = "Sf%d" % ((cur + 1) % 2)
                    stt(nxt, Sf[cur], decay[:, c:c + 1], PS[pu][:, 0:128], ALU.mult, ALU.add,
                        ["Sf%d" % cur, "decay", ("ps", pu)], [nk])
                    cur = (cur + 1) % 2
                    last_in_slot = (c % 8 == 7) if fwd else (c % 8 == 0)
                    if last_in_slot:
                        act(Sfin[:, slot, :], Sf[cur], AF.Identity, ["Sf%d" % cur], ["Sfin"])
                    if filler and idx % 2 == 1:
                        filler.pop(0)()
                while filler:
                    filler.pop(0)()
                dma("sp", ost_d[l, dr, h], Sfin, ["Sfin"], [("ost", l, dr, h)], ("ost",))

            def inter(dr):
                for c in range(NCH):
                    mm1(o_ap(c * 32, 32), Sall[:, c, :], qb[:, c * 32:(c + 1) * 32], False, False,
                        ["Sall", "qb"], [("ps", pos[(c * 32) // 512])])

            for t_ in prepA_ops(0):
                t_()
            prepB(0)
            chain(0, prepA_ops(1))
            inter(0)
            prepB(1)
            chain(1, [])
            inter(1)
            for ti, (a, n) in enumerate(TT):
                act(osq[:, 0:n], PS[pos[ti]][:, 0:n], AF.Square, [("ps", pos[ti]), "d1"], ["d1"])
                pss = next_ps(exclude=pos)
                mm_group(PS[pss][:, 0:n], [(ones[:], osq[:, 0:n])], reads=["ones", "d1"], writes=[("ps", pss)])
                act(rstdh[:, 0:n], PS[pss][:, 0:n], AF.Ln, [("ps", pss), "d1"], ["d1"], scale=1.0 / 128.0, bias=RMS_EPS)
                act(rstdh[:, 0:n], rstdh[:, 0:n], AF.Exp, ["d1"], ["d1"], scale=-0.5)
                tt("dve", th[:, 0:n], PS[pos[ti]][:, 0:n], rstdh[:, 0:n], ALU.mult, [("ps", pos[ti]), "d1"], ["d1"])
                stt(MIX[:, 8 + h, a:a + n], th[:, 0:n], hgg[:, l:l + 1], gsil[:, a:a + n], ALU.mult, ALU.mult,
                    ["d1", "hgg", "gsil"], [("MIX", 8 + h)])
        assert wt[0] == N_IN_TILES

        if _STOP[0] == 6:
            raise _StopBuild()
        barrier(ATT_TEMPK + HG_TEMPK + XK)
        for k4 in range(4):
            dma("sp", X[:, 4 * k4:4 * k4 + 4, :], xsp_d[:, 4 * k4:4 * k4 + 4, :], ["xspill"], XK[4 * k4:4 * k4 + 4],
                ("xin", k4))
        for k4 in range(4):
            act(X[:, 4 * k4:4 * k4 + 4, :], X[:, 4 * k4:4 * k4 + 4, :], AF.Identity, XK[4 * k4:4 * k4 + 4],
                XK[4 * k4:4 * k4 + 4], scale=ALPHA)

        if _STOP[0] == 7:
            raise _StopBuild()
        def resid_accum(oc, ti, a, n, pbank, gate):
            j = 0 if ti < 2 else 1
            stt(X[:, oc, a:a + n], PS[pbank][:, 0:n], gate[:, oc, j:j + 1], X[:, oc, a:a + n], ALU.mult, ALU.add,
                [("ps", pbank), MK, ("X", oc)], [("X", oc)])

        for oc in range(KC):
            ri = load_w(wo_d[l][oc])
            for ti, (a, n) in enumerate(TT):
                pb = next_ps()
                mm_group(PS[pb][:, 0:n], [(RING[ri][:, k, :], MIX[:, k, a:a + n]) for k in range(KC)],
                         reads=[("ring", ri)] + MIXK, writes=[("ps", pb)])
                resid_accum(oc, ti, a, n, pb, g1)

        if _STOP[0] == 8:
            raise _StopBuild()
        def layer_norm(which, write_xm):
            for ti, (a, n) in enumerate(TT):
                j = 0 if ti < 2 else 1
                act(xb[:, :, 0:n], X[:, :, a:a + n], AF.Identity, XK, XBK)
                act(xsq[:, :, 0:n], X[:, :, a:a + n], AF.Square, XK, XSQK)
                p1 = next_ps()
                p2 = next_ps()
                mm_group(PS[p1][:, 0:n], [(ones[:], xb[:, k, 0:n]) for k in range(KC)], reads=["ones"] + XBK,
                         writes=[("ps", p1)])
                mm_group(PS[p2][:, 0:n], [(ones[:], xsq[:, k, 0:n]) for k in range(KC)], reads=["ones"] + XSQK,
                         writes=[("ps", p2)])
                act(mean[:, 0:n], PS[p1][:, 0:n], AF.Identity, [("ps", p1)], ["mean"], scale=1.0 / D)
                tt("dve", rtmp[:, 0:n], mean[:, 0:n], mean[:, 0:n], ALU.mult, ["mean"], ["rtmp"])
                stt(rstd[:, 0:n], PS[p2][:, 0:n], 1.0 / D, rtmp[:, 0:n], ALU.mult, ALU.subtract,
                    [("ps", p2), "rtmp"], ["rstd"])
                act(rstd[:, 0:n], rstd[:, 0:n], AF.Ln, ["rstd"], ["rstd"], bias=LN_EPS)
                act(rstd[:, 0:n], rstd[:, 0:n], AF.Exp, ["rstd"], ["rstd"], scale=-0.5)
                for k in range(KC):
                    tt("pool", X[:, k, a:a + n], X[:, k, a:a + n], mean[:, 0:n], ALU.subtract, [("X", k), "mean"],
                       [("X", k)])
                    tt("dve", X[:, k, a:a + n], X[:, k, a:a + n], rstd[:, 0:n], ALU.mult, [("X", k), "rstd"],
                       [("X", k)])
                    act(X[:, k, a:a + n], X[:, k, a:a + n], AF.Identity, [("X", k), "lng", "lnb"], [("X", k)],
                        scale=lng[:, l, which, k:k + 1], bias=lnb[:, l, which, k:k + 1])
                    if write_xm:
                        act(XM[:, k, a:a + n], X[:, k, a:a + n], AF.Identity, [("X", k), S2K, MK], [("XM", k)],
                            scale=sc2p[:, k, j:j + 1], bias=sh2[:, k, j:j + 1])

        layer_norm(0, True)
        for k4 in range(4):
            act(X[:, 4 * k4:4 * k4 + 4, :], X[:, 4 * k4:4 * k4 + 4, :], AF.Identity, XK[4 * k4:4 * k4 + 4],
                XK[4 * k4:4 * k4 + 4], scale=ALPHA)

        if _STOP[0] == 9:
            raise _StopBuild()
        nxt = (l + 1 < L)
        pmn = next_ps() if nxt else None
        excl = (pmn,) if nxt else ()
        mct = [0]

        def mods_step(nt):
            if not nxt:
                return
            for _ in range(nt):
                if mct[0] < 96:
                    mods_tile(l + 1, mct[0], pmn)
                    mct[0] += 1

        for grp in range(4):
            for jj in range(16):
                ri = load_w(wup_d[l][grp * 16 + jj])
                for ti, (a, n) in enumerate(TT):
                    pb = next_ps(exclude=excl)
                    project(ri, a, n, pb)
                    rl = tmp_relu[ti % 2]
                    act(rl[:, 0:n], PS[pb][:, 0:n], AF.Relu, [("ps", pb)], [("relu", ti % 2)])
                    act(HID[:, jj, a:a + n], rl[:, 0:n], AF.Square, [("relu", ti % 2)], [("MIX", jj)])
                mods_step(1)
            for oc in range(KC):
                ri = load_w(wdn_d[l][grp * 16 + oc])
                for ti, (a, n) in enumerate(TT):
                    pb = next_ps(exclude=excl)
                    mm_group(PS[pb][:, 0:n], [(RING[ri][:, k, :], HID[:, k, a:a + n]) for k in range(KC)],
                             reads=[("ring", ri)] + MIXK, writes=[("ps", pb)])
                    resid_accum(oc, ti, a, n, pb, g2)
                mods_step(1)
        if nxt:
            mods_step(96)
            mods_finish(l + 1, pmn)
        layer_norm(1, False)

    try:
        for l in range(L):
            layer(l)
    except _StopBuild:
        pass
    for k4 in range(4):
        dma("sp", yT_d[:, 4 * k4:4 * k4 + 4, :], X[:, 4 * k4:4 * k4 + 4, :], XK[4 * k4:4 * k4 + 4], [("yT", k4)],
            ("yout", k4))

    with nc.Block() as block:
        S.emit(nc, stack, block)
    stack.close()
    return nc


def _core_slots(core):
    if core < 2:
        return [("s", core, 256 * s) for s in range(4)] + [("p", 30 + core, 0)]
    return [("p", (core - 2) * 5 + s, 0) for s in range(5)]


def _fm(vec2d):
    a = np.asarray(vec2d, np.float32)
    n = a.shape[-1] // 128
    a = a.reshape(a.shape[:-1] + (n, 128))
    return np.ascontiguousarray(np.moveaxis(a, -1, 0))


def _retile(w):
    K, N = w.shape
    n = N // 128
    return np.ascontiguousarray(w.reshape(K // 128, 128, n, 128).transpose(2, 1, 0, 3)).reshape(n, 128, (K // 128) * 128)


def _win_cols():
    perm = np.arange(128) ^ 32
    ar = np.arange(128)
    cols = []
    for g in range(2):
        for r in range(4):
            h = 4 * g + r
            cols.append(h * 128 + ar)
            cols.append(h * 128 + perm)
        cols.append(1024 + g * 128 + ar)
        cols.append(1024 + g * 128 + perm)
        if g == 0:
            cols.append(1280 + ar)
            cols.append(1280 + 128 + ar)
    for h in range(8):
        for base in (1536, 2560, 3584, 4608, 5632):
            cols.append(base + h * 128 + ar)
    return np.concatenate(cols)


def _rope_tables(slots):
    cosT = np.ones((128, NTOK), np.float32)
    sinT = np.zeros((128, NTOK), np.float32)
    inv = (np.float32(10000.0) ** (-np.arange(32, dtype=np.float32) / np.float32(32))).astype(np.float32)
    for s, (kind, seq, start) in enumerate(slots):
        if kind != "s":
            continue
        t = start + np.arange(256)
        row = (t // 64).astype(np.float32)
        col = (t % 64).astype(np.float32)
        ang_r = (row[None, :] * inv[:, None]).astype(np.float32)
        ang_c = (col[None, :] * inv[:, None]).astype(np.float32)
        sl = slice(256 * s, 256 * (s + 1))
        cr, sr, cc, sc_ = np.cos(ang_r), np.sin(ang_r), np.cos(ang_c), np.sin(ang_c)
        cosT[0:32, sl] = cr
        cosT[32:64, sl] = cr
        cosT[64:96, sl] = cc
        cosT[96:128, sl] = cc
        sinT[0:32, sl] = -sr
        sinT[32:64, sl] = sr
        sinT[64:96, sl] = -sc_
        sinT[96:128, sl] = sc_
    return cosT, sinT


def _masks(slots):
    m = np.zeros((128, NBLK, 3, 128), np.float32)
    ctxflag = np.zeros((128, NBLK), np.float32)
    ar = np.arange(128)
    for B in range(NBLK):
        kind, seq, start = slots[B // 2]
        qpos = start + (B % 2) * 128 + ar
        if kind == "s":
            ctxflag[:, B] = 1.0
        for nb, KB in ((0, B - 1), (1, B), (2, B + 1)):
            if KB < 0 or KB >= NBLK:
                continue
            kkind, kseq, kstart = slots[KB // 2]
            if (kkind, kseq) != (kind, seq):
                continue
            kpos = kstart + (KB % 2) * 128 + ar
            if kind == "s":
                ok = np.abs(kpos[:, None] - qpos[None, :]) <= 128
            else:
                ok = np.ones((128, 128), bool)
            m[:, B, nb, :] = ok.astype(np.float32)
    cfl = np.full((128, 1), 1.0 if slots[0][0] == "s" else 0.0, np.float32)
    return m, ctxflag, cfl


def _const_mats():
    cm = np.zeros((4, 128, 128), np.float32)
    ar = np.arange(128)
    cm[0] = np.eye(128, dtype=np.float32)
    same = (ar[:, None] // 32) == (ar[None, :] // 32)
    cm[1] = (same & (ar[:, None] <= ar[None, :])).astype(np.float32)
    cm[2] = (same & (ar[:, None] >= ar[None, :])).astype(np.float32)
    rowmask = np.zeros((128, 4), np.float32)
    for c in range(4):
        rowmask[32 * c:32 * (c + 1), c] = 1.0
    return cm, rowmask


def _prepare(inp, L, cores):
    f = lambda k: np.asarray(inp[k], np.float32)
    w_in = f("w_in")
    cols = _win_cols()
    shared = {}
    for l in range(L):
        shared[f"wmod{l}"] = _retile(f("w_mod")[l])
        shared[f"win{l}"] = _retile(w_in[l][:, cols])
        shared[f"wo{l}"] = _retile(f("w_o")[l])
        shared[f"wup{l}"] = _retile(f("w_up")[l])
        shared[f"wdn{l}"] = np.ascontiguousarray(
            f("w_down")[l].reshape(4, 16, 128, 16, 128).transpose(0, 3, 2, 1, 4)).reshape(64, 128, 2048)
    shared.update({
        "bmod": _fm(f("b_mod")[:L]),
        "lng": _fm(f("ln_g")[:L]),
        "lnb": _fm(f("ln_b")[:L]),
        "attg": _fm(f("attn_norm_g")[:L]),
        "hgg": np.ascontiguousarray(f("hg_norm_g")[:L].T),
        "lbl": _fm(f("hg_lb_logits")),
        "sink": np.ascontiguousarray(np.broadcast_to(f("attn_sink")[:L][None], (128, L, 8))),
    })
    cm, rowmask = _const_mats()
    shared["cmats"] = cm
    shared["rowmask"] = rowmask
    xp, xs = f("x_prompt"), f("x_sample")
    maps = []
    for core in cores:
        slots = _core_slots(core)
        xs_ = []
        for kind, seq, start in slots:
            xs_.append(xs[seq, start:start + 256] if kind == "s" else xp[seq])
        xc = np.concatenate(xs_, 0)
        d = dict(shared)
        d["xT"] = np.ascontiguousarray(xc.reshape(NTOK, KC, 128).transpose(2, 1, 0))
        condA = f("c")[core] if core < 2 else f("c_ctx")
        condB = f("c_ctx")
        d["cond"] = _fm(np.stack([condA, condB]))
        d["cond"] = np.ascontiguousarray(d["cond"].transpose(0, 2, 1))
        cosT, sinT = _rope_tables(slots)
        d["cosT"], d["sinT"] = cosT, sinT
        m, ctxflag, cfl = _masks(slots)
        d["masks"], d["ctxflag"], d["cflag"] = m, ctxflag, cfl
        if core < 2:
            ck = f("cache_k")[core][:L]
            cv = f("cache_v")[core][:L]
            d["ckT"] = np.ascontiguousarray(ck.transpose(0, 3, 2, 1))
            d["cv"] = np.ascontiguousarray(cv.reshape(L, 4, 128, 256).transpose(0, 2, 1, 3))
            d["s0"] = np.ascontiguousarray(np.stack([f("state_hgrn_fwd")[core][:L], f("state_hgrn_bwd")[core][:L]], 1))
        else:
            d["ckT"] = np.zeros((L, 128, 2, 512), np.float32)
            d["cv"] = np.zeros((L, 128, 4, 256), np.float32)
            d["s0"] = np.zeros((L, 2, 8, 128, 128), np.float32)
        maps.append(d)
    return maps


_PROG = {}


def _run(inp, L=DEPTH, cores=tuple(range(8)), trace=False):
    if L not in _PROG:
        _PROG[L] = build_program(L)
    nc = _PROG[L]
    maps = _prepare(inp, L, cores)
    res = run_bass_kernel_spmd(nc, maps, core_ids=list(range(len(cores))), **({"trace": True} if trace else {}))
    return res


def _gather(results, L, cores):
    BATCH, SEQ, DEC_BATCH, DEC_SEQ = 32, 256, 2, 1024
    y_p = np.zeros((BATCH, SEQ, D), np.float32)
    y_s = np.zeros((DEC_BATCH, DEC_SEQ, D), np.float32)
    nk = np.zeros((BATCH, L, SEQ, 2, 128), np.float32)
    nv = np.zeros((BATCH, L, SEQ, 2, 128), np.float32)
    sf = np.zeros((BATCH, L, 8, 128, 128), np.float32)
    sbw = np.zeros((BATCH, L, 8, 128, 128), np.float32)
    for ci, core in enumerate(cores):
        r = results[ci]
        y = np.asarray(r["yT"]).transpose(2, 1, 0).reshape(NTOK, D)
        okT = np.asarray(r["okT"])
        ov = np.asarray(r["ov"])
        ost = np.asarray(r["ost"])
        for s, (kind, seq, start) in enumerate(_core_slots(core)):
            ys = y[256 * s:256 * (s + 1)]
            if kind == "s":
                y_s[seq, start:start + 256] = ys
            else:
                y_p[seq] = ys
                nk[seq] = okT[:, :, :, 256 * s:256 * (s + 1)].transpose(0, 3, 1, 2)
                nv[seq] = ov[:, 2 * s:2 * s + 2].reshape(L, 256, 2, 128)
                sf[seq] = ost[:, 0, :, :, s, :]
                sbw[seq] = ost[:, 1, :, :, s, :]
    return y_p, y_s, nk, nv, sf, sbw


def kernel(**inputs):
    res = _run(inputs, DEPTH, tuple(range(8)))
    return _gather(res.results, DEPTH, tuple(range(8)))
```

```python
from contextlib import ExitStack
import numpy as np
import concourse.bass as bass
import concourse.mybir as mybir
from concourse.bass_utils import run_bass_kernel_spmd

F32 = mybir.dt.float32
BF16 = mybir.dt.bfloat16
AF = mybir.ActivationFunctionType
ALU = mybir.AluOpType

D = 2048
KC = 16
NTOK = 1280
NBLK = 10
NCH = 40
NSLOT = 5
DEPTH = 4
ALPHA = (2 * DEPTH) ** 0.25
ATT_SCALE = 128 ** -0.5
LN_EPS = 1e-5
RMS_EPS = 1e-6
TT = [(0, 512), (512, 512), (1024, 256)]
NRING = 6
N_IN_TILES = 62
_STOP = [None]


class _StopBuild(Exception):
    pass


class _Op:
    __slots__ = ("eng", "fn", "deps", "sig", "sem", "tick", "dma", "ndma", "epoch", "waits", "vc")


class Sched:
    def __init__(self):
        self.ops = []
        self.lastw = {}
        self.readers = {}
        self.epoch = 0

    def add(self, eng, fn, reads=(), writes=(), dma=None, ndma=1):
        op = _Op()
        op.eng, op.fn, op.dma, op.ndma, op.sig = eng, fn, dma, ndma, False
        op.epoch = self.epoch
        deps = {}
        psr = [r for r in reads if isinstance(r, tuple) and r and r[0] == "ps"]
        if psr:
            reads = [r for r in reads if r not in psr]
            writes = list(writes) + [r for r in psr if r not in writes]

        def consider(d, kind):
            if d is None or d is op:
                return
            if d.dma is None and op.dma is None and d.eng == eng:
                if eng == "pe":
                    return
            deps[id(d)] = d

        for r in reads:
            consider(self.lastw.get(r), "raw")
        for w in writes:
            consider(self.lastw.get(w), "waw")
            for rd in self.readers.get(w, ()):
                consider(rd, "war")
        op.deps = list(deps.values())
        for d in op.deps:
            d.sig = True
        for r in reads:
            self.readers.setdefault(r, []).append(op)
        for w in writes:
            self.lastw[w] = op
            self.readers[w] = []
        self.ops.append(op)
        return op

    def emit(self, nc, stack, block):
        sems = {}

        def get_sem(key):
            if key not in sems:
                sems[key] = stack.enter_context(nc.semaphore("s_" + "_".join(str(k) for k in key)))
            return sems[key]

        counts = {}
        for op in self.ops:
            if op.dma is not None:
                key = ("d", op.dma)
                counts[key] = counts.get(key, 0) + 16 * op.ndma
            else:
                key = ("e", op.eng, op.epoch)
                if op.sig:
                    counts[key] = counts.get(key, 0) + 1
            op.sem = key
            op.tick = counts.get(key, 0)
        known = {}
        for op in self.ops:
            k = known.setdefault(op.eng, {})
            need = []
            for d in sorted(op.deps, key=lambda d_: -d_.tick):
                if k.get(d.sem, 0) < d.tick:
                    need.append((d.sem, d.tick))
                    k[d.sem] = d.tick
                    for s_, t_ in d.vc.items():
                        if k.get(s_, 0) < t_:
                            k[s_] = t_
            op.waits = need
            op.vc = dict(k) if (op.sig or op.dma is not None) else None
        by_eng = {}
        for op in self.ops:
            by_eng.setdefault(op.eng, []).append(op)
        for op in self.ops:
            if op.sig or op.dma is not None:
                get_sem(op.sem)

        def run(engname, e):
            for op in by_eng.get(engname, []):
                waits = op.waits
                for sem_, tick_ in waits[1:]:
                    e.wait_ge(get_sem(sem_), tick_)
                ins = op.fn(e)
                if isinstance(ins, tuple):
                    first, last = ins
                elif isinstance(ins, list):
                    first, last = ins[0], ins
                else:
                    first = last = ins
                if waits:
                    first.wait_op(get_sem(waits[0][0]), waits[0][1], "sem-ge")
                if op.dma is not None:
                    lst = last if isinstance(last, list) else [last]
                    assert len(lst) == op.ndma, (len(lst), op.ndma)
                    for i_ in lst:
                        i_.then_inc(get_sem(op.sem), 16)
                elif op.sig:
                    last.then_inc(get_sem(op.sem), 1)
            done = {}
            for op in by_eng.get(engname, []):
                if op.dma is not None:
                    done[op.sem] = max(done.get(op.sem, 0), op.tick)
            for sem_, tick_ in done.items():
                e.wait_ge(get_sem(sem_), tick_)

        @block.tensor
        def _(e):
            run("pe", e)

        @block.scalar
        def _(e):
            run("act", e)

        @block.vector
        def _(e):
            run("dve", e)

        @block.gpsimd
        def _(e):
            run("pool", e)

        @block.sync
        def _(e):
            run("sp", e)


def build_program(n_layers=DEPTH):
    L = n_layers
    nc = bass.Bass("TRN2", target_bir_lowering=False)

    def din(name, shape):
        return nc.dram_tensor(name, list(shape), F32, kind="ExternalInput").ap()

    def dout(name, shape):
        return nc.dram_tensor(name, list(shape), F32, kind="ExternalOutput").ap()

    xT_d = din("xT", [128, KC, NTOK])
    cond_d = din("cond", [128, KC, 2])
    bmod_d = din("bmod", [128, L, 96])
    lng_d = din("lng", [128, L, 2, KC])
    lnb_d = din("lnb", [128, L, 2, KC])
    attg_d = din("attg", [128, L, 8])
    hgg_d = din("hgg", [128, L])
    lbl_d = din("lbl", [128, 2, DEPTH, 8])
    sink_d = din("sink", [128, L, 8])
    cos_d = din("cosT", [128, NTOK])
    sin_d = din("sinT", [128, NTOK])
    msk_d = din("masks", [128, NBLK, 3, 128])
    ctxf_d = din("ctxflag", [128, NBLK])
    cfl_d = din("cflag", [128, 1])
    ckT_d = din("ckT", [L, 128, 2, 512])
    cv_d = din("cv", [L, 128, 4, 256])
    s0_d = din("s0", [L, 2, 8, 128, 128])
    cm_d = din("cmats", [4, 128, 128])
    rmask_d = din("rowmask", [128, 4])
    wmod_d = [din(f"wmod{l}", [96, 128, 2048]) for l in range(L)]
    win_d = [din(f"win{l}", [N_IN_TILES, 128, 2048]) for l in range(L)]
    wo_d = [din(f"wo{l}", [16, 128, 2048]) for l in range(L)]
    wup_d = [din(f"wup{l}", [64, 128, 2048]) for l in range(L)]
    wdn_d = [din(f"wdn{l}", [64, 128, 2048]) for l in range(L)]

    yT_d = dout("yT", [128, KC, NTOK])
    okT_d = dout("okT", [L, 2, 128, NTOK])
    ov_d = dout("ov", [L, NBLK, 128, 256])
    ost_d = dout("ost", [L, 2, 8, 128, NSLOT, 128])
    xsp_d = nc.dram_tensor("xspill", [128, KC, NTOK], F32, kind="Internal").ap()

    S = Sched()
    stack = ExitStack()

    def sb(name, shape, dt):
        return stack.enter_context(nc.sbuf_tensor(name, list(shape), dt))

    RA = sb("RA", [128, KC * NTOK], F32)
    XM = sb("XM", [128, KC, NTOK], BF16)
    RC = sb("RC", [128, KC * NTOK], BF16)
    RING = [sb(f"ring{i}", [128, KC, 128], BF16) for i in range(NRING)]
    X = RA[:].rearrange("p (k t) -> p k t", k=KC)
    MIX = RC[:].rearrange("p (k t) -> p k t", k=KC)
    HID = MIX

    cond_f = sb("cond_f", [128, KC, 2], F32)
    condS = sb("condS", [128, KC, 2], BF16)
    bmod = sb("bmod_s", [128, L, 96], F32)
    lng = sb("lng_s", [128, L, 2, KC], F32)
    lnb = sb("lnb_s", [128, L, 2, KC], F32)
    attg = sb("attg_s", [128, L, 8], F32)
    hgg = sb("hgg_s", [128, L], F32)
    lbl = sb("lbl_s", [128, 2, DEPTH, 8], F32)
    lbv = sb("lbv", [128, 2, DEPTH, 8], F32)
    oml = sb("oml", [128, 2, DEPTH, 8], F32)
    lbsum = sb("lbsum", [128, 2, 8], F32)
    esink = sb("esink", [128, L, 8], F32)
    ctxf = sb("ctxf", [128, NBLK], F32)
    cfl = sb("cfl", [128, 1], F32)
    rmask = sb("rmask", [128, 4], F32)
    cmats = sb("cmats_s", [128, 3, 128], BF16)
    ones = sb("ones", [128, 128], BF16)
    rmaskF = sb("rmaskF", [128, 4, 128], BF16)
    scanm = sb("scanm", [128, NCH, 32], BF16)
    mods2 = [sb("mods0", [128, 96, 2], F32), sb("mods1", [128, 96, 2], F32)]
    sc1p2 = [sb("sc1p0", [128, KC, 2], F32), sb("sc1p1", [128, KC, 2], F32)]
    sc2p2 = [sb("sc2p0", [128, KC, 2], F32), sb("sc2p1", [128, KC, 2], F32)]
    dummy = sb("dummy_t", [128, 8], F32)

    PS = [stack.enter_context(nc.psum_tensor(f"ps{i}", [128, 512], F32)) for i in range(8)]

    ra_off = [0]

    def ra_f32(nwords):
        o = ra_off[0]
        ra_off[0] += nwords
        assert ra_off[0] <= KC * NTOK, ra_off[0]
        return RA[:, o:o + nwords]

    def ra_bf16(nel):
        assert nel % 2 == 0
        return ra_f32(nel // 2).bitcast(BF16)

    cosT = ra_f32(NTOK)
    sinT = ra_f32(NTOK)
    masks = ra_bf16(NBLK * 3 * 128).rearrange("p (b n q) -> p b n q", b=NBLK, n=3)
    ckT = ra_bf16(2 * 512).rearrange("p (g t) -> p g t", g=2)
    cvt = ra_bf16(4 * 256).rearrange("p (b c) -> p b c", b=4)
    qT = ra_bf16(4 * NTOK).rearrange("p (r t) -> p r t", r=4)
    kT = ra_bf16(NTOK)
    vtok = ra_bf16(NBLK * 256).rearrange("p (b c) -> p b c", b=NBLK)
    kraw = ra_f32(NTOK)
    vraw = [ra_f32(256), ra_f32(256)]
    tmpA = [ra_f32(512), ra_f32(512)]
    tmpB = [ra_f32(512), ra_f32(512)]
    eT = [ra_bf16(512) for _ in range(4)]
    attraw1 = ra_f32(4 * NTOK).rearrange("p (r t) -> p r t", r=4)
    attraw0 = RC[:, 8 * NTOK:16 * NTOK].bitcast(F32).rearrange("p (r t) -> p r t", r=4)
    attraw = [attraw0, attraw1]
    ATTK = [[("MIX", k) for k in range(8, 16)], ["attraw1"]]
    ra_off[0] = 0
    qf = ra_f32(NTOK)
    sigf = [ra_f32(NTOK), ra_f32(NTOK)]
    gsil = ra_f32(NTOK)
    vtokH = ra_bf16(NBLK * 128).rearrange("p (b c) -> p b c", b=NBLK)
    kk = ra_f32(NTOK)
    E1 = ra_f32(NTOK)
    logf = E1
    bc = ra_f32(NTOK)
    d1 = ra_f32(NTOK)
    qmid = ra_bf16(NTOK)
    kmid = ra_bf16(NTOK)
    vTh = kmid
    klast = ra_bf16(NTOK)
    qb = ra_bf16(NTOK)
    decay = ra_f32(NCH)
    klastM = ra_bf16(NCH * 128).rearrange("p (c k) -> p c k", c=NCH)
    aT = [ra_bf16(128), ra_bf16(128)]
    Sall = ra_bf16(NCH * 128).rearrange("p (c v) -> p c v", c=NCH)
    Sf = [ra_f32(128), ra_f32(128)]
    Sfin = ra_f32(NSLOT * 128).rearrange("p (s v) -> p s v", s=NSLOT)
    S0t = ra_f32(128)
    osq = d1[:, 0:256].bitcast(BF16)
    rstdh = d1[:, 256:768]
    th = d1[:, 768:1280]
    xb = RC[:, 0:KC * 512].rearrange("p (k t) -> p k t", k=KC)
    xsq = RC[:, 7 * NTOK:7 * NTOK + KC * 512].rearrange("p (k t) -> p k t", k=KC)
    XBK = [("MIX", k) for k in range(0, 7)]
    XSQK = [("MIX", k) for k in range(7, 14)]
    mean = sb("ln_mean", [128, 512], F32)
    rstd = sb("ln_rstd", [128, 512], F32)
    rtmp = sb("ln_tmp", [128, 512], F32)
    tmp_relu = [sb("relu0", [128, 512], F32), sb("relu1", [128, 512], F32)]

    ps_rr = [0]

    def next_ps(exclude=()):
        while True:
            i = ps_rr[0] % 8
            ps_rr[0] += 1
            if i not in exclude:
                return i

    ring_rr = [0]

    def load_w(src_ap):
        i = ring_rr[0] % NRING
        ring_rr[0] += 1
        dst = RING[i]
        S.add("pool", lambda e, dst=dst, src=src_ap: e.dma_start(
            out=dst[:].rearrange("p k c -> p (k c)"), in_=src),
            writes=[("ring", i)], dma=("ring", i))
        return i

    def mm_group(out_ap, pairs, reads, writes):
        def fn(e, out_ap=out_ap, pairs=pairs):
            ins = first = None
            n = len(pairs)
            for j, (l_, r_) in enumerate(pairs):
                ins = e.matmul(out_ap, l_, r_, start=(j == 0), stop=(j == n - 1))
                if first is None:
                    first = ins
            return (first, ins)
        return S.add("pe", fn, reads=reads, writes=writes)

    def mm1(out_ap, l_, r_, start, stop, reads, writes):
        return S.add("pe", lambda e: e.matmul(out_ap, l_, r_, start=start, stop=stop, skip_group_check=True),
                     reads=reads, writes=writes)

    def act(out, in_, func, reads, writes, scale=None, bias=None):
        kw = {}
        if scale is not None:
            kw["scale"] = scale
        if bias is not None:
            kw["bias"] = bias
        return S.add("act", lambda e: e.activation(out=out, in_=in_, func=func, **kw), reads=reads, writes=writes)

    def tt(eng, out, in0, in1, op, reads, writes):
        return S.add(eng, lambda e: e.tensor_tensor(out=out, in0=in0, in1=in1, op=op), reads=reads, writes=writes)

    def ts(eng, out, in0, s1, s2, op0, op1, reads, writes):
        if op1 is None:
            return S.add(eng, lambda e: e.tensor_scalar(out=out, in0=in0, scalar1=s1, scalar2=None, op0=op0),
                         reads=reads, writes=writes)
        return S.add(eng, lambda e: e.tensor_scalar(out=out, in0=in0, scalar1=s1, scalar2=s2, op0=op0, op1=op1),
                     reads=reads, writes=writes)

    def stt(out, in0, scalar, in1, op0, op1, reads, writes):
        return S.add("dve", lambda e: e.scalar_tensor_tensor(out=out, in0=in0, scalar=scalar, in1=in1,
                                                             op0=op0, op1=op1), reads=reads, writes=writes)

    def vcopy(out, in_, reads, writes):
        return S.add("dve", lambda e: e.tensor_copy(out=out, in_=in_), reads=reads, writes=writes)

    def vmemset(ap, val, writes):
        return S.add("dve", lambda e: e.memset(ap, val), writes=writes)

    def dma(q, out, in_, reads, writes, key):
        return S.add(q, lambda e: e.dma_start(out=out, in_=in_), reads=reads, writes=writes, dma=key)

    def barrier(keys):
        return S.add("dve", lambda e: e.memset(dummy[:, 0:1], 0.0), writes=list(keys))

    XK = [("X", k) for k in range(KC)]
    XMK = [("XM", k) for k in range(KC)]
    MIXK = [("MIX", k) for k in range(KC)]
    ATT_TEMPK = (["cosT", "sinT", "masks", "ckT", "cvt", "kT", "kraw", "vraw0", "vraw1", "attraw1"] +
                 [("qT", r) for r in range(4)] + [("vtok", B) for B in range(NBLK)] +
                 [("tmpA", 0), ("tmpA", 1), ("tmpB", 0), ("tmpB", 1)] + [("eT", i) for i in range(4)])
    HG_TEMPK = ["qf", "sig0", "sig1", "gsil", "vtokH", "kk", "E1", "bc", "d1", "qmid", "kmid", "klast", "qb",
                "decay", "klastM", "aT0", "aT1", "Sall", "Sf0", "Sf1", "Sfin", "S0t"]

    for k4 in range(4):
        dma("sp", X[:, 4 * k4:4 * k4 + 4, :], xT_d[:, 4 * k4:4 * k4 + 4, :], [], XK[4 * k4:4 * k4 + 4], ("xin", k4))
    small = [(cond_f, cond_d, "cond"), (bmod, bmod_d, "bmod"), (lng, lng_d, "lng"), (lnb, lnb_d, "lnb"),
             (attg, attg_d, "attg"), (hgg, hgg_d, "hgg"), (lbl, lbl_d, "lbl"), (esink, sink_d, "esink"),
             (ctxf, ctxf_d, "ctxf"), (cfl, cfl_d, "cfl"), (rmask, rmask_d, "rmask")]
    for t_, d_, nm in small:
        dma("sp", t_[:], d_, [], [nm], ("c", nm))
    dma("pool", cmats[:], cm_d[0:3].rearrange("m p c -> p m c"), [], ["cmats"], ("c", "cmats"))
    vmemset(ones[:], 1.0, ["ones"])
    vcopy(rmaskF[:], rmask[:].unsqueeze(2).to_broadcast([128, 4, 128]), ["rmask"], ["rmaskF"])
    vmemset(scanm[:], 1.0, ["scanm"])
    vmemset(scanm[:, :, 0:1], 0.0, ["scanm"])
    act(condS[:], cond_f[:], AF.Silu, ["cond"], ["condS"])
    act(esink[:], esink[:], AF.Exp, ["esink"], ["esink"])
    act(lbl[:], lbl[:], AF.Exp, ["lbl"], ["lbl"])
    tt("dve", lbsum[:], lbl[:, :, 0, :], lbl[:, :, 1, :], ALU.add, ["lbl"], ["lbsum"])
    for l_ in range(2, DEPTH):
        tt("dve", lbsum[:], lbsum[:], lbl[:, :, l_, :], ALU.add, ["lbl", "lbsum"], ["lbsum"])
    S.add("dve", lambda e: e.reciprocal(out=lbsum[:], in_=lbsum[:]), reads=["lbsum"], writes=["lbsum"])
    vmemset(lbv[:, :, 0, :], 0.0, ["lbv"])
    for l_ in range(1, DEPTH):
        tt("dve", lbv[:, :, l_, :], lbl[:, :, l_, :], lbsum[:], ALU.mult, ["lbl", "lbsum", "lbv"], ["lbv"])
        if l_ > 1:
            tt("dve", lbv[:, :, l_, :], lbv[:, :, l_, :], lbv[:, :, l_ - 1, :], ALU.add, ["lbv"], ["lbv"])
    ts("dve", oml[:], lbv[:], -1.0, 1.0, ALU.mult, ALU.add, ["lbv"], ["oml"])

    def r4(ap):
        return ap.rearrange("p (r q) -> p r q", r=4)

    def c32(ap):
        return ap.rearrange("p (c s) -> p c s", s=32)

    def mods_tile(l, ct, pm):
        ri = load_w(wmod_d[l][ct])
        mm_group(PS[pm][:, 2 * ct:2 * ct + 2],
                 [(RING[ri][:, k, :], condS[:, k, :]) for k in range(KC)],
                 reads=[("ring", ri), "condS"], writes=[("ps", pm)])

    def mods_finish(l, pm):
        p_ = l % 2
        tt("dve", mods2[p_][:], PS[pm][:, 0:192].rearrange("p (c j) -> p c j", j=2),
           bmod[:, l, :].unsqueeze(2).to_broadcast([128, 96, 2]), ALU.add,
           [("ps", pm), "bmod"], [("mods", p_)])
        ts("dve", sc1p2[p_][:], mods2[p_][:, 16:32, :], 1.0, None, ALU.add, None, [("mods", p_)], [("sc1p", p_)])
        ts("dve", sc2p2[p_][:], mods2[p_][:, 64:80, :], 1.0, None, ALU.add, None, [("mods", p_)], [("sc2p", p_)])

    def layer(l):
        S.epoch = l
        if l == 0:
            pm = next_ps()
            for ct in range(96):
                mods_tile(0, ct, pm)
            mods_finish(0, pm)
        mods, sc1p, sc2p = mods2[l % 2], sc1p2[l % 2], sc2p2[l % 2]
        MK, S1K, S2K = ("mods", l % 2), ("sc1p", l % 2), ("sc2p", l % 2)
        sh1 = mods[:, 0:16, :]
        g1 = mods[:, 32:48, :]
        sh2 = mods[:, 48:64, :]
        g2 = mods[:, 80:96, :]

        if _STOP[0] == 1:
            raise _StopBuild()
        for k in range(KC):
            for j, (a, b) in enumerate([(0, 1024), (1024, 1280)]):
                if (k + j) % 2 == 0:
                    ts("dve", XM[:, k, a:b], X[:, k, a:b], sc1p[:, k, j:j + 1], sh1[:, k, j:j + 1], ALU.mult, ALU.add,
                       [("X", k), S1K, MK], [("XM", k)])
                else:
                    act(XM[:, k, a:b], X[:, k, a:b], AF.Identity, [("X", k), S1K, MK], [("XM", k)],
                        scale=sc1p[:, k, j:j + 1], bias=sh1[:, k, j:j + 1])
        if _STOP[0] == 2:
            raise _StopBuild()
        for k4 in range(4):
            dma("sp", xsp_d[:, 4 * k4:4 * k4 + 4, :], X[:, 4 * k4:4 * k4 + 4, :], XK[4 * k4:4 * k4 + 4], ["xspill"],
                ("xsp", k4))
        barrier(XK + ATT_TEMPK + HG_TEMPK)
        dma("sp", cosT, cos_d, [], ["cosT"], ("c", "cosT"))
        dma("sp", sinT, sin_d, [], ["sinT"], ("c", "sinT"))
        dma("pool", masks, msk_d, [], ["masks"], ("c", "masks"))
        dma("pool", ckT, ckT_d[l], [], ["ckT"], ("c", "ckT"))
        dma("pool", cvt, cv_d[l], [], ["cvt"], ("c", "cvt"))

        if _STOP[0] == 3:
            raise _StopBuild()
        wt = [0]

        def project(ri, a, n, pbank):
            mm_group(PS[pbank][:, 0:n], [(RING[ri][:, k, :], XM[:, k, a:a + n]) for k in range(KC)],
                     reads=[("ring", ri)] + XMK, writes=[("ps", pbank)])

        def rope_pair(dst_fn, keyname, raw_out=None):
            r0 = load_w(win_d[l][wt[0]])
            r1 = load_w(win_d[l][wt[0] + 1])
            wt[0] += 2
            for ti, (a, n) in enumerate(TT):
                p0 = next_ps()
                p1 = next_ps()
                project(r0, a, n, p0)
                project(r1, a, n, p1)
                tA = tmpA[ti % 2]
                tB = tmpB[ti % 2]
                if raw_out is not None:
                    vcopy(raw_out[:, a:a + n], PS[p0][:, 0:n], [("ps", p0)], ["kraw"])
                tt("dve", tA[:, 0:n], PS[p0][:, 0:n], cosT[:, a:a + n], ALU.mult, [("ps", p0), "cosT"], [("tmpA", ti % 2)])
                tt("dve", tB[:, 0:n], PS[p1][:, 0:n], sinT[:, a:a + n], ALU.mult, [("ps", p1), "sinT"], [("tmpB", ti % 2)])
                tt("pool", dst_fn(a, n), tA[:, 0:n], tB[:, 0:n], ALU.add, [("tmpA", ti % 2), ("tmpB", ti % 2)], [keyname])

        for g in range(2):
            if _STOP[0] == 34 and g == 1:
                raise _StopBuild()
            for r in range(4):
                rope_pair(lambda a, n, r=r: qT[:, r, a:a + n], ("qT", r))
            if _STOP[0] == 30:
                raise _StopBuild()
            rope_pair(lambda a, n: kT[:, a:a + n], "kT", raw_out=kraw)
            if _STOP[0] == 31:
                raise _StopBuild()
            dma("sp", okT_d[l, g], kraw, ["kraw"], [("okT", l, g)], ("okT",))
            if _STOP[0] == 32:
                raise _StopBuild()
            if g == 0:
                rv = [load_w(win_d[l][wt[0]]), load_w(win_d[l][wt[0] + 1])]
                wt[0] += 2
                for B in range(NBLK):
                    pv = next_ps()
                    for vg in range(2):
                        mm_group(PS[pv][:, vg * 128:(vg + 1) * 128],
                                 [(XM[:, k, B * 128:(B + 1) * 128], RING[rv[vg]][:, k, :]) for k in range(KC)],
                                 reads=[("ring", rv[vg])] + XMK, writes=[("ps", pv)])
                    act(vtok[:, B, :], PS[pv][:, 0:256], AF.Identity, [("ps", pv)], [("vtok", B)])
                    vcopy(vraw[B % 2], PS[pv][:, 0:256], [("ps", pv)], ["vraw%d" % (B % 2)])
                    dma("sp", ov_d[l, B], vraw[B % 2], ["vraw%d" % (B % 2)], [("ov", l, B)], ("ov", B % 2))
            if _STOP[0] == 33:
                raise _StopBuild()
            for B in range(NBLK):
                kbs = []
                if B - 1 >= 0:
                    kbs.append(("loc", B - 1, 0))
                kbs.append(("loc", B, None))
                if B + 1 < NBLK:
                    kbs.append(("loc", B + 1, 2))
                for j in range(4):
                    kbs.append(("ctx", j, None))
                po = next_ps()
                pd = next_ps(exclude=(po,))
                qrhs = qT[:, :, B * 128:(B + 1) * 128]
                qkeys = [("qT", r) for r in range(4)]
                nkb = len(kbs)
                sbank = {}

                def issue_s(i):
                    kind, j, mi = kbs[i]
                    pb = next_ps(exclude=(po, pd))
                    sbank[i] = pb
                    if kind == "loc":
                        lhs = kT[:, j * 128:(j + 1) * 128]
                        rd = ["kT"]
                    else:
                        lhs = ckT[:, g, j * 128:(j + 1) * 128]
                        rd = ["ckT"]
                    mm_group(r4(PS[pb][:]), [(lhs, qrhs)], reads=rd + qkeys, writes=[("ps", pb)])

                issue_s(0)
                for i in range(nkb):
                    if i + 1 < nkb:
                        issue_s(i + 1)
                    kind, j, mi = kbs[i]
                    pb = sbank[i]
                    et = eT[i % 4]
                    ek = ("eT", i % 4)
                    act(et, PS[pb][:], AF.Exp, [("ps", pb)], [ek], scale=ATT_SCALE)
                    if kind == "loc" and mi is not None:
                        tt("dve", r4(et), r4(et), masks[:, B, mi, :].unsqueeze(1).to_broadcast([128, 4, 128]),
                           ALU.mult, [ek, "masks"], [ek])
                    if kind == "ctx":
                        act(et, et, AF.Identity, [ek, "ctxf"], [ek], scale=ctxf[:, B:B + 1])
                    if kind == "loc":
                        vl = vtok[:, j, g * 128:(g + 1) * 128]
                        vrd = [("vtok", j)]
                    else:
                        vl = cvt[:, j, g * 128:(g + 1) * 128]
                        vrd = ["cvt"]
                    mm1(PS[po][:], vl, et, (i == 0), (i == nkb - 1), vrd + [ek], [("ps", po)])
                    mm1(PS[pd][:], ones[:], et, (i == 0), (i == nkb - 1), ["ones", ek], [("ps", pd)])
                tA = tmpA[B % 2]
                tk = ("tmpA", B % 2)
                tt("dve", r4(tA), r4(PS[pd][:]),
                   esink[:, l, 4 * g:4 * g + 4].unsqueeze(2).to_broadcast([128, 4, 128]), ALU.add,
                   [("ps", pd), "esink"], [tk])
                S.add("dve", lambda e, tA=tA: e.reciprocal(out=tA, in_=tA), reads=[tk], writes=[tk])
                tt("dve", attraw[g][:, :, B * 128:(B + 1) * 128], r4(PS[po][:]), r4(tA), ALU.mult,
                   [("ps", po), tk], ATTK[g])
        if _STOP[0] == 4:
            raise _StopBuild()
        for ti, (a, n) in enumerate(TT):
            pss = next_ps()
            for h in range(8):
                o_ = eT[h % 4]
                act(o_[:, 0:n], attraw[h // 4][:, h % 4, a:a + n], AF.Square, ATTK[h // 4], [("eT", h % 4)])
                mm1(PS[pss][:, 0:n], ones[:], o_[:, 0:n], (h == 0), (h == 7), ["ones", ("eT", h % 4)], [("ps", pss)])
            tA = tmpA[ti % 2]
            tk = ("tmpA", ti % 2)
            act(tA[:, 0:n], PS[pss][:, 0:n], AF.Ln, [("ps", pss)], [tk], scale=1.0 / 1024.0, bias=RMS_EPS)
            act(tA[:, 0:n], tA[:, 0:n], AF.Exp, [tk], [tk], scale=-0.5)
            for h in range(8):
                stt(MIX[:, h, a:a + n], attraw[h // 4][:, h % 4, a:a + n], attg[:, l, h:h + 1], tA[:, 0:n],
                    ALU.mult, ALU.mult, ATTK[h // 4] + ["attg", tk], [("MIX", h)])

        if _STOP[0] == 5:
            raise _StopBuild()
        barrier(ATT_TEMPK + HG_TEMPK)
        ident = cmats[:, 0, :]
        for h in range(8):
            rts = [load_w(win_d[l][wt[0] + i]) for i in range(5)]
            wt[0] += 5
            for ti, (a, n) in enumerate(TT):
                pb = [next_ps() for _ in range(5)]
                for i in range(5):
                    project(rts[i], a, n, pb[i])
                act(qf[:, a:a + n], PS[pb[0]][:, 0:n], AF.Silu, [("ps", pb[0])], ["qf"])
                act(sigf[0][:, a:a + n], PS[pb[1]][:, 0:n], AF.Sigmoid, [("ps", pb[1])], ["sig0"])
                act(sigf[1][:, a:a + n], PS[pb[2]][:, 0:n], AF.Sigmoid, [("ps", pb[2])], ["sig1"])
                act(vTh[:, a:a + n], PS[pb[3]][:, 0:n], AF.Identity, [("ps", pb[3])], ["kmid"])
                act(gsil[:, a:a + n], PS[pb[4]][:, 0:n], AF.Silu, [("ps", pb[4])], ["gsil"])
            for B in range(NBLK):
                pt = next_ps()
                ptv = PS[pt][:].bitcast(BF16)[:, 0:128]
                S.add("pe", lambda e, ptv=ptv, B=B: e.transpose(ptv, vTh[:, B * 128:(B + 1) * 128], ident),
                      reads=["kmid", "cmats"], writes=[("ps", pt)])
                act(vtokH[:, B, :], ptv, AF.Identity, [("ps", pt)], ["vtokH"])
            pos = [next_ps() for _ in range(3)]
            for i in range(3):
                vmemset(PS[pos[i]][:], 0.0, [("ps", pos[i])])

            def o_ap(t0, n):
                return PS[pos[t0 // 512]][:, t0 % 512:t0 % 512 + n]

            def prepA_ops(dr):
                fwd = (dr == 0)
                mid = 15 if fwd else 16
                lastp = 31 if fwd else 0
                bmid = c32(bc)[:, :, mid:mid + 1].to_broadcast([128, NCH, 32])
                blast = c32(bc)[:, :, lastp:lastp + 1].to_broadcast([128, NCH, 32])
                ops = []
                ops.append(lambda: act(kk, sigf[dr], AF.Identity, ["sig%d" % dr, "oml", "lbv"], ["kk"],
                                       scale=oml[:, dr, l, h:h + 1], bias=lbv[:, dr, l, h:h + 1]))
                ops.append(lambda: act(logf, kk, AF.Ln, ["kk"], ["E1"]))
                ops.append(lambda: act(kk, kk, AF.Identity, ["kk", "E1"], ["kk"], scale=-1.0, bias=1.0))
                ops.append(lambda: S.add("dve", lambda e: e.tensor_tensor_scan(
                    out=bc, data0=scanm[:].rearrange("p c s -> p (c s)"), data1=logf, initial=0.0,
                    op0=ALU.mult, op1=ALU.add), reads=["scanm", "E1"], writes=["bc"]))
                if not fwd:
                    ops.append(lambda: tt("dve", d1, logf, bc, ALU.subtract, ["E1", "bc"], ["d1"]))
                    ops.append(lambda: tt("dve", c32(bc), c32(d1), c32(bc)[:, :, 31:32].to_broadcast([128, NCH, 32]),
                                          ALU.add, ["d1", "bc"], ["bc"]))
                ops.append(lambda: tt("dve", c32(d1), c32(bc), bmid, ALU.subtract, ["bc"], ["d1"]))
                ops.append(lambda: act(E1, d1, AF.Exp, ["d1"], ["E1"]))
                ops.append(lambda: tt("dve", qmid, qf, E1, ALU.mult, ["qf", "E1"], ["qmid"]))
                ops.append(lambda: act(E1, d1, AF.Exp, ["d1", "qmid"], ["E1"], scale=-1.0))
                ops.append(lambda: tt("dve", kmid, kk, E1, ALU.mult, ["kk", "E1"], ["kmid"]))
                ops.append(lambda: tt("dve", c32(d1), blast, c32(bc), ALU.subtract, ["bc", "kmid"], ["d1"]))
                ops.append(lambda: act(E1, d1, AF.Exp, ["d1"], ["E1"]))
                ops.append(lambda: tt("dve", klast, kk, E1, ALU.mult, ["kk", "E1"], ["klast"]))
                return ops

            def prepB(dr):
                fwd = (dr == 0)
                lastp = 31 if fwd else 0
                act(E1, bc, AF.Exp, ["bc", "klast"], ["E1"])
                tt("dve", qb, qf, E1, ALU.mult, ["qf", "E1"], ["qb"])
                act(decay.unsqueeze(2), c32(bc)[:, :, lastp:lastp + 1], AF.Exp, ["bc"], ["decay"])
                for B in range(NBLK):
                    pt = next_ps(exclude=pos)
                    ptv = PS[pt][:].bitcast(BF16)[:, 0:128]
                    S.add("pe", lambda e, ptv=ptv, B=B: e.transpose(ptv, klast[:, B * 128:(B + 1) * 128], ident),
                          reads=["klast", "cmats"], writes=[("ps", pt)])
                    tt("dve", klastM[:, 4 * B:4 * B + 4, :], ptv.unsqueeze(1).to_broadcast([128, 4, 128]), rmaskF[:],
                       ALU.mult, [("ps", pt), "rmaskF"], ["klastM"])
                mk = cmats[:, 1 if fwd else 2, :]
                for B in range(NBLK):
                    pa = next_ps(exclude=pos)
                    mm_group(PS[pa][:, 0:128], [(kmid[:, B * 128:(B + 1) * 128], qmid[:, B * 128:(B + 1) * 128])],
                             reads=["kmid", "qmid"], writes=[("ps", pa)])
                    tt("dve", aT[B % 2], PS[pa][:, 0:128], mk, ALU.mult, [("ps", pa), "cmats"], ["aT%d" % (B % 2)])
                    mm1(o_ap(B * 128, 128), vtokH[:, B, :], aT[B % 2], False, False,
                        ["vtokH", "aT%d" % (B % 2)], [("ps", pos[(B * 128) // 512])])

            def chain(dr, filler):
                fwd = (dr == 0)
                filler = list(filler)
                order = list(range(NCH)) if fwd else list(range(NCH - 1, -1, -1))
                cur = 0
                for idx, c in enumerate(order):
                    slot = c // 8
                    first_in_slot = (c % 8 == 0) if fwd else (c % 8 == 7)
                    if first_in_slot:
                        nxt = Sf[(cur + 1) % 2]
                        nk = "Sf%d" % ((cur + 1) % 2)
                        if slot == 4:
                            vmemset(nxt, 0.0, [nk])
                        elif (fwd and slot == 0) or ((not fwd) and slot == 3):
                            dma("sp", S0t, s0_d[l, dr, h], [], ["S0t"], ("c", "S0t"))
                            vcopy(nxt, S0t, ["S0t"], [nk])
                        else:
                            ts("dve", nxt, Sf[cur], cfl[:, 0:1], None, ALU.mult, None, ["Sf%d" % cur, "cfl"], [nk])
                        cur = (cur + 1) % 2
                    act(Sall[:, c, :], Sf[cur], AF.Identity, ["Sf%d" % cur], ["Sall"])
                    pu = next_ps(exclude=pos)
                    mm_group(PS[pu][:, 0:128], [(klastM[:, c, :], vtokH[:, c // 4, :])], reads=["klastM", "vtokH"],
                             writes=[("ps", pu)])
                    nxt = Sf[(cur + 1) % 2]
                    nk = "Sf%d" % ((cur + 1) % 2)
                    stt(nxt, Sf[cur], decay[:, c:c + 1], PS[pu][:, 0:128], ALU.mult, ALU.add,
                        ["Sf%d" % cur, "decay", ("ps", pu)], [nk])
                    cur = (cur + 1) % 2
                    last_in_slot = (c % 8 == 7) if fwd else (c % 8 == 0)
                    if last_in_slot:
                        act(Sfin[:, slot, :], Sf[cur], AF.Identity, ["Sf%d" % cur], ["Sfin"])
                    if filler and idx >= 2 and idx % 3 == 2:
                        filler.pop(0)()
                while filler:
                    filler.pop(0)()
                dma("sp", ost_d[l, dr, h], Sfin, ["Sfin"], [("ost", l, dr, h)], ("ost",))

            def inter(dr):
                for c in range(NCH):
                    mm1(o_ap(c * 32, 32), Sall[:, c, :], qb[:, c * 32:(c + 1) * 32], False, False,
                        ["Sall", "qb"], [("ps", pos[(c * 32) // 512])])

            for t_ in prepA_ops(0):
                t_()
            prepB(0)
            chain(0, prepA_ops(1))
            inter(0)
            prepB(1)
            chain(1, [])
            inter(1)
            for ti, (a, n) in enumerate(TT):
                act(osq[:, 0:n], PS[pos[ti]][:, 0:n], AF.Square, [("ps", pos[ti]), "d1"], ["d1"])
                pss = next_ps(exclude=pos)
                mm_group(PS[pss][:, 0:n], [(ones[:], osq[:, 0:n])], reads=["ones", "d1"], writes=[("ps", pss)])
                act(rstdh[:, 0:n], PS[pss][:, 0:n], AF.Ln, [("ps", pss), "d1"], ["d1"], scale=1.0 / 128.0, bias=RMS_EPS)
                act(rstdh[:, 0:n], rstdh[:, 0:n], AF.Exp, ["d1"], ["d1"], scale=-0.5)
                tt("dve", th[:, 0:n], PS[pos[ti]][:, 0:n], rstdh[:, 0:n], ALU.mult, [("ps", pos[ti]), "d1"], ["d1"])
                stt(MIX[:, 8 + h, a:a + n], th[:, 0:n], hgg[:, l:l + 1], gsil[:, a:a + n], ALU.mult, ALU.mult,
                    ["d1", "hgg", "gsil"], [("MIX", 8 + h)])
        assert wt[0] == N_IN_TILES

        if _STOP[0] == 6:
            raise _StopBuild()
        barrier(ATT_TEMPK + HG_TEMPK + XK)
        for k4 in range(4):
            dma("sp", X[:, 4 * k4:4 * k4 + 4, :], xsp_d[:, 4 * k4:4 * k4 + 4, :], ["xspill"], XK[4 * k4:4 * k4 + 4],
                ("xin", k4))
        for k4 in range(4):
            act(X[:, 4 * k4:4 * k4 + 4, :], X[:, 4 * k4:4 * k4 + 4, :], AF.Identity, XK[4 * k4:4 * k4 + 4],
                XK[4 * k4:4 * k4 + 4], scale=ALPHA)

        if _STOP[0] == 7:
            raise _StopBuild()
        def resid_accum(oc, ti, a, n, pbank, gate):
            j = 0 if ti < 2 else 1
            stt(X[:, oc, a:a + n], PS[pbank][:, 0:n], gate[:, oc, j:j + 1], X[:, oc, a:a + n], ALU.mult, ALU.add,
                [("ps", pbank), MK, ("X", oc)], [("X", oc)])

        for oc in range(KC):
            ri = load_w(wo_d[l][oc])
            for ti, (a, n) in enumerate(TT):
                pb = next_ps()
                mm_group(PS[pb][:, 0:n], [(RING[ri][:, k, :], MIX[:, k, a:a + n]) for k in range(KC)],
                         reads=[("ring", ri)] + MIXK, writes=[("ps", pb)])
                resid_accum(oc, ti, a, n, pb, g1)

        if _STOP[0] == 8:
            raise _StopBuild()
        def layer_norm(which, write_xm):
            for ti, (a, n) in enumerate(TT):
                j = 0 if ti < 2 else 1
                act(xb[:, :, 0:n], X[:, :, a:a + n], AF.Identity, XK, XBK)
                act(xsq[:, :, 0:n], X[:, :, a:a + n], AF.Square, XK, XSQK)
                p1 = next_ps()
                p2 = next_ps()
                mm_group(PS[p1][:, 0:n], [(ones[:], xb[:, k, 0:n]) for k in range(KC)], reads=["ones"] + XBK,
                         writes=[("ps", p1)])
                mm_group(PS[p2][:, 0:n], [(ones[:], xsq[:, k, 0:n]) for k in range(KC)], reads=["ones"] + XSQK,
                         writes=[("ps", p2)])
                act(mean[:, 0:n], PS[p1][:, 0:n], AF.Identity, [("ps", p1)], ["mean"], scale=1.0 / D)
                tt("dve", rtmp[:, 0:n], mean[:, 0:n], mean[:, 0:n], ALU.mult, ["mean"], ["rtmp"])
                stt(rstd[:, 0:n], PS[p2][:, 0:n], 1.0 / D, rtmp[:, 0:n], ALU.mult, ALU.subtract,
                    [("ps", p2), "rtmp"], ["rstd"])
                act(rstd[:, 0:n], rstd[:, 0:n], AF.Ln, ["rstd"], ["rstd"], bias=LN_EPS)
                act(rstd[:, 0:n], rstd[:, 0:n], AF.Exp, ["rstd"], ["rstd"], scale=-0.5)
                for k in range(KC):
                    tt("pool", X[:, k, a:a + n], X[:, k, a:a + n], mean[:, 0:n], ALU.subtract, [("X", k), "mean"],
                       [("X", k)])
                    tt("dve", X[:, k, a:a + n], X[:, k, a:a + n], rstd[:, 0:n], ALU.mult, [("X", k), "rstd"],
                       [("X", k)])
                    act(X[:, k, a:a + n], X[:, k, a:a + n], AF.Identity, [("X", k), "lng", "lnb"], [("X", k)],
                        scale=lng[:, l, which, k:k + 1], bias=lnb[:, l, which, k:k + 1])
                    if write_xm:
                        act(XM[:, k, a:a + n], X[:, k, a:a + n], AF.Identity, [("X", k), S2K, MK], [("XM", k)],
                            scale=sc2p[:, k, j:j + 1], bias=sh2[:, k, j:j + 1])

        layer_norm(0, True)
        for k4 in range(4):
            act(X[:, 4 * k4:4 * k4 + 4, :], X[:, 4 * k4:4 * k4 + 4, :], AF.Identity, XK[4 * k4:4 * k4 + 4],
                XK[4 * k4:4 * k4 + 4], scale=ALPHA)

        if _STOP[0] == 9:
            raise _StopBuild()
        nxt = (l + 1 < L)
        pmn = next_ps() if nxt else None
        excl = (pmn,) if nxt else ()
        mct = [0]

        def mods_step(nt):
            if not nxt:
                return
            for _ in range(nt):
                if mct[0] < 96:
                    mods_tile(l + 1, mct[0], pmn)
                    mct[0] += 1

        for grp in range(4):
            for jj in range(16):
                ri = load_w(wup_d[l][grp * 16 + jj])
                for ti, (a, n) in enumerate(TT):
                    pb = next_ps(exclude=excl)
                    project(ri, a, n, pb)
                    rl = tmp_relu[ti % 2]
                    act(rl[:, 0:n], PS[pb][:, 0:n], AF.Relu, [("ps", pb)], [("relu", ti % 2)])
                    act(HID[:, jj, a:a + n], rl[:, 0:n], AF.Square, [("relu", ti % 2)], [("MIX", jj)])
                mods_step(1)
            for oc in range(KC):
                ri = load_w(wdn_d[l][grp * 16 + oc])
                for ti, (a, n) in enumerate(TT):
                    pb = next_ps(exclude=excl)
                    mm_group(PS[pb][:, 0:n], [(RING[ri][:, k, :], HID[:, k, a:a + n]) for k in range(KC)],
                             reads=[("ring", ri)] + MIXK, writes=[("ps", pb)])
                    resid_accum(oc, ti, a, n, pb, g2)
                mods_step(1)
        if nxt:
            mods_step(96)
            mods_finish(l + 1, pmn)
        layer_norm(1, False)

    try:
        for l in range(L):
            layer(l)
    except _StopBuild:
        pass
    for k4 in range(4):
        dma("sp", yT_d[:, 4 * k4:4 * k4 + 4, :], X[:, 4 * k4:4 * k4 + 4, :], XK[4 * k4:4 * k4 + 4], [("yT", k4)],
            ("yout", k4))

    with nc.Block() as block:
        S.emit(nc, stack, block)
    stack.close()
    return nc


def _core_slots(core):
    if core < 2:
        return [("s", core, 256 * s) for s in range(4)] + [("p", 30 + core, 0)]
    return [("p", (core - 2) * 5 + s, 0) for s in range(5)]


def _fm(vec2d):
    a = np.asarray(vec2d, np.float32)
    n = a.shape[-1] // 128
    a = a.reshape(a.shape[:-1] + (n, 128))
    return np.ascontiguousarray(np.moveaxis(a, -1, 0))


def _retile(w):
    K, N = w.shape
    n = N // 128
    return np.ascontiguousarray(w.reshape(K // 128, 128, n, 128).transpose(2, 1, 0, 3)).reshape(n, 128, (K // 128) * 128)


def _win_cols():
    perm = np.arange(128) ^ 32
    ar = np.arange(128)
    cols = []
    for g in range(2):
        for r in range(4):
            h = 4 * g + r
            cols.append(h * 128 + ar)
            cols.append(h * 128 + perm)
        cols.append(1024 + g * 128 + ar)
        cols.append(1024 + g * 128 + perm)
        if g == 0:
            cols.append(1280 + ar)
            cols.append(1280 + 128 + ar)
    for h in range(8):
        for base in (1536, 2560, 3584, 4608, 5632):
            cols.append(base + h * 128 + ar)
    return np.concatenate(cols)


def _rope_tables(slots):
    cosT = np.ones((128, NTOK), np.float32)
    sinT = np.zeros((128, NTOK), np.float32)
    inv = (np.float32(10000.0) ** (-np.arange(32, dtype=np.float32) / np.float32(32))).astype(np.float32)
    for s, (kind, seq, start) in enumerate(slots):
        if kind != "s":
            continue
        t = start + np.arange(256)
        row = (t // 64).astype(np.float32)
        col = (t % 64).astype(np.float32)
        ang_r = (row[None, :] * inv[:, None]).astype(np.float32)
        ang_c = (col[None, :] * inv[:, None]).astype(np.float32)
        sl = slice(256 * s, 256 * (s + 1))
        cr, sr, cc, sc_ = np.cos(ang_r), np.sin(ang_r), np.cos(ang_c), np.sin(ang_c)
        cosT[0:32, sl] = cr
        cosT[32:64, sl] = cr
        cosT[64:96, sl] = cc
        cosT[96:128, sl] = cc
        sinT[0:32, sl] = -sr
        sinT[32:64, sl] = sr
        sinT[64:96, sl] = -sc_
        sinT[96:128, sl] = sc_
    return cosT, sinT


def _masks(slots):
    m = np.zeros((128, NBLK, 3, 128), np.float32)
    ctxflag = np.zeros((128, NBLK), np.float32)
    ar = np.arange(128)
    for B in range(NBLK):
        kind, seq, start = slots[B // 2]
        qpos = start + (B % 2) * 128 + ar
        if kind == "s":
            ctxflag[:, B] = 1.0
        for nb, KB in ((0, B - 1), (1, B), (2, B + 1)):
            if KB < 0 or KB >= NBLK:
                continue
            kkind, kseq, kstart = slots[KB // 2]
            if (kkind, kseq) != (kind, seq):
                continue
            kpos = kstart + (KB % 2) * 128 + ar
            if kind == "s":
                ok = np.abs(kpos[:, None] - qpos[None, :]) <= 128
            else:
                ok = np.ones((128, 128), bool)
            m[:, B, nb, :] = ok.astype(np.float32)
    cfl = np.full((128, 1), 1.0 if slots[0][0] == "s" else 0.0, np.float32)
    return m, ctxflag, cfl


def _const_mats():
    cm = np.zeros((4, 128, 128), np.float32)
    ar = np.arange(128)
    cm[0] = np.eye(128, dtype=np.float32)
    same = (ar[:, None] // 32) == (ar[None, :] // 32)
    cm[1] = (same & (ar[:, None] <= ar[None, :])).astype(np.float32)
    cm[2] = (same & (ar[:, None] >= ar[None, :])).astype(np.float32)
    rowmask = np.zeros((128, 4), np.float32)
    for c in range(4):
        rowmask[32 * c:32 * (c + 1), c] = 1.0
    return cm, rowmask


def _prepare(inp, L, cores):
    f = lambda k: np.asarray(inp[k], np.float32)
    w_in = f("w_in")
    cols = _win_cols()
    shared = {}
    for l in range(L):
        shared[f"wmod{l}"] = _retile(f("w_mod")[l])
        shared[f"win{l}"] = _retile(w_in[l][:, cols])
        shared[f"wo{l}"] = _retile(f("w_o")[l])
        shared[f"wup{l}"] = _retile(f("w_up")[l])
        shared[f"wdn{l}"] = np.ascontiguousarray(
            f("w_down")[l].reshape(4, 16, 128, 16, 128).transpose(0, 3, 2, 1, 4)).reshape(64, 128, 2048)
    shared.update({
        "bmod": _fm(f("b_mod")[:L]),
        "lng": _fm(f("ln_g")[:L]),
        "lnb": _fm(f("ln_b")[:L]),
        "attg": _fm(f("attn_norm_g")[:L]),
        "hgg": np.ascontiguousarray(f("hg_norm_g")[:L].T),
        "lbl": _fm(f("hg_lb_logits")),
        "sink": np.ascontiguousarray(np.broadcast_to(f("attn_sink")[:L][None], (128, L, 8))),
    })
    cm, rowmask = _const_mats()
    shared["cmats"] = cm
    shared["rowmask"] = rowmask
    xp, xs = f("x_prompt"), f("x_sample")
    maps = []
    for core in cores:
        slots = _core_slots(core)
        xs_ = []
        for kind, seq, start in slots:
            xs_.append(xs[seq, start:start + 256] if kind == "s" else xp[seq])
        xc = np.concatenate(xs_, 0)
        d = dict(shared)
        d["xT"] = np.ascontiguousarray(xc.reshape(NTOK, KC, 128).transpose(2, 1, 0))
        condA = f("c")[core] if core < 2 else f("c_ctx")
        condB = f("c_ctx")
        d["cond"] = _fm(np.stack([condA, condB]))
        d["cond"] = np.ascontiguousarray(d["cond"].transpose(0, 2, 1))
        cosT, sinT = _rope_tables(slots)
        d["cosT"], d["sinT"] = cosT, sinT
        m, ctxflag, cfl = _masks(slots)
        d["masks"], d["ctxflag"], d["cflag"] = m, ctxflag, cfl
        if core < 2:
            ck = f("cache_k")[core][:L]
            cv = f("cache_v")[core][:L]
            d["ckT"] = np.ascontiguousarray(ck.transpose(0, 3, 2, 1))
            d["cv"] = np.ascontiguousarray(cv.reshape(L, 4, 128, 256).transpose(0, 2, 1, 3))
            d["s0"] = np.ascontiguousarray(np.stack([f("state_hgrn_fwd")[core][:L], f("state_hgrn_bwd")[core][:L]], 1))
        else:
            d["ckT"] = np.zeros((L, 128, 2, 512), np.float32)
            d["cv"] = np.zeros((L, 128, 4, 256), np.float32)
            d["s0"] = np.zeros((L, 2, 8, 128, 128), np.float32)
        maps.append(d)
    return maps


_PROG = {}


def _run(inp, L=DEPTH, cores=tuple(range(8)), trace=False):
    if L not in _PROG:
        _PROG[L] = build_program(L)
    nc = _PROG[L]
    maps = _prepare(inp, L, cores)
    res = run_bass_kernel_spmd(nc, maps, core_ids=list(range(len(cores))), **({"trace": True} if trace else {}))
    return res


def _gather(results, L, cores):
    BATCH, SEQ, DEC_BATCH, DEC_SEQ = 32, 256, 2, 1024
    y_p = np.zeros((BATCH, SEQ, D), np.float32)
    y_s = np.zeros((DEC_BATCH, DEC_SEQ, D), np.float32)
    nk = np.zeros((BATCH, L, SEQ, 2, 128), np.float32)
    nv = np.zeros((BATCH, L, SEQ, 2, 128), np.float32)
    sf = np.zeros((BATCH, L, 8, 128, 128), np.float32)
    sbw = np.zeros((BATCH, L, 8, 128, 128), np.float32)
    for ci, core in enumerate(cores):
        r = results[ci]
        y = np.asarray(r["yT"]).transpose(2, 1, 0).reshape(NTOK, D)
        okT = np.asarray(r["okT"])
        ov = np.asarray(r["ov"])
        ost = np.asarray(r["ost"])
        for s, (kind, seq, start) in enumerate(_core_slots(core)):
            ys = y[256 * s:256 * (s + 1)]
            if kind == "s":
                y_s[seq, start:start + 256] = ys
            else:
                y_p[seq] = ys
                nk[seq] = okT[:, :, :, 256 * s:256 * (s + 1)].transpose(0, 3, 1, 2)
                nv[seq] = ov[:, 2 * s:2 * s + 2].reshape(L, 256, 2, 128)
                sf[seq] = ost[:, 0, :, :, s, :]
                sbw[seq] = ost[:, 1, :, :, s, :]
    return y_p, y_s, nk, nv, sf, sbw


def kernel(**inputs):
    res = _run(inputs, DEPTH, tuple(range(8)))
    return _gather(res.results, DEPTH, tuple(range(8)))
```
